# Optimizing a Trainium2 kernel written in Bass

```python
import math
import jax
import jax.numpy as jnp
from jax import lax
import numpy as np

D_MODEL = 2048
BATCH = 4
SEQ = 2048
DEPTH = 4

GRID_W = 64
CTX_LEN = 256
D_FF = 5632
N_MOD = 9
EPS = 1e-6

POOL_WIDTH = 512
POOL_WINDOWS = (2, 4, 8, 16)
POOL_GC = POOL_WIDTH // len(POOL_WINDOWS)

HEAD_DIM = 128
N_Q_HEADS = 4
N_KV_HEADS = 2
Q_GROUP = N_Q_HEADS // N_KV_HEADS
ATTN_WIDTH = N_Q_HEADS * HEAD_DIM
KV_WIDTH = N_KV_HEADS * HEAD_DIM
Q_BLOCK = 128
ROPE_THETA = 10000.0
ROPE_AXIS_DIM = HEAD_DIM // 2
ROPE_FREQS = ROPE_AXIS_DIM // 2

HYENA_WIDTH = 512
HYENA_EMB = 33
HYENA_BANDS = (HYENA_EMB - 1) // 2
HYENA_HIDDEN = 64
HYENA_TARGET = 1e-2
HYENA_FAST_PCT = 0.3
HYENA_SLOW_PCT = 1.5

S5_WIDTH = 512
S5_GC = 16
S5_GROUPS = S5_WIDTH // S5_GC
S5_STATE = 64

N_BRANCH = 4
BRANCH_WIDTH = 512

OFF_POOL = 0
OFF_Q = OFF_POOL + POOL_WIDTH
OFF_K = OFF_Q + ATTN_WIDTH
OFF_V = OFF_K + KV_WIDTH
OFF_HY = OFF_V + KV_WIDTH
OFF_S5 = OFF_HY + 3 * HYENA_WIDTH
IN_WIDTH = OFF_S5 + S5_WIDTH

kernel_name = 'hybrid_pool_gqa_hyena_s5_dit'

F32 = jnp.float32


def rmsnorm(x, g):
    x32 = x.astype(F32)
    y = x32 * lax.rsqrt(jnp.mean(x32 * x32, axis=-1, keepdims=True) + EPS)
    return (y * g.astype(F32)).astype(x.dtype)


def modulate(x, g, shift, scale):
    return rmsnorm(x, g) * (1 + scale) + shift


def swiglu(u, wi, wo):
    a, b = jnp.split(u @ wi, 2, axis=-1)
    return (jax.nn.silu(a) * b) @ wo


def ffn_sublayer(x, mod, g, wi, wo, base):
    u = modulate(x, g, mod[:, base], mod[:, base + 1])
    return x + 0.5 * mod[:, base + 2] * swiglu(u, wi, wo)


def rope_2d(x, rows, cols):
    freqs = ROPE_THETA ** (-jnp.arange(ROPE_FREQS, dtype=F32) / ROPE_FREQS)

    def rot(xh, pos):
        ang = pos[:, None] * freqs[None, :]
        cos = jnp.cos(ang)[None, :, None, :]
        sin = jnp.sin(ang)[None, :, None, :]
        x1, x2 = xh[..., :ROPE_FREQS], xh[..., ROPE_FREQS:]
        return jnp.concatenate([x1 * cos - x2 * sin, x1 * sin + x2 * cos], axis=-1)

    x32 = x.astype(F32)
    out = jnp.concatenate([rot(x32[..., :ROPE_AXIS_DIM], rows), rot(x32[..., ROPE_AXIS_DIM:], cols)], axis=-1)
    return out.astype(x.dtype)


def block_attention(q, k, v):
    b_, lq = q.shape[:2]
    nb = lq // Q_BLOCK
    qb = q.reshape(b_, nb, Q_BLOCK, N_KV_HEADS, Q_GROUP, HEAD_DIM).transpose(1, 0, 3, 4, 2, 5)
    scale = 1.0 / math.sqrt(HEAD_DIM)

    def one_block(qblk):
        s = jnp.einsum('bhgqd,bkhd->bhgqk', qblk, k).astype(F32) * scale
        p = jax.nn.softmax(s, axis=-1).astype(v.dtype)
        return jnp.einsum('bhgqk,bkhd->bhgqd', p, v)

    o = lax.map(one_block, qb)
    return o.transpose(1, 0, 4, 2, 3, 5).reshape(b_, lq, ATTN_WIDTH)


def pool_mix(a, pool_w, pool_scale):
    b_, l_, _ = a.shape
    a32 = a.astype(F32)
    cs = jnp.concatenate([jnp.zeros((b_, 1, POOL_WIDTH), F32), jnp.cumsum(a32, axis=1)], axis=1)
    t = jnp.arange(l_)
    means = []
    for gi, w in enumerate(POOL_WINDOWS):
        lo = jnp.clip(t - w // 2, 0, l_)
        hi = jnp.clip(t - w // 2 + w, 0, l_)
        cg = cs[..., gi * POOL_GC:(gi + 1) * POOL_GC]
        s = jnp.take(cg, hi, axis=1) - jnp.take(cg, lo, axis=1)
        means.append(s / (hi - lo).astype(F32)[None, :, None])
    pooled = jnp.concatenate(means, axis=-1) - a32
    y = jnp.einsum('blgc,gcd->blgd', pooled.reshape(b_, l_, len(POOL_WINDOWS), POOL_GC), pool_w.astype(F32))
    return (y.reshape(b_, l_, POOL_WIDTH) * pool_scale.astype(F32)).astype(a.dtype)


def short_conv3(x, w, b):
    xp = jnp.pad(x, ((0, 0), (1, 1), (0, 0)))
    return xp[:, :-2] * w[0] + xp[:, 1:-1] * w[1] + xp[:, 2:] * w[2] + b


def hyena_filter(l_, lp):
    t = jnp.linspace(0.0, 1.0, l_, dtype=F32)[:, None]
    f = jnp.linspace(1e-4, HYENA_BANDS - 1, HYENA_BANDS, dtype=F32)
    w = 2.0 * math.pi * jnp.arange(l_, dtype=F32) / l_
    fw = w[:, None] * f[None, :]
    z = jnp.concatenate([t, jnp.cos(fw), -jnp.sin(fw)], axis=-1)
    freq = lp['hy_freq'].astype(F32)
    h = jnp.sin(freq * (z @ lp['hy_f1_w'].astype(F32) + lp['hy_f1_b'].astype(F32)))
    h = jnp.sin(freq * (h @ lp['hy_f2_w'].astype(F32) + lp['hy_f2_b'].astype(F32)))
    h = h @ lp['hy_f3_w'].astype(F32)
    max_decay = math.log(HYENA_TARGET) / HYENA_FAST_PCT
    min_decay = math.log(HYENA_TARGET) / HYENA_SLOW_PCT
    deltas = jnp.abs(jnp.linspace(min_decay, max_decay, HYENA_WIDTH, dtype=F32))
    decay = jnp.exp(-t * deltas[None, :])
    return h[:, :HYENA_WIDTH] * decay, h[:, HYENA_WIDTH:] * decay


def hyena_mix(z_in, lp):
    l_ = z_in.shape[1]
    z = short_conv3(z_in, lp['hy_short_w'], lp['hy_short_b'])
    x0, x1, v = jnp.split(z, 3, axis=-1)
    hf, hb = hyena_filter(l_, lp)
    k2 = jnp.concatenate([hf, jnp.zeros((1, HYENA_WIDTH), F32), jnp.flip(hb[1:], axis=0)], axis=0)
    vx = (v * x1).astype(F32)
    n = 2 * l_
    y = jnp.fft.irfft(jnp.fft.rfft(vx, n=n, axis=1) * jnp.fft.rfft(k2, n=n, axis=0)[None], n=n, axis=1)[:, :l_]
    y = y + vx * lp['hy_bias'].astype(F32)
    return (y * x0.astype(F32)).astype(z_in.dtype)


def s5_direction(u, lp, d, h0, reverse, need_out):
    b_, l_, _ = u.shape
    ug = u.astype(F32).reshape(b_, l_, S5_GROUPS, S5_GC)
    if reverse:
        ug = jnp.flip(ug, axis=1)
    lam = lax.complex(lp['s5_a_re'][d].astype(F32), lp['s5_a_im'][d].astype(F32))
    dt = jnp.exp(lp['s5_log_dt'][d].astype(F32))[:, None]
    lam_dt = lam * dt
    a_bar = jnp.exp(lam_dt)
    b_mat = lax.complex(lp['s5_b_re'][d].astype(F32), lp['s5_b_im'][d].astype(F32))
    b_bar = ((a_bar - 1.0) / lam)[..., None] * b_mat
    bu = jnp.einsum('blgc,gpc->blgp', ug, b_bar)
    a_full = jnp.broadcast_to(a_bar, bu.shape)

    def binop(e1, e2):
        a1, b1 = e1
        a2, b2 = e2
        return a1 * a2, a2 * b1 + b2

    _, xs = lax.associative_scan(binop, (a_full, bu), axis=1)
    if h0 is not None:
        steps = jnp.arange(1, l_ + 1, dtype=F32)
        powers = jnp.exp(lam_dt[None] * steps[:, None, None])
        xs = xs + powers[None] * h0[:, None]
    h_last = xs[:, -1]
    if not need_out:
        return None, h_last
    c_mat = lax.complex(lp['s5_c_re'][d].astype(F32), lp['s5_c_im'][d].astype(F32))
    y = jnp.real(jnp.einsum('blgp,gcp->blgc', xs, c_mat)) + lp['s5_d'][d].astype(F32).reshape(S5_GROUPS, S5_GC) * ug
    if reverse:
        y = jnp.flip(y, axis=1)
    return y.reshape(b_, l_, S5_WIDTH).astype(u.dtype), h_last


def s5_glu(y, w, b):
    g = jax.nn.gelu(y) @ w + b
    a, gt = jnp.split(g, 2, axis=-1)
    return a * jax.nn.sigmoid(gt)


def token_mixer(u, lp, ctx_side, rows, cols, need_out):
    b_, l_, _ = u.shape
    proj = u @ lp['w_in']
    p_in = proj[..., OFF_POOL:OFF_Q]
    q = rmsnorm(proj[..., OFF_Q:OFF_K].reshape(b_, l_, N_Q_HEADS, HEAD_DIM), lp['q_norm'])
    k = rmsnorm(proj[..., OFF_K:OFF_V].reshape(b_, l_, N_KV_HEADS, HEAD_DIM), lp['k_norm'])
    v = proj[..., OFF_V:OFF_HY].reshape(b_, l_, N_KV_HEADS, HEAD_DIM)
    hy_in = proj[..., OFF_HY:OFF_S5]
    s5_in = proj[..., OFF_S5:IN_WIDTH]
    if rows is not None:
        q = rope_2d(q, rows, cols)
        k = rope_2d(k, rows, cols)
    h0_f = None if ctx_side is None else ctx_side[2]
    h0_b = None if ctx_side is None else ctx_side[3]
    y_f, h_f = s5_direction(s5_in, lp, 0, h0_f, False, need_out)
    y_b, h_b = s5_direction(s5_in, lp, 1, h0_b, True, need_out)
    if not need_out:
        return None, (k, v, h_f, h_b)
    if ctx_side is None:
        k_all, v_all = k, v
    else:
        k_all = jnp.concatenate([ctx_side[0], k], axis=1)
        v_all = jnp.concatenate([ctx_side[1], v], axis=1)
    y_attn = block_attention(q, k_all, v_all)
    y_pool = pool_mix(p_in, lp['pool_w'], lp['pool_scale'])
    y_hy = hyena_mix(hy_in, lp)
    y_s5 = s5_glu(y_f + y_b, lp['s5_glu_w'], lp['s5_glu_b'])
    gates = jax.nn.sigmoid((u @ lp['w_gate'] + lp['b_gate']).astype(F32)).astype(u.dtype)
    gates = gates.reshape(b_, l_, N_BRANCH, D_MODEL)
    branches = jnp.stack([y_pool, y_attn, y_hy, y_s5], axis=2)
    proj_b = jnp.einsum('blnc,ncd->blnd', branches, lp['w_branch'])
    merged = jnp.sum(gates * proj_b, axis=2)
    return merged @ lp['w_out'], (k, v, h_f, h_b)


def setup_inputs(seed: int = 0) -> dict:
    key = jax.random.key(seed)
    it = iter(list(jax.random.split(key, 48)))

    def nrm(shape, scale):
        return jax.random.normal(next(it), shape, F32) * scale

    def gain(shape):
        return 1.0 + nrm(shape, 0.01)

    dm = D_MODEL
    a_im = math.pi * jnp.arange(S5_STATE, dtype=F32)
    return {
        'x': nrm((BATCH, SEQ, dm), 1.0),
        'c': nrm((BATCH, dm), 1.0),
        'ctx': nrm((BATCH, CTX_LEN, dm), 1.0),
        'c_ctx': nrm((dm,), 1.0),
        'w_ada': nrm((DEPTH, dm, N_MOD * dm), 0.5 * dm ** -0.5),
        'b_ada': nrm((DEPTH, N_MOD * dm), 0.01),
        'norm_ffn1': gain((DEPTH, dm)),
        'norm_mix': gain((DEPTH, dm)),
        'norm_ffn2': gain((DEPTH, dm)),
        'norm_final': gain((dm,)),
        'ffn1_wi': nrm((DEPTH, dm, 2 * D_FF), dm ** -0.5),
        'ffn1_wo': nrm((DEPTH, D_FF, dm), D_FF ** -0.5),
        'ffn2_wi': nrm((DEPTH, dm, 2 * D_FF), dm ** -0.5),
        'ffn2_wo': nrm((DEPTH, D_FF, dm), D_FF ** -0.5),
        'w_in': nrm((DEPTH, dm, IN_WIDTH), dm ** -0.5),
        'w_gate': nrm((DEPTH, dm, N_BRANCH * dm), dm ** -0.5),
        'b_gate': nrm((DEPTH, N_BRANCH * dm), 0.01),
        'w_branch': nrm((DEPTH, N_BRANCH, BRANCH_WIDTH, dm), BRANCH_WIDTH ** -0.5),
        'w_out': nrm((DEPTH, dm, dm), dm ** -0.5),
        'pool_w': nrm((DEPTH, len(POOL_WINDOWS), POOL_GC, POOL_GC), POOL_GC ** -0.5),
        'pool_scale': gain((DEPTH, POOL_WIDTH)),
        'q_norm': gain((DEPTH, HEAD_DIM)),
        'k_norm': gain((DEPTH, HEAD_DIM)),
        'hy_short_w': nrm((DEPTH, 3, 3 * HYENA_WIDTH), 3 ** -0.5),
        'hy_short_b': nrm((DEPTH, 3 * HYENA_WIDTH), 0.01),
        'hy_f1_w': nrm((DEPTH, HYENA_EMB, HYENA_HIDDEN), HYENA_EMB ** -0.5),
        'hy_f1_b': nrm((DEPTH, HYENA_HIDDEN), 0.01),
        'hy_f2_w': nrm((DEPTH, HYENA_HIDDEN, HYENA_HIDDEN), HYENA_HIDDEN ** -0.5),
        'hy_f2_b': nrm((DEPTH, HYENA_HIDDEN), 0.01),
        'hy_f3_w': nrm((DEPTH, HYENA_HIDDEN, 2 * HYENA_WIDTH), 0.2 * HYENA_HIDDEN ** -0.5),
        'hy_freq': gain((DEPTH, HYENA_HIDDEN)),
        'hy_bias': nrm((DEPTH, HYENA_WIDTH), 0.1),
        's5_a_re': -0.5 + nrm((DEPTH, 2, S5_GROUPS, S5_STATE), 0.01),
        's5_a_im': a_im + nrm((DEPTH, 2, S5_GROUPS, S5_STATE), 0.01),
        's5_log_dt': jax.random.uniform(next(it), (DEPTH, 2, S5_GROUPS), F32, math.log(1e-3), math.log(1e-1)),
        's5_b_re': nrm((DEPTH, 2, S5_GROUPS, S5_STATE, S5_GC), (2 * S5_GC) ** -0.5),
        's5_b_im': nrm((DEPTH, 2, S5_GROUPS, S5_STATE, S5_GC), (2 * S5_GC) ** -0.5),
        's5_c_re': nrm((DEPTH, 2, S5_GROUPS, S5_GC, S5_STATE), S5_STATE ** -0.5),
        's5_c_im': nrm((DEPTH, 2, S5_GROUPS, S5_GC, S5_STATE), S5_STATE ** -0.5),
        's5_d': nrm((DEPTH, 2, S5_WIDTH), 1.0),
        's5_glu_w': nrm((DEPTH, S5_WIDTH, 2 * S5_WIDTH), S5_WIDTH ** -0.5),
        's5_glu_b': nrm((DEPTH, 2 * S5_WIDTH), 0.01),
    }


def reference(x, c, ctx, c_ctx, w_ada, b_ada, norm_ffn1, norm_mix, norm_ffn2, norm_final,
              ffn1_wi, ffn1_wo, ffn2_wi, ffn2_wo, w_in, w_gate, b_gate, w_branch, w_out,
              pool_w, pool_scale, q_norm, k_norm,
              hy_short_w, hy_short_b, hy_f1_w, hy_f1_b, hy_f2_w, hy_f2_b, hy_f3_w, hy_freq, hy_bias,
              s5_a_re, s5_a_im, s5_log_dt, s5_b_re, s5_b_im, s5_c_re, s5_c_im, s5_d, s5_glu_w, s5_glu_b):
    n_tok = x.shape[1]
    ROWS = n_tok // GRID_W
    rows = jnp.repeat(jnp.arange(ROWS, dtype=F32), GRID_W)
    cols = jnp.tile(jnp.arange(GRID_W, dtype=F32), ROWS)
    xl, xc = x, ctx
    for l in range(DEPTH):
        last = l == DEPTH - 1
        lp = {
            'w_in': w_in[l], 'w_gate': w_gate[l], 'b_gate': b_gate[l], 'w_branch': w_branch[l], 'w_out': w_out[l],
            'pool_w': pool_w[l], 'pool_scale': pool_scale[l], 'q_norm': q_norm[l], 'k_norm': k_norm[l],
            'hy_short_w': hy_short_w[l], 'hy_short_b': hy_short_b[l], 'hy_f1_w': hy_f1_w[l], 'hy_f1_b': hy_f1_b[l],
            'hy_f2_w': hy_f2_w[l], 'hy_f2_b': hy_f2_b[l], 'hy_f3_w': hy_f3_w[l], 'hy_freq': hy_freq[l],
            'hy_bias': hy_bias[l],
            's5_a_re': s5_a_re[l], 's5_a_im': s5_a_im[l], 's5_log_dt': s5_log_dt[l], 's5_b_re': s5_b_re[l],
            's5_b_im': s5_b_im[l], 's5_c_re': s5_c_re[l], 's5_c_im': s5_c_im[l], 's5_d': s5_d[l],
            's5_glu_w': s5_glu_w[l], 's5_glu_b': s5_glu_b[l],
        }
        mod_l = (jax.nn.silu(c) @ w_ada[l] + b_ada[l]).reshape(-1, N_MOD, 1, D_MODEL)
        mod_c = (jax.nn.silu(c_ctx)[None] @ w_ada[l] + b_ada[l]).reshape(1, N_MOD, 1, D_MODEL)
        xl = ffn_sublayer(xl, mod_l, norm_ffn1[l], ffn1_wi[l], ffn1_wo[l], 0)
        xc = ffn_sublayer(xc, mod_c, norm_ffn1[l], ffn1_wi[l], ffn1_wo[l], 0)
        uc = modulate(xc, norm_mix[l], mod_c[:, 3], mod_c[:, 4])
        ul = modulate(xl, norm_mix[l], mod_l[:, 3], mod_l[:, 4])
        yc, ctx_side = token_mixer(uc, lp, None, None, None, not last)
        yl, _ = token_mixer(ul, lp, ctx_side, rows, cols, True)
        xl = xl + mod_l[:, 5] * yl
        xl = ffn_sublayer(xl, mod_l, norm_ffn2[l], ffn2_wi[l], ffn2_wo[l], 6)
        if not last:
            xc = xc + mod_c[:, 5] * yc
            xc = ffn_sublayer(xc, mod_c, norm_ffn2[l], ffn2_wi[l], ffn2_wo[l], 6)
    return rmsnorm(xl, norm_final)
```

```python
import math
from contextlib import ExitStack, contextmanager
import numpy as np
import ml_dtypes
import concourse.bass as bass
import concourse.mybir as mybir
from concourse.bass_utils import run_bass_kernel_spmd

F32 = mybir.dt.float32
BF16 = mybir.dt.bfloat16
AF = mybir.ActivationFunctionType
ALU = mybir.AluOpType

DM = 2048; SEQ = 2048; CTX = 256; NTOK = SEQ + CTX; DFF = 5632; NMOD = 9; EPS = 1e-6
INW = 3584; OFF_Q = 512; OFF_K = 1024; OFF_V = 1280; OFF_HY = 1536; OFF_S5 = 3072
NPASS = 2; PW = 576; NL = 1152
PI = math.pi
TWO_PI_S = 2.0 * math.pi * 0.999999
I32 = mybir.dt.int32


class Buf:
    __slots__ = ("w", "r")

    def __init__(self):
        self.w = None
        self.r = {}


class View:
    __slots__ = ("ap", "buf")

    def __init__(self, ap, buf):
        self.ap = ap
        self.buf = buf


class Tile:
    def __init__(self, t, buf=None):
        self.t = t
        self.buf = buf if buf is not None else Buf()

    def __getitem__(self, key):
        return View(self.t[key], self.buf)


class DT:
    def __init__(self, ap):
        self.ap = ap

    def __getitem__(self, key):
        return View(self.ap[key], None)


def rev(v):
    a = v.ap
    dims = list(a.ap)
    s, c = dims[-1]
    nd = [list(d) for d in dims[:-1]] + [[-s, c]]
    return View(bass.AP(a.tensor, a.offset + s * (c - 1), nd), v.buf)


def bcast(v, shape):
    return View(v.ap.to_broadcast(list(shape)), v.buf)


class Stream:
    def __init__(self, name, eng, sem):
        self.name = name
        self.eng = eng
        self.sem = sem
        self.count = 0
        self.waited = {}


class Prog:
    NRING = 20

    def __init__(self, nc, es):
        self.nc = nc
        self.es = es
        self.S = {}
        for name, eng in (("pe", nc.tensor), ("act", nc.scalar), ("dve", nc.vector),
                          ("pool", nc.gpsimd), ("sp", nc.sync)):
            sem = es.enter_context(nc.semaphore("sem_" + name))
            self.S[name] = Stream(name, eng, sem)
        self.rings = {}
        for q in ("sp", "pool"):
            sems = [es.enter_context(nc.semaphore("ring_%s_%d" % (q, i))) for i in range(self.NRING)]
            self.rings[q] = {"sems": sems, "n": 0, "cnt": [0] * self.NRING}
        self.ntile = 0
        self.cc_sem = None
        self.cc_count = 0

    @contextmanager
    def scope(self):
        with ExitStack() as es:
            yield es
            self.barrier()

    def sb(self, es, shape, dtype, name=None):
        self.ntile += 1
        t = es.enter_context(self.nc.sbuf_tensor("%s_%d" % (name or "t", self.ntile), list(shape), dtype))
        return Tile(t)

    def ps(self, es, shape, dtype=F32, name=None):
        self.ntile += 1
        t = es.enter_context(self.nc.psum_tensor("%s_%d" % (name or "p", self.ntile), list(shape), dtype))
        return Tile(t)

    def _wait(self, st, ev):
        sem, val = ev
        k = id(sem)
        if st.waited.get(k, 0) >= val:
            return
        st.eng.wait_ge(sem, val)
        st.waited[k] = val

    def _deps(self, st, reads, writes):
        evs = []
        for b in reads:
            if b is not None and b.w is not None:
                evs.append(b.w)
        for b in writes:
            if b is None:
                continue
            if b.w is not None:
                evs.append(b.w)
            evs.extend(b.r.values())
        for ev in evs:
            if st.name == "pe" and ev[0] is st.sem:
                continue
            self._wait(st, ev)

    def _mark(self, key, ev, reads, writes):
        for b in reads:
            if b is not None:
                b.r[key] = ev
        for b in writes:
            if b is not None:
                b.w = ev
                b.r = {}

    def op(self, sname, fn, ins, outs):
        st = self.S[sname]
        reads = [v.buf for v in ins if isinstance(v, View)]
        writes = [v.buf for v in outs]
        self._deps(st, reads, writes)
        inst = fn()
        st.count += 1
        inst.then_inc(st.sem, 1)
        self._mark(sname, (st.sem, st.count), reads, writes)

    def dma(self, q, out, in_, extra_r=(), extra_w=(), slow=False):
        st = self.S[q]
        ring = self.rings[q]
        reads = [in_.buf] + list(extra_r)
        writes = [out.buf] + list(extra_w)
        self._deps(st, reads, writes)
        j = ring["n"] % self.NRING
        ring["n"] += 1
        sem = ring["sems"][j]
        if ring["cnt"][j] > 0:
            self._wait(st, (sem, 16 * ring["cnt"][j]))
        if slow:
            with self.nc.allow_non_contiguous_dma(reason="small strided vector load"):
                inst = st.eng.dma_start(out=out.ap, in_=in_.ap)
        else:
            inst = st.eng.dma_start(out=out.ap, in_=in_.ap)
        inst.then_inc(sem, 16)
        ring["cnt"][j] += 1
        ev = (sem, 16 * ring["cnt"][j])
        self._mark(("dma", q, j), ev, reads, writes)
        return ev

    def barrier(self):
        evs = []
        for st in self.S.values():
            if st.count:
                evs.append((st.sem, st.count))
        for ring in self.rings.values():
            for j, c in enumerate(ring["cnt"]):
                if c:
                    evs.append((ring["sems"][j], 16 * c))
        for st in self.S.values():
            for ev in evs:
                if ev[0] is st.sem:
                    continue
                self._wait(st, ev)

    def allgather(self, in_ap, out_ap):
        self.barrier()
        if self.cc_sem is None:
            self.cc_sem = self.es.enter_context(self.nc.semaphore("cc_sem"))
        CR = 256
        nrow = in_ap.shape[0]
        for k in range(nrow // CR):
            inst = self.nc.gpsimd.collective_compute(
                "AllGather", ALU.bypass, replica_groups=[[2 * i, 2 * i + 1] for i in range(self.ncores // 2)],
                ins=[in_ap[k * CR:(k + 1) * CR, :]], outs=[out_ap[2 * k * CR:2 * (k + 1) * CR, :]])
            self.cc_count += 1
            inst.then_inc(self.cc_sem, 1)
        for st in self.S.values():
            self._wait(st, (self.cc_sem, self.cc_count))

    def mm(self, out, lhsT, rhs, start=True, stop=True):
        self.op("pe", lambda: self.nc.tensor.matmul(out.ap, lhsT.ap, rhs.ap, start=start, stop=stop),
                [lhsT, rhs], [out])

    def transpose(self, out, in_, ident):
        self.op("pe", lambda: self.nc.tensor.transpose(out.ap, in_.ap, ident.ap), [in_, ident], [out])

    def act(self, out, in_, func, bias=0.0, scale=1.0):
        ins = [in_]
        b = bias
        s = scale
        if isinstance(bias, View):
            ins.append(bias); b = bias.ap
        if isinstance(scale, View):
            ins.append(scale); s = scale.ap
        self.op("act", lambda: self.nc.scalar.activation(out=out.ap, in_=in_.ap, func=func, bias=b, scale=s),
                ins, [out])

    def tt(self, out, in0, in1, op, eng="dve"):
        e = self.S[eng].eng
        self.op(eng, lambda: e.tensor_tensor(out=out.ap, in0=in0.ap, in1=in1.ap, op=op), [in0, in1], [out])

    def ts(self, out, in0, s1, s2=None, op0=ALU.mult, op1=None, eng="dve"):
        e = self.S[eng].eng
        ins = [in0]
        a1 = s1
        a2 = s2
        if isinstance(s1, View):
            ins.append(s1); a1 = s1.ap
        if isinstance(s2, View):
            ins.append(s2); a2 = s2.ap
        if op1 is None:
            self.op(eng, lambda: e.tensor_scalar(out=out.ap, in0=in0.ap, scalar1=a1, scalar2=None, op0=op0),
                    ins, [out])
        else:
            self.op(eng, lambda: e.tensor_scalar(out=out.ap, in0=in0.ap, scalar1=a1, scalar2=a2, op0=op0, op1=op1),
                    ins, [out])

    def stt(self, out, in0, scalar, in1, op0, op1, eng="dve"):
        e = self.S[eng].eng
        ins = [in0, in1]
        sc = scalar
        if isinstance(scalar, View):
            ins.append(scalar); sc = scalar.ap
        self.op(eng, lambda: e.scalar_tensor_tensor(out=out.ap, in0=in0.ap, scalar=sc, in1=in1.ap, op0=op0, op1=op1),
                ins, [out])

    def copy(self, out, in_, eng="dve"):
        if eng == "act":
            self.act(out, in_, AF.Copy)
        else:
            e = self.S[eng].eng
            self.op(eng, lambda: e.tensor_copy(out=out.ap, in_=in_.ap), [in_], [out])

    def memset(self, out, val, eng="dve"):
        e = self.S[eng].eng
        self.op(eng, lambda: e.memset(out.ap, val), [], [out])

    def scan(self, out, d0, d1, init):
        ins = [d0, d1]
        iv = init
        if isinstance(init, View):
            ins.append(init); iv = init.ap
        self.op("dve", lambda: self.nc.vector.tensor_tensor_scan(out=out.ap, data0=d0.ap, data1=d1.ap, initial=iv,
                                                                 op0=ALU.mult, op1=ALU.add), ins, [out])

    def recip(self, out, in_):
        self.op("dve", lambda: self.nc.vector.reciprocal(out=out.ap, in_=in_.ap), [in_], [out])


def chunks(n, w=512):
    out = []
    c = 0
    while c < n:
        out.append((c, min(w, n - c)))
        c += w
    return out


def _tables(L):
    N = 2 * L
    f = np.arange(L, dtype=np.float64) + 0.5
    s = np.arange(L, dtype=np.float64)
    ang = 2.0 * np.pi * np.outer(s, f) / N
    fc = np.cos(ang); fs = -np.sin(ang)
    nt = L // 128
    def tile_fwd(t):
        return t.reshape(nt, 128, nt, 128).transpose(2, 1, 0, 3)
    fwd = np.stack([tile_fwd(fc), tile_fwd(fs)], axis=2).reshape(nt, 128, 2 * nt * 128)
    ic = (2.0 / N) * np.cos(ang.T); isn = -(2.0 / N) * np.sin(ang.T)
    TC = min(512, L); ntc = L // TC
    def tile_inv(t):
        return t.reshape(nt, 128, ntc, TC).transpose(2, 1, 0, 3)
    inv = np.stack([tile_inv(ic), tile_inv(isn)], axis=2).reshape(ntc, 128, 2 * nt * TC)
    t = np.linspace(0.0, 1.0, L, dtype=np.float32)[:, None]
    fb = np.linspace(1e-4, 15.0, 16, dtype=np.float32)
    w = (2.0 * np.pi * np.arange(L, dtype=np.float32) / L).astype(np.float32)
    fw = w[:, None] * fb[None, :]
    z = np.concatenate([t, np.cos(fw), -np.sin(fw)], axis=-1).astype(np.float32)
    max_decay = math.log(1e-2) / 0.3
    min_decay = math.log(1e-2) / 1.5
    deltas = np.abs(np.linspace(min_decay, max_decay, 512, dtype=np.float32))
    decay = np.exp(-t * deltas[None, :]).astype(np.float32)
    decay_b = decay.copy(); decay_b[0, :] = 0.0
    dec = np.concatenate([decay, decay_b], axis=1)
    tt = np.arange(L)
    rc = []
    for wdw in (2, 4, 8, 16):
        lo = np.clip(tt - wdw // 2, 0, L); hi = np.clip(tt - wdw // 2 + wdw, 0, L)
        rc.append(1.0 / (hi - lo).astype(np.float32))
    rc = np.stack(rc).astype(np.float32)
    return dict(fwd=fwd.astype(ml_dtypes.bfloat16), inv=inv.astype(ml_dtypes.bfloat16),
                zT=np.ascontiguousarray(z.T), dec=dec, rc=rc)


def _consts():
    c = {}
    for L in (2048, 256):
        tb = _tables(L)
        for k, v in tb.items():
            c["%s%d" % (k, L)] = np.ascontiguousarray(v)
    freqs = (10000.0 ** (-np.arange(32, dtype=np.float32) / 32)).astype(np.float32)
    tpos = np.arange(2048)
    rows = (tpos // 64).astype(np.float32); cols = (tpos % 64).astype(np.float32)
    cosT = np.zeros((128, 2048), np.float32); sinT = np.zeros((128, 2048), np.float32)
    for p in range(128):
        pos = rows if p < 64 else cols
        a = pos * freqs[p % 32]
        cosT[p] = np.cos(a); sinT[p] = np.sin(a)
    c["ropecs"] = np.stack([cosT, sinT], axis=1).reshape(128, 4096).copy()
    P = np.zeros((128, 128), np.float32)
    J = np.zeros((128, 128), np.float32)
    for p in range(128):
        if p % 64 < 32:
            P[p, p + 32] = -1.0
        else:
            P[p, p - 32] = 1.0
        if p < 64:
            J[p, p + 64] = -1.0
        else:
            J[p, p - 64] = 1.0
    mats = np.stack([np.eye(128, dtype=np.float32), P.T.copy(), J.T.copy()], axis=1)
    c["mats"] = mats.reshape(128, 384).copy()
    c["iota1"] = (np.arange(2048, dtype=np.float32) + 1.0).reshape(1, 2048)
    return c


def build(depth, dbg=False, mode="full", ncores=8):
    nc = bass.Bass("TRN2", target_bir_lowering=False)
    IN = {}

    BIG = ("w_ada", "ffn1_wi", "ffn1_wo", "ffn2_wi", "ffn2_wo", "w_in", "w_gate", "w_branch", "w_out")

    def din(name, shape, dt=F32):
        if mode == "mix" and name in BIG:
            return None
        IN[name] = DT(nc.dram_tensor(name, list(shape), dt, kind="ExternalInput").ap())
        return IN[name]

    din("xl", [NL, DM]); din("cvec", [2, DM]); din("sel", [1, 2])
    din("w_ada", [depth, DM, NMOD * DM]); din("b_ada", [depth, NMOD * DM])
    din("norm_ffn1", [depth, DM]); din("norm_mix", [depth, DM]); din("norm_ffn2", [depth, DM]); din("norm_final", [DM])
    din("ffn1_wi", [depth, DM, 2 * DFF]); din("ffn1_wo", [depth, DFF, DM])
    din("ffn2_wi", [depth, DM, 2 * DFF]); din("ffn2_wo", [depth, DFF, DM])
    din("w_in", [depth, DM, INW]); din("w_gate", [depth, DM, 4 * DM]); din("b_gate", [depth, 4 * DM])
    din("w_branch", [depth, 4, 512, DM]); din("w_out", [depth, DM, DM])
    din("pool_w", [depth, 4, 128, 128]); din("pool_scale", [depth, 512])
    din("q_norm", [depth, 128]); din("k_norm", [depth, 128])
    din("hy_short_w", [depth, 3, 1536]); din("hy_short_b", [depth, 1536])
    din("hy_f1_w", [depth, 33, 64]); din("hy_f1_b", [depth, 64]); din("hy_f2_w", [depth, 64, 64]); din("hy_f2_b", [depth, 64])
    din("hy_f3_w", [depth, 64, 1024]); din("hy_freq", [depth, 64]); din("hy_bias", [depth, 512])
    NG = 16
    din("s5_a_re", [depth, 2, NG, 64]); din("s5_a_im", [depth, 2, NG, 64]); din("s5_log_dt", [depth, 2, NG])
    din("s5_bT_re", [depth, 2, NG, 16, 64]); din("s5_bT_im", [depth, 2, NG, 16, 64])
    din("s5_cT_re", [depth, 2, NG, 64, 16]); din("s5_cT_im", [depth, 2, NG, 64, 16])
    din("s5_d", [depth, 2, 256]); din("s5_glu_w", [depth, 512, 1024]); din("s5_glu_b", [depth, 1024])
    for L in (2048, 256):
        nt = L // 128; TC = min(512, L); ntc = L // TC
        din("fwd%d" % L, [nt, 128, 2 * nt * 128], BF16); din("inv%d" % L, [ntc, 128, 2 * nt * TC], BF16)
        din("zT%d" % L, [33, L]); din("dec%d" % L, [L, 1024]); din("rc%d" % L, [4, L])
    din("ropecs", [128, 4096]); din("mats", [128, 384]); din("iota1", [1, 2048])
    OUT = DT(nc.dram_tensor("out", [NL, DM], F32, kind="ExternalOutput").ap())
    xT = DT(nc.dram_tensor("xT", [DM, NL], F32).ap())
    projL = [DT(nc.dram_tensor("projL%d" % i, [INW, NL], F32).ap()) for i in range(2)]
    projG = DT(nc.dram_tensor("projG", [2 * INW, NL], F32).ap())
    ybr = [DT(nc.dram_tensor("ybr%d" % i, [DM, NL], BF16).ap()) for i in range(2)]

    def seq_pieces(c0, L):
        out = []
        if c0 < NL:
            w = min(c0 + L, NL) - c0
            out.append((0, c0, w, 0))
        if c0 + L > NL:
            b0 = max(c0, NL) - NL
            out.append((1, b0, c0 + L - NL - b0, max(c0, NL) - c0))
        return out

    def ldp(dstf, r0, c0, L):
        for (h, b0, w, off) in seq_pieces(c0, L):
            g0 = (r0 // 256) * 512 + h * 256 + (r0 % 256)
            P.dma("sp", dstf(off, off + w), projG[g0:g0 + 128, b0:b0 + w])

    def sty(r0, c0, L, srcf):
        for (h, b0, w, off) in seq_pieces(c0, L):
            P.dma("sp", ybr[h][r0:r0 + 128, b0:b0 + w], srcf(off, off + w))
    s5scr = DT(nc.dram_tensor("s5scr", [2, 4, 16, 64], F32).ap())
    g16L = DT(nc.dram_tensor("g16L", [256, NTOK], BF16).ap())
    g16G = DT(nc.dram_tensor("g16G", [512, NTOK], BF16).ap())
    agL = DT(nc.dram_tensor("agL", [256, NTOK], BF16).ap())
    agG = DT(nc.dram_tensor("agG", [512, NTOK], BF16).ap())
    DBG = {}
    if dbg:
        for nm, shp, dt_ in (("d_xT", [DM, NL], F32), ("d_projT", [2 * INW, NL], F32), ("d_ybr0", [DM, NL], BF16),
                             ("d_ybr1", [DM, NL], BF16), ("d_xT2", [DM, NL], F32)):
            DBG[nm] = DT(nc.dram_tensor(nm, shp, dt_, kind="ExternalOutput").ap())

    top = ExitStack()
    with top:
        P = Prog(nc, top)
        P.ncores = ncores
        PS = [P.ps(top, [128, 512], F32, "bank") for _ in range(8)]
        mats = P.sb(top, [128, 3, 128], F32, "mats")
        P.dma("sp", mats[:], View(IN["mats"].ap.rearrange("p (a b) -> p a b", a=3), None))
        ident = mats[:, 0, :]; ropePT = mats[:, 1, :]; JT = mats[:, 2, :]
        ones_dm = P.sb(top, [128, 128], BF16, "ones_dm"); P.memset(ones_dm[:], 1.0 / DM)
        ones_hd = P.sb(top, [128, 128], BF16, "ones_hd"); P.memset(ones_hd[:], 1.0 / 128)
        ones1 = P.sb(top, [128, 128], BF16, "ones1"); P.memset(ones1[:], 1.0)
        modT = P.sb(top, [128, NMOD * 16, 2], F32, "modT")
        Acoef = P.sb(top, [128, 3, 16, 2], F32, "Acoef")
        gateh = P.sb(top, [128, 3, 16, 2], F32, "gateh")
        ncols = P.sb(top, [128, 3, 16], F32, "ncols")
        nfin = P.sb(top, [128, 16], F32, "nfin")
        bgate = P.sb(top, [128, 64], F32, "bgate")
        P.dma("sp", nfin[:], View(IN["norm_final"].ap.rearrange("(t p) -> p t", p=128), None), slow=True)

        def colload(dst, src_ap):
            P.dma("sp", dst, View(src_ap.rearrange("(t p) -> p t", p=128), None), slow=True)

        def segs_of(ps_):
            if ps_ == 0:
                return [(0, 256, 1), (256, 256, 0), (512, 64, 0)]
            return [(0, 512, 0), (512, 64, 0)]

        def modsegs(ps_):
            return [(0, 256, 1), (256, PW - 256, 0)] if ps_ == 0 else [(0, PW, 0)]

        selc = P.sb(top, [128, 2], F32, "selc")
        P.dma("sp", selc[:], View(IN["sel"].ap.broadcast_to([128, 2]), None))
        with P.scope() as es:
            stg = [P.sb(es, [128, DM], F32, "xin") for _ in range(2)]
            oT = [P.sb(es, [128, 16, 128], F32, "xo") for _ in range(2)]
            for ti in range(NL // 128):
                s = stg[ti % 2]; o = oT[ti % 2]
                src = IN["xl"][ti * 128:(ti + 1) * 128, :]
                P.dma("sp", s[:], src)
                for dq in range(4):
                    bank = PS[(ti * 4 + dq) % 8]
                    for di in range(4):
                        d = dq * 4 + di
                        P.transpose(bank[:, di * 128:(di + 1) * 128], s[:, d * 128:(d + 1) * 128], ident)
                    P.copy(View(o.t[:, dq * 4:(dq + 1) * 4, :], o.buf),
                           View(bank.t[:, :].rearrange("p (a b) -> p a b", a=4), bank.buf),
                           eng="act" if dq % 2 else "dve")
                P.dma("sp", View(xT.ap.rearrange("(d p) t -> p d t", p=128)[:, :, ti * 128:(ti + 1) * 128], None), o[:])
        P.barrier()

        def load_X(X, ps_):
            c0 = ps_ * PW
            for dt_ in range(16):
                P.dma("sp", X[:, dt_, :], xT[dt_ * 128:(dt_ + 1) * 128, c0:c0 + PW])

        def store_X(X, ps_):
            c0 = ps_ * PW
            for dt_ in range(16):
                P.dma("sp", xT[dt_ * 128:(dt_ + 1) * 128, c0:c0 + PW], X[:, dt_, :])

        def norm_mod(es, X, U, ps_, sub, lidx):
            sq = [P.sb(es, [128, PW], BF16, "sq") for _ in range(2)]
            RS = P.sb(es, [128, PW], F32, "RS")
            tmp = [P.sb(es, [128, PW], F32, "nt") for _ in range(2)]
            cks = chunks(PW)
            for dt_ in range(16):
                s = sq[dt_ % 2]
                P.act(s[:], X[:, dt_, :], AF.Square)
                for ci, (c0, cw) in enumerate(cks):
                    P.mm(PS[ci][:, 0:cw], ones_dm[:], s[:, c0:c0 + cw], start=(dt_ == 0), stop=(dt_ == 15))
            for ci, (c0, cw) in enumerate(cks):
                P.act(RS[:, c0:c0 + cw], PS[ci][:, 0:cw], AF.Sqrt, bias=EPS)
                P.recip(RS[:, c0:c0 + cw], RS[:, c0:c0 + cw])
            for dt_ in range(16):
                t = tmp[dt_ % 2]
                P.tt(t[:], X[:, dt_, :], RS[:], ALU.mult)
                for (s0, sw, mi) in modsegs(ps_):
                    P.act(U[:, dt_, s0:s0 + sw], t[:, s0:s0 + sw], AF.Identity,
                          bias=modT[:, (3 * sub) * 16 + dt_, mi:mi + 1], scale=Acoef[:, sub, dt_, mi:mi + 1])

        def ffn(es, X, U, ps_, wi, wo, sub, l):
            GT = 4
            wa = [P.sb(es, [128, 16, GT * 128], BF16, "wa") for _ in range(2)]
            wb = [P.sb(es, [128, 16, GT * 128], BF16, "wb") for _ in range(2)]
            wot = [P.sb(es, [128, GT, DM], BF16, "wo") for _ in range(2)]
            H = [P.sb(es, [128, GT, PW], BF16, "H") for _ in range(2)]
            sa = [P.sb(es, [128, 512], F32, "sa") for _ in range(2)]
            cks = chunks(PW)
            wiv = wi.ap[l].rearrange("(k p) n -> p k n", p=128)
            wov = wo.ap[l].rearrange("(j p) n -> p j n", p=128)
            ngrp = DFF // (GT * 128)
            cnt = 0
            for g in range(ngrp):
                sl = g % 2
                P.dma("pool", wa[sl][:], View(wiv[:, :, g * GT * 128:(g + 1) * GT * 128], None))
                P.dma("pool", wb[sl][:], View(wiv[:, :, DFF + g * GT * 128:DFF + (g + 1) * GT * 128], None))
                P.dma("pool", wot[sl][:], View(wov[:, g * GT:(g + 1) * GT, :], None))
                for j in range(GT):
                    for ci, (c0, cw) in enumerate(cks):
                        pa = PS[(cnt % 2) * 2]; pb = PS[(cnt % 2) * 2 + 1]; cnt += 1
                        for kt in range(16):
                            P.mm(pa[:, 0:cw], wa[sl][:, kt, j * 128:(j + 1) * 128], U[:, kt, c0:c0 + cw],
                                 start=(kt == 0), stop=(kt == 15))
                        for kt in range(16):
                            P.mm(pb[:, 0:cw], wb[sl][:, kt, j * 128:(j + 1) * 128], U[:, kt, c0:c0 + cw],
                                 start=(kt == 0), stop=(kt == 15))
                        s_ = sa[cnt % 2]
                        P.act(s_[:, 0:cw], pa[:, 0:cw], AF.Silu)
                        P.tt(H[sl][:, j, c0:c0 + cw], s_[:, 0:cw], pb[:, 0:cw], ALU.mult)
                oc = 0
                for dt_ in range(16):
                    for ci, (c0, cw) in enumerate(cks):
                        po = PS[4 + (oc % 4)]; oc += 1
                        for j in range(GT):
                            P.mm(po[:, 0:cw], wot[sl][:, j, dt_ * 128:(dt_ + 1) * 128], H[sl][:, j, c0:c0 + cw],
                                 start=(j == 0), stop=(j == GT - 1))
                        for (s0, sw, mi) in segs_of(ps_):
                            if s0 < c0 or s0 >= c0 + cw:
                                continue
                            P.stt(X[:, dt_, s0:s0 + sw], po[:, s0 - c0:s0 - c0 + sw], gateh[:, sub, dt_, mi:mi + 1],
                                  X[:, dt_, s0:s0 + sw], ALU.mult, ALU.add)

        for l in range(depth):
            last = (l == depth - 1)
            with P.scope() as es:
                cs = P.sb(es, [128, 16, 2], F32, "cs")
                csb = P.sb(es, [128, 16, 2], BF16, "csb")
                bada = P.sb(es, [128, NMOD * 16], F32, "bada")
                for r in range(2):
                    colload(cs[:, :, r], IN["cvec"].ap[r])
                P.act(csb[:], cs[:], AF.Silu)
                for i in range(NMOD):
                    colload(bada[:, i * 16:(i + 1) * 16], IN["b_ada"].ap[l][i * DM:(i + 1) * DM])
                for i, nm in enumerate(("norm_ffn1", "norm_mix", "norm_ffn2")):
                    colload(ncols[:, i, :], IN[nm].ap[l])
                for n_ in range(4):
                    colload(bgate[:, n_ * 16:(n_ + 1) * 16], IN["b_gate"].ap[l][n_ * DM:(n_ + 1) * DM])
                wsl = [P.sb(es, [128, 16, 512], BF16, "wada") for _ in range(2)]
                wv = IN["w_ada"].ap[l].rearrange("(k p) n -> p k n", p=128)
                for blk in range(NMOD * DM // 512):
                    w = wsl[blk % 2]
                    P.dma("pool", w[:], View(wv[:, :, blk * 512:(blk + 1) * 512], None))
                    bank = PS[blk % 4]
                    for fi in range(4):
                        for kt in range(16):
                            P.mm(bank[:, fi * 2:fi * 2 + 2], w[:, kt, fi * 128:(fi + 1) * 128], csb[:, kt, :],
                                 start=(kt == 0), stop=(kt == 15))
                    P.tt(modT[:, blk * 4:(blk + 1) * 4, :],
                         View(bank.t[:, 0:8].rearrange("p (a b) -> p a b", b=2), bank.buf),
                         View(bada.t[:, blk * 4:(blk + 1) * 4].unsqueeze(2).to_broadcast([128, 4, 2]), bada.buf),
                         ALU.add)
                for sub in range(3):
                    sc = modT[:, (3 * sub + 1) * 16:(3 * sub + 2) * 16, :]
                    P.ts(Acoef[:, sub, :, :], sc, 1.0, None, op0=ALU.add)
                    P.tt(Acoef[:, sub, :, :], Acoef[:, sub, :, :],
                         View(ncols.t[:, sub, :].unsqueeze(2).to_broadcast([128, 16, 2]), ncols.buf), ALU.mult)
                    gt_ = modT[:, (3 * sub + 2) * 16:(3 * sub + 3) * 16, :]
                    P.ts(gateh[:, sub, :, :], gt_, 1.0 if sub == 1 else 0.5, None, op0=ALU.mult)
            P.barrier()


            for ps_ in range(NPASS):
                with P.scope() as es:
                    X = P.sb(es, [128, 16, PW], F32, "X")
                    U = P.sb(es, [128, 16, PW], BF16, "U")
                    load_X(X, ps_)
                    with P.scope() as es2:
                        norm_mod(es2, X, U, ps_, 0, l)
                    with P.scope() as es2:
                        ffn(es2, X, U, ps_, IN["ffn1_wi"], IN["ffn1_wo"], 0, l)
                    P.barrier()

                    with P.scope() as es2:
                        norm_mod(es2, X, U, ps_, 1, l)
                    with P.scope() as es2:
                        wsl = [P.sb(es2, [128, 16, 256], BF16, "win") for _ in range(2)]
                        pj = [P.sb(es2, [128, PW], F32, "pj") for _ in range(2)]
                        wv = IN["w_in"].ap[l].rearrange("(k p) n -> p k n", p=128)
                        for blk in range(INW // 256):
                            w = wsl[blk % 2]
                            P.dma("pool", w[:], View(wv[:, :, blk * 256:(blk + 1) * 256], None))
                            for j in range(2):
                                ct = blk * 2 + j
                                o = pj[ct % 2]
                                for ci, (c0, cw) in enumerate(chunks(PW)):
                                    bank = PS[(ct * 2 + ci) % 8]
                                    for kt in range(16):
                                        P.mm(bank[:, 0:cw], w[:, kt, j * 128:(j + 1) * 128], U[:, kt, c0:c0 + cw],
                                             start=(kt == 0), stop=(kt == 15))
                                    P.copy(o[:, c0:c0 + cw], bank[:, 0:cw], eng="act" if ci else "dve")
                                P.dma("sp", projL[l % 2][ct * 128:(ct + 1) * 128, ps_ * PW:(ps_ + 1) * PW], o[:])
                    store_X(X, ps_)
                P.barrier()
            P.allgather(projL[l % 2].ap[:, :], projG.ap[:, :])
            if dbg and l == 0:
                P.dma("sp", DBG["d_xT"][:, :], xT[:, :]); P.dma("sp", DBG["d_projT"][:, :], projG[:, :])
                P.barrier()

            mixers(P, nc, IN, PS, l, False, ldp, sty, s5scr, ident, ropePT, JT, ones_hd, ones1, colload, DBG,
                   selc=selc, g16L=g16L, g16G=g16G, agL=agL, agG=agG)
            P.barrier()
            if dbg and l == 0:
                P.dma("sp", DBG["d_ybr0"][:, :], ybr[0][:, :]); P.dma("sp", DBG["d_ybr1"][:, :], ybr[1][:, :])
                P.barrier()

            for ps_ in range(NPASS):
                with P.scope() as es:
                    X = P.sb(es, [128, 16, PW], F32, "X")
                    U = P.sb(es, [128, 16, PW], BF16, "U")
                    load_X(X, ps_)
                    with P.scope() as es2:
                        norm_mod(es2, X, U, ps_, 1, l)
                    with P.scope() as es2:
                        MG = P.sb(es2, [128, 16, PW], BF16, "MG")
                        with P.scope() as es3:
                            YB = P.sb(es3, [128, 16, PW], BF16, "YB")
                            ya = [P.sb(es3, [128, PW], BF16, "ya") for _ in range(2)]
                            yb_ = [P.sb(es3, [128, PW], BF16, "yb") for _ in range(2)]
                            for dt_ in range(16):
                                a_ = ya[dt_ % 2]; b_ = yb_[dt_ % 2]
                                if 4 <= dt_ < 8:
                                    P.dma("sp", a_[:], agG[(dt_ - 4) * 128:(dt_ - 3) * 128, ps_ * PW:(ps_ + 1) * PW])
                                    P.dma("sp", b_[:], agG[(dt_ - 4) * 128:(dt_ - 3) * 128, NL + ps_ * PW:NL + (ps_ + 1) * PW])
                                else:
                                    P.dma("sp", a_[:], ybr[0][dt_ * 128:(dt_ + 1) * 128, ps_ * PW:(ps_ + 1) * PW])
                                    P.dma("sp", b_[:], ybr[1][dt_ * 128:(dt_ + 1) * 128, ps_ * PW:(ps_ + 1) * PW])
                                P.ts(a_[:], a_[:], selc[:, 0:1], None, op0=ALU.mult)
                                P.stt(YB[:, dt_, :], b_[:], selc[:, 1:2], a_[:], ALU.mult, ALU.add)
                            wg = [P.sb(es3, [128, 16, 256], BF16, "wg") for _ in range(2)]
                            wbr = [P.sb(es3, [128, 4, 256], BF16, "wbr") for _ in range(2)]
                            acc = P.sb(es3, [128, 2, PW], F32, "acc")
                            sg = [P.sb(es3, [128, 512], F32, "sg") for _ in range(2)]
                            wgv = IN["w_gate"].ap[l].rearrange("(k p) n -> p k n", p=128)
                            it = 0
                            for mq in range(8):
                                for n_ in range(4):
                                    sl = it % 2; it += 1
                                    P.dma("pool", wg[sl][:], View(wgv[:, :, n_ * DM + mq * 256:n_ * DM + (mq + 1) * 256], None))
                                    P.dma("pool", wbr[sl][:], View(IN["w_branch"].ap[l][n_].rearrange("(k p) n -> p k n", p=128)[:, :, mq * 256:(mq + 1) * 256], None))
                                    for mi in range(2):
                                        m = mq * 2 + mi
                                        for ci, (c0, cw) in enumerate(chunks(PW)):
                                            pg = PS[(it * 4 + mi * 2 + ci) % 4]; pp = PS[4 + (it * 4 + mi * 2 + ci) % 4]
                                            for kt in range(16):
                                                P.mm(pg[:, 0:cw], wg[sl][:, kt, mi * 128:(mi + 1) * 128], U[:, kt, c0:c0 + cw],
                                                     start=(kt == 0), stop=(kt == 15))
                                            for k4 in range(4):
                                                P.mm(pp[:, 0:cw], wbr[sl][:, k4, mi * 128:(mi + 1) * 128], YB[:, n_ * 4 + k4, c0:c0 + cw],
                                                     start=(k4 == 0), stop=(k4 == 3))
                                            s_ = sg[(mi * 2 + ci) % 2]
                                            P.act(s_[:, 0:cw], pg[:, 0:cw], AF.Sigmoid, bias=bgate[:, n_ * 16 + m:n_ * 16 + m + 1])
                                            if n_ == 0:
                                                P.tt(acc[:, mi, c0:c0 + cw], s_[:, 0:cw], pp[:, 0:cw], ALU.mult)
                                            else:
                                                P.tt(s_[:, 0:cw], s_[:, 0:cw], pp[:, 0:cw], ALU.mult)
                                                P.tt(acc[:, mi, c0:c0 + cw], acc[:, mi, c0:c0 + cw], s_[:, 0:cw], ALU.add)
                                    if n_ == 3:
                                        for mi in range(2):
                                            P.copy(MG[:, mq * 2 + mi, :], acc[:, mi, :], eng="act")
                        P.barrier()
                        with P.scope() as es3:
                            wsl = [P.sb(es3, [128, 16, 256], BF16, "wout") for _ in range(2)]
                            wv = IN["w_out"].ap[l].rearrange("(k p) n -> p k n", p=128)
                            oc = 0
                            for blk in range(8):
                                w = wsl[blk % 2]
                                P.dma("pool", w[:], View(wv[:, :, blk * 256:(blk + 1) * 256], None))
                                for j in range(2):
                                    m2 = blk * 2 + j
                                    for ci, (c0, cw) in enumerate(chunks(PW)):
                                        po = PS[oc % 8]; oc += 1
                                        for kt in range(16):
                                            P.mm(po[:, 0:cw], w[:, kt, j * 128:(j + 1) * 128], MG[:, kt, c0:c0 + cw],
                                                 start=(kt == 0), stop=(kt == 15))
                                        for (s0, sw, mi) in segs_of(ps_):
                                            if s0 < c0 or s0 >= c0 + cw:
                                                continue
                                            P.stt(X[:, m2, s0:s0 + sw], po[:, s0 - c0:s0 - c0 + sw], gateh[:, 1, m2, mi:mi + 1],
                                                  X[:, m2, s0:s0 + sw], ALU.mult, ALU.add)
                    P.barrier()
                    with P.scope() as es2:
                        norm_mod(es2, X, U, ps_, 2, l)
                    with P.scope() as es2:
                        ffn(es2, X, U, ps_, IN["ffn2_wi"], IN["ffn2_wo"], 2, l)
                    P.barrier()
                    if not last:
                        store_X(X, ps_)
                    else:
                        if dbg:
                            store_X(X, ps_)
                        with P.scope() as es2:
                            sq = [P.sb(es2, [128, PW], BF16, "sq") for _ in range(2)]
                            RS = P.sb(es2, [128, PW], F32, "RS")
                            yo = [P.sb(es2, [128, DM], F32, "yo") for _ in range(2)]
                            cks = chunks(PW)
                            for dt_ in range(16):
                                s = sq[dt_ % 2]
                                P.act(s[:], X[:, dt_, :], AF.Square)
                                for ci, (c0, cw) in enumerate(cks):
                                    P.mm(PS[ci][:, 0:cw], ones_dm[:], s[:, c0:c0 + cw], start=(dt_ == 0), stop=(dt_ == 15))
                            for ci, (c0, cw) in enumerate(cks):
                                P.act(RS[:, c0:c0 + cw], PS[ci][:, 0:cw], AF.Sqrt, bias=EPS)
                                P.recip(RS[:, c0:c0 + cw], RS[:, c0:c0 + cw])
                            for dt_ in range(16):
                                P.stt(X[:, dt_, :], X[:, dt_, :], nfin[:, dt_:dt_ + 1], RS[:], ALU.mult, ALU.mult)
                            P.barrier()
                            for tt_, (k0, kw) in enumerate(chunks(PW, 128)):
                                o = yo[tt_ % 2]
                                for dq in range(4):
                                    bank = PS[(tt_ * 4 + dq) % 8]
                                    for di in range(4):
                                        d = dq * 4 + di
                                        P.transpose(bank[0:kw, di * 128:(di + 1) * 128], X[:, d, k0:k0 + kw], ident)
                                    P.copy(o[0:kw, dq * 512:(dq + 1) * 512], bank[0:kw, :], eng="act" if dq % 2 else "dve")
                                tok0 = ps_ * PW + k0
                                P.dma("sp", OUT[tok0:tok0 + kw, :], o[0:kw, :])
                P.barrier()
            if dbg and l == 0:
                P.dma("sp", DBG["d_xT2"][:, :], xT[:, :])
                P.barrier()
        P.barrier()
    return nc


def mixers(P, nc, IN, PS, l, last, ldp, sty, s5scr, ident, ropePT, JT, ones_hd, ones1, colload, DBG=None,
           selc=None, g16L=None, g16G=None, agL=None, agG=None):
    NG = 16; NCT = 2

    def ldsel(dst, tmp, row_lo, row_hi, c0, L):
        ldp(lambda a, b: dst[:, a:b], row_lo, c0, L)
        ldp(lambda a, b: tmp[:, a:b], row_hi, c0, L)
        P.ts(dst[:, 0:L], dst[:, 0:L], selc[:, 0:1], None, op0=ALU.mult)
        P.stt(dst[:, 0:L], tmp[:, 0:L], selc[:, 1:2], dst[:, 0:L], ALU.mult, ALU.add)

    SEQS = [(0, CTX), (CTX, SEQ)]

    with P.scope() as es:
        OFFP = 16
        pw_ = P.sb(es, [128, 4, 128], BF16, "poolw")
        P.dma("pool", pw_[:], View(IN["pool_w"].ap[l].rearrange("g c d -> c g d"), None))
        pscale = P.sb(es, [128, 4], F32, "pscale")
        colload(pscale[:], IN["pool_scale"].ap[l])
        A = P.sb(es, [128, SEQ + 32], F32, "pa")
        W1 = P.sb(es, [128, SEQ + 32], F32, "pw1")
        W2 = P.sb(es, [128, SEQ + 32], F32, "pw2")
        RC = P.sb(es, [128, SEQ], F32, "prc")
        PO = P.sb(es, [128, SEQ], BF16, "ppo")
        YO = P.sb(es, [128, SEQ], BF16, "pyo")
        for gi in range(4):
            for (c0, L) in SEQS:
                P.memset(A[:, 0:OFFP], 0.0); P.memset(A[:, OFFP + L:OFFP + L + 16], 0.0)
                ldp(lambda a, b: A[:, OFFP + a:OFFP + b], gi * 128, c0, L)
                P.dma("sp", RC[:, 0:L], View(IN["rc%d" % L].ap[gi:gi + 1, :].broadcast_to([128, L]), None))
                lo, hi = OFFP - 8, OFFP + L + 8
                P.tt(W1[:, lo:hi], A[:, lo - 1:hi - 1], A[:, lo:hi], ALU.add)
                cur, nxt = W1, W2
                half = 1
                for lev in range(gi):
                    lo += 2; hi -= 2
                    if lev == 2:
                        lo, hi = OFFP, OFFP + L
                    P.tt(nxt[:, lo:hi], cur[:, lo - half:hi - half], cur[:, lo + half:hi + half], ALU.add)
                    cur, nxt = nxt, cur
                    half *= 2
                P.tt(nxt[:, OFFP:OFFP + L], cur[:, OFFP:OFFP + L], RC[:, 0:L], ALU.mult)
                P.tt(PO[:, 0:L], nxt[:, OFFP:OFFP + L], A[:, OFFP:OFFP + L], ALU.subtract)
                for ci, (k0, kw) in enumerate(chunks(L)):
                    bank = PS[ci % 8]
                    P.mm(bank[:, 0:kw], pw_[:, gi, :], PO[:, k0:k0 + kw])
                    P.act(YO[:, k0:k0 + kw], bank[:, 0:kw], AF.Identity, scale=pscale[:, gi:gi + 1])
                sty(gi * 128, c0, L, lambda a, b: YO[:, a:b])
    P.barrier()

    with P.scope() as es:
        qn = P.sb(es, [128, 1], F32, "qn"); kn = P.sb(es, [128, 1], F32, "kn")
        colload(qn[:], IN["q_norm"].ap[l]); colload(kn[:], IN["k_norm"].ap[l])
        CS = P.sb(es, [128, 2, SEQ], F32, "ropecs")
        P.dma("sp", CS[:], View(IN["ropecs"].ap.rearrange("p (a b) -> p a b", a=2), None))
        Q16 = P.sb(es, [128, 2, NTOK], BF16, "Q16")
        K16 = P.sb(es, [128, 1, NTOK], BF16, "K16")
        Vt = P.sb(es, [128, NTOK // 128, 128], BF16, "Vt")
        with P.scope() as es2:
            raw = [P.sb(es2, [128, NTOK], F32, "raw") for _ in range(2)]
            raw2 = P.sb(es2, [128, NTOK], F32, "raw2")
            sq = P.sb(es2, [128, NTOK], BF16, "asq")
            RS = P.sb(es2, [128, NTOK], F32, "aRS")
            t1 = P.sb(es2, [128, 512], F32, "at1"); t2 = P.sb(es2, [128, 512], F32, "at2")
            for hi_ in range(3):
                r = raw[hi_ % 2]
                if hi_ < 2:
                    ldsel(r, raw2, OFF_Q + hi_ * 128, OFF_Q + (2 + hi_) * 128, 0, NTOK)
                else:
                    ldsel(r, raw2, OFF_K, OFF_K + 128, 0, NTOK)
                gcol = qn if hi_ < 2 else kn
                dst = Q16 if hi_ < 2 else K16
                hh = hi_ if hi_ < 2 else 0
                P.act(sq[:], r[:], AF.Square)
                for ci, (k0, kw) in enumerate(chunks(NTOK)):
                    bank = PS[ci % 4]
                    P.mm(bank[:, 0:kw], ones_hd[:], sq[:, k0:k0 + kw])
                    P.act(RS[:, k0:k0 + kw], bank[:, 0:kw], AF.Sqrt, bias=EPS)
                    P.recip(RS[:, k0:k0 + kw], RS[:, k0:k0 + kw])
                P.stt(r[:], r[:], gcol[:, 0:1], RS[:], ALU.mult, ALU.mult)
                P.copy(dst[:, hh, 0:CTX], r[:, 0:CTX], eng="act")
                for ci, (k0, kw) in enumerate(chunks(SEQ)):
                    bank = PS[4 + ci % 4]
                    P.mm(bank[:, 0:kw], ropePT, r[:, CTX + k0:CTX + k0 + kw])
                    P.tt(t1[:, 0:kw], r[:, CTX + k0:CTX + k0 + kw], CS[:, 0, k0:k0 + kw], ALU.mult)
                    P.tt(t2[:, 0:kw], bank[:, 0:kw], CS[:, 1, k0:k0 + kw], ALU.mult)
                    P.tt(dst[:, hh, CTX + k0:CTX + k0 + kw], t1[:, 0:kw], t2[:, 0:kw], ALU.add)
            for hk in range(1):
                r = raw[hk % 2]
                ldsel(r, raw2, OFF_V, OFF_V + 128, 0, NTOK)
                for kt in range(NTOK // 128):
                    bank = PS[kt % 8]
                    P.transpose(bank[:, 0:128], r[:, kt * 128:(kt + 1) * 128], ident)
                    P.copy(Vt[:, kt, hk * 128:(hk + 1) * 128], bank[:, 0:128], eng="act" if kt % 2 else "dve")
        P.barrier()
        with P.scope() as es2:
            Pt = [P.sb(es2, [128, 512], BF16, "Pt") for _ in range(3)]
            rd = P.sb(es2, [128, 512], F32, "rd")
            O = [P.sb(es2, [128, 512], BF16, "O") for _ in range(2)]
            sc = 1.0 / math.sqrt(128.0)
            it = 0
            for h in range(2):
                kv = 0
                qcs = [(0, CTX, [0, 1])] + [(CTX + k0, kw, list(range(NTOK // 128))) for (k0, kw) in chunks(SEQ)]
                if last:
                    qcs = qcs[1:]
                for (q0, qw, kts) in qcs:
                    po = PS[4 + (it % 2) * 2]; pd = PS[5 + (it % 2) * 2]; it += 1
                    for i, kt in enumerate(kts):
                        pss = PS[i % 4]
                        P.mm(pss[:, 0:qw], K16[:, kv, kt * 128:(kt + 1) * 128], Q16[:, h, q0:q0 + qw])
                        pt = Pt[i % 3]
                        P.act(pt[:, 0:qw], pss[:, 0:qw], AF.Exp, bias=-11.3, scale=sc)
                        P.mm(po[:, 0:qw], Vt[:, kt, kv * 128:(kv + 1) * 128], pt[:, 0:qw], start=(i == 0), stop=(i == len(kts) - 1))
                        P.mm(pd[:, 0:qw], ones1[:], pt[:, 0:qw], start=(i == 0), stop=(i == len(kts) - 1))
                    P.recip(rd[:, 0:qw], pd[:, 0:qw])
                    o = O[it % 2]
                    P.tt(o[:, 0:qw], po[:, 0:qw], rd[:, 0:qw], ALU.mult)
                    P.dma("sp", agL[h * 128:(h + 1) * 128, q0:q0 + qw], o[:, 0:qw])
    P.allgather(agL.ap[:, :], agG.ap[:, :])

    with P.scope() as es:
        hw = P.sb(es, [128, 4, 12], F32, "hyw")
        for tap in range(3):
            colload(hw[:, tap, :], IN["hy_short_w"].ap[l][tap])
        colload(hw[:, 3, :], IN["hy_short_b"].ap[l])
        hbias = P.sb(es, [128, 4], F32, "hybias"); colload(hbias[:], IN["hy_bias"].ap[l])
        f1w = P.sb(es, [33, 64], F32, "f1w"); P.dma("sp", f1w[:], View(IN["hy_f1_w"].ap[l], None))
        f2w = P.sb(es, [64, 64], F32, "f2w"); P.dma("sp", f2w[:], View(IN["hy_f2_w"].ap[l], None))
        f3w = P.sb(es, [64, 1024], F32, "f3w"); P.dma("sp", f3w[:], View(IN["hy_f3_w"].ap[l], None))
        fv = P.sb(es, [64, 3], F32, "fv")
        for i, nm in enumerate(("hy_f1_b", "hy_f2_b", "hy_freq")):
            P.dma("sp", fv[:, i:i + 1], View(IN[nm].ap[l].rearrange("(p o) -> p o", o=1), None), slow=True)
        fb = P.sb(es, [64, 3], F32, "fb")
        P.ts(fb[:, 0:2], fv[:, 0:2], fv[:, 2:3], 1.0 / (2.0 * PI), op0=ALU.mult, op1=ALU.mult)
        P.ts(fb[:, 2:3], fv[:, 2:3], 1.0 / (2.0 * PI), None, op0=ALU.mult)
        for (c0, L) in SEQS:
            if last and c0 == 0:
                continue
            nt = L // 128; TC = min(512, L); ntc = L // TC
            with P.scope() as es1:
                vxT = P.sb(es1, [128, 4, L], F32, "vxT")
                Y = P.sb(es1, [128, nt, 2, 512], BF16, "hyY")
                with P.scope() as es2:
                    AB = P.sb(es2, [128, nt, 2, 512], BF16, "hyAB")
                    vx = P.sb(es2, [128, nt, 512], BF16, "vxtok")
                    with P.scope() as es3:
                        zT = P.sb(es3, [33, L], F32, "zT"); P.dma("sp", zT[:], View(IN["zT%d" % L].ap, None))
                        h1 = P.sb(es3, [64, L], F32, "h1"); h2 = P.sb(es3, [64, L], F32, "h2")
                        hki = P.sb(es3, [64, L], I32, "hki")
                        dec = [P.sb(es3, [128, 1024], F32, "dec") for _ in range(2)]
                        hh = P.sb(es3, [128, 1024], F32, "hh")
                        for (wt, src, dst, bi) in ((f1w, zT, h1, 0), (f2w, h1, h2, 1)):
                            for ci, (k0, kw) in enumerate(chunks(L)):
                                bank = PS[ci % 4]
                                P.mm(bank[0:64, 0:kw], wt[:, :], src[:, k0:k0 + kw])
                                P.ts(dst[:, k0:k0 + kw], bank[0:64, 0:kw], fb[:, 2:3], fb[:, bi:bi + 1], op0=ALU.mult, op1=ALU.add)
                                P.copy(hki[:, k0:k0 + kw], dst[:, k0:k0 + kw])
                                P.tt(dst[:, k0:k0 + kw], dst[:, k0:k0 + kw], hki[:, k0:k0 + kw], ALU.subtract)
                                P.act(dst[:, k0:k0 + kw], dst[:, k0:k0 + kw], AF.Sin, scale=TWO_PI_S)
                        for st in range(nt):
                            d_ = dec[st % 2]
                            P.dma("sp", d_[:], View(IN["dec%d" % L].ap[st * 128:(st + 1) * 128, :], None))
                            for half in range(2):
                                bank = PS[4 + (st * 2 + half) % 4]
                                P.mm(bank[:, 0:512], h2[:, st * 128:(st + 1) * 128], f3w[:, half * 512:(half + 1) * 512])
                                P.tt(hh[:, half * 512:(half + 1) * 512], bank[:, 0:512], d_[:, half * 512:(half + 1) * 512], ALU.mult)
                            P.tt(AB[:, st, 0, :], hh[:, 0:512], hh[:, 512:1024], ALU.add)
                            P.tt(AB[:, st, 1, :], hh[:, 0:512], hh[:, 512:1024], ALU.subtract)
                    with P.scope() as es3:
                        rin = [P.sb(es3, [128, L + 2], F32, "hrin") for _ in range(2)]
                        z1 = P.sb(es3, [128, L], F32, "hz1"); z2 = P.sb(es3, [128, L], F32, "hz2")

                        def sconv(dst, tl, buf_i):
                            r = rin[buf_i]
                            P.memset(r[:, 0:1], 0.0); P.memset(r[:, L + 1:L + 2], 0.0)
                            ldp(lambda a, b, r=r: r[:, 1 + a:1 + b], OFF_HY + tl * 128, c0, L)
                            P.ts(dst, r[:, 1:L + 1], hw[:, 1, tl:tl + 1], hw[:, 3, tl:tl + 1], op0=ALU.mult, op1=ALU.add)
                            P.stt(dst, r[:, 0:L], hw[:, 0, tl:tl + 1], dst, ALU.mult, ALU.add)
                            P.stt(dst, r[:, 2:L + 2], hw[:, 2, tl:tl + 1], dst, ALU.mult, ALU.add)
                        for ct in range(4):
                            sconv(z1[:], 4 + ct, 0)
                            sconv(z2[:], 8 + ct, 1)
                            P.tt(vxT[:, ct, :], z1[:], z2[:], ALU.mult)
                            for st in range(nt):
                                bank = PS[st % 8]
                                P.transpose(bank[:, 0:128], vxT[:, ct, st * 128:(st + 1) * 128], ident)
                                P.copy(vx[:, st, ct * 128:(ct + 1) * 128], bank[:, 0:128], eng="act" if st % 2 else "dve")
                    with P.scope() as es3:
                        ft_ = [P.sb(es3, [128, 2, nt, 128], BF16, "fwd") for _ in range(2)]
                        xk = [P.sb(es3, [128, 4, 512], F32, "xk") for _ in range(2)]
                        tq = P.sb(es3, [128, 4, 512], F32, "tq")
                        for ft in range(nt):
                            tb = ft_[ft % 2]
                            P.dma("sp", tb[:], View(IN["fwd%d" % L].ap[ft].rearrange("p (a s m) -> p a s m", a=2, s=nt), None))
                            banks = [PS[(ft % 2) * 4 + i] for i in range(4)]
                            for i, (cs_, src, comp) in enumerate(((0, vx, None), (1, vx, None), (0, AB, 0), (1, AB, 1))):
                                for st in range(nt):
                                    rhs = src[:, st, :] if comp is None else src[:, st, comp, :]
                                    P.mm(banks[i][:, 0:512], tb[:, cs_, st, :], rhs, start=(st == 0), stop=(st == nt - 1))
                            x_ = xk[ft % 2]
                            for i in range(4):
                                P.copy(x_[:, i, :], banks[i][:, 0:512], eng="act")
                            P.tt(tq[:, 0, :], x_[:, 0, :], x_[:, 2, :], ALU.mult)
                            P.tt(tq[:, 1, :], x_[:, 1, :], x_[:, 3, :], ALU.mult)
                            P.tt(tq[:, 2, :], x_[:, 0, :], x_[:, 3, :], ALU.mult)
                            P.tt(tq[:, 3, :], x_[:, 1, :], x_[:, 2, :], ALU.mult)
                            P.tt(Y[:, ft, 0, :], tq[:, 0, :], tq[:, 1, :], ALU.subtract)
                            P.tt(Y[:, ft, 1, :], tq[:, 2, :], tq[:, 3, :], ALU.add)
                with P.scope() as es2:
                    itb = [P.sb(es2, [128, 2, nt, TC], BF16, "inv") for _ in range(2)]
                    rin = P.sb(es2, [128, L + 2], F32, "hrin0")
                    x0 = P.sb(es2, [128, L], F32, "hx0")
                    yv = P.sb(es2, [128, TC], F32, "hyv")
                    yo = [P.sb(es2, [128, TC], BF16, "hyo") for _ in range(2)]
                    for ct in range(4):
                        P.memset(rin[:, 0:1], 0.0); P.memset(rin[:, L + 1:L + 2], 0.0)
                        ldp(lambda a, b: rin[:, 1 + a:1 + b], OFF_HY + ct * 128, c0, L)
                        P.ts(x0[:], rin[:, 1:L + 1], hw[:, 1, ct:ct + 1], hw[:, 3, ct:ct + 1], op0=ALU.mult, op1=ALU.add)
                        P.stt(x0[:], rin[:, 0:L], hw[:, 0, ct:ct + 1], x0[:], ALU.mult, ALU.add)
                        P.stt(x0[:], rin[:, 2:L + 2], hw[:, 2, ct:ct + 1], x0[:], ALU.mult, ALU.add)
                        for tc in range(ntc):
                            tb = itb[tc % 2]
                            P.dma("sp", tb[:], View(IN["inv%d" % L].ap[tc].rearrange("p (a f j) -> p a f j", a=2, f=nt), None))
                            bank = PS[(ct * ntc + tc) % 8]
                            n_acc = 2 * nt
                            i = 0
                            for comp in range(2):
                                for ft in range(nt):
                                    P.mm(bank[:, 0:TC], Y[:, ft, comp, ct * 128:(ct + 1) * 128], tb[:, comp, ft, :],
                                         start=(i == 0), stop=(i == n_acc - 1))
                                    i += 1
                            P.stt(yv[:], vxT[:, ct, tc * TC:(tc + 1) * TC], hbias[:, ct:ct + 1], bank[:, 0:TC], ALU.mult, ALU.add)
                            o = yo[tc % 2]
                            P.tt(o[:], yv[:], x0[:, tc * TC:(tc + 1) * TC], ALU.mult)
                            sty(1024 + ct * 128, c0 + tc * TC, TC, lambda a, b, o=o: o[:, a:b])
            P.barrier()
    P.barrier()

    with P.scope() as es:
        Bp = [P.sb(es, [128, NG, 2, 128], BF16, "Bp") for _ in range(2)]
        Cp = [P.sb(es, [128, NG, 2, 128], BF16, "Cp") for _ in range(2)]
        rcol = [P.sb(es, [128, NG], F32, "rcol") for _ in range(2)]
        thcol = [P.sb(es, [128, NG], F32, "thcol") for _ in range(2)]
        H0 = [P.sb(es, [128, NG], F32, "H0") for _ in range(2)]
        Dc = P.sb(es, [128, 2, NCT], F32, "Dc"); Dsum = P.sb(es, [128, NCT], F32, "Dsum")
        for d in range(2):
            colload(Dc[:, d, :], IN["s5_d"].ap[l][d])
        P.tt(Dsum[:], Dc[:, 0, :], Dc[:, 1, :], ALU.add)
        glub = P.sb(es, [128, 8], F32, "glub"); colload(glub[:], IN["s5_glu_b"].ap[l])
        gluw = P.sb(es, [128, 4, 1024], BF16, "gluw")
        P.dma("pool", gluw[:], View(IN["s5_glu_w"].ap[l].rearrange("(k p) n -> p k n", p=128), None))
        iota = P.sb(es, [128, SEQ], F32, "iota")
        P.dma("sp", iota[:], View(IN["iota1"].ap.broadcast_to([128, SEQ]), None))
        hpi = P.sb(es, [128, 1], F32, "hpi"); P.memset(hpi[:], 0.5 * PI * 0.999999)
        scrb = Buf()
        with P.scope() as es2:
            for d in range(2):
                are = P.sb(es2, [NG, 64], F32, "are"); aim = P.sb(es2, [NG, 64], F32, "aim")
                ldt = P.sb(es2, [NG, 1], F32, "ldt")
                P.dma("sp", are[:], View(IN["s5_a_re"].ap[l][d], None)); P.dma("sp", aim[:], View(IN["s5_a_im"].ap[l][d], None))
                P.dma("sp", ldt[:], View(IN["s5_log_dt"].ap[l][d].rearrange("(p o) -> p o", o=1), None), slow=True)
                dtc = P.sb(es2, [NG, 1], F32, "dtc"); P.act(dtc[:], ldt[:], AF.Exp)
                lre = P.sb(es2, [NG, 64], F32, "lre"); th = P.sb(es2, [NG, 64], F32, "th")
                P.ts(lre[:], are[:], dtc[:, 0:1], None, op0=ALU.mult)
                P.ts(th[:], aim[:], dtc[:, 0:1], None, op0=ALU.mult)
                r_ = P.sb(es2, [NG, 64], F32, "r_"); P.act(r_[:], lre[:], AF.Exp)
                sn = P.sb(es2, [NG, 64], F32, "sn"); cn = P.sb(es2, [NG, 64], F32, "cn")
                ki = P.sb(es2, [NG, 64], I32, "ki")
                P.ts(th[:], th[:], 1.0 / (2.0 * PI), None, op0=ALU.mult)
                P.copy(ki[:], th[:]); P.tt(sn[:], th[:], ki[:], ALU.subtract)
                P.act(sn[:], sn[:], AF.Sin, scale=TWO_PI_S)
                P.ts(cn[:], th[:], 0.25, None, op0=ALU.add)
                P.copy(ki[:], cn[:]); P.tt(cn[:], cn[:], ki[:], ALU.subtract)
                P.act(cn[:], cn[:], AF.Sin, scale=TWO_PI_S)
                nr = P.sb(es2, [NG, 64], F32, "nr"); ni = P.sb(es2, [NG, 64], F32, "ni")
                P.tt(nr[:], r_[:], cn[:], ALU.mult); P.ts(nr[:], nr[:], -1.0, None, op0=ALU.add)
                P.tt(ni[:], r_[:], sn[:], ALU.mult)
                den = P.sb(es2, [NG, 64], F32, "den"); t_ = P.sb(es2, [NG, 64], F32, "t_")
                P.tt(den[:], are[:], are[:], ALU.mult); P.tt(t_[:], aim[:], aim[:], ALU.mult)
                P.tt(den[:], den[:], t_[:], ALU.add); P.recip(den[:], den[:])
                cr = P.sb(es2, [NG, 64], F32, "cr"); ci_ = P.sb(es2, [NG, 64], F32, "ci")
                P.tt(cr[:], nr[:], are[:], ALU.mult); P.tt(t_[:], ni[:], aim[:], ALU.mult)
                P.tt(cr[:], cr[:], t_[:], ALU.add); P.tt(cr[:], cr[:], den[:], ALU.mult)
                P.tt(ci_[:], ni[:], are[:], ALU.mult); P.tt(t_[:], nr[:], aim[:], ALU.mult)
                P.tt(ci_[:], ci_[:], t_[:], ALU.subtract); P.tt(ci_[:], ci_[:], den[:], ALU.mult)
                for i, src in enumerate((cr, ci_, r_, th)):
                    P.dma("sp", s5scr[d, i, :, :], src[:], extra_w=[scrb])
            P.barrier()
            for d in range(2):
              with P.scope() as es2:
                for half in range(2):
                    P.dma("sp", rcol[d][half * 64:(half + 1) * 64, :], View(s5scr.ap[d, 2].rearrange("g p -> p g"), None), slow=True)
                    P.dma("sp", thcol[d][half * 64:(half + 1) * 64, :], View(s5scr.ap[d, 3].rearrange("g p -> p g"), None), slow=True)
                crb = P.sb(es2, [16, NG * 64], F32, "crb"); cib = P.sb(es2, [16, NG * 64], F32, "cib")
                P.dma("sp", crb[:], View(s5scr.ap[d, 0].rearrange("g p -> (g p)").rearrange("(o n) -> o n", o=1).broadcast_to([16, NG * 64]), None))
                P.dma("sp", cib[:], View(s5scr.ap[d, 1].rearrange("g p -> (g p)").rearrange("(o n) -> o n", o=1).broadcast_to([16, NG * 64]), None))
                bre = P.sb(es2, [16, NG, 64], F32, "bre"); bim = P.sb(es2, [16, NG, 64], F32, "bim")
                P.dma("sp", bre[:], View(IN["s5_bT_re"].ap[l][d].rearrange("g c p -> c g p"), None))
                P.dma("sp", bim[:], View(IN["s5_bT_im"].ap[l][d].rearrange("g c p -> c g p"), None))
                crv = View(crb.t[:, :].rearrange("c (g p) -> c g p", g=NG), crb.buf)
                civ = View(cib.t[:, :].rearrange("c (g p) -> c g p", g=NG), cib.buf)
                t1 = P.sb(es2, [16, NG, 64], F32, "t1"); t2 = P.sb(es2, [16, NG, 64], F32, "t2")
                bbr = P.sb(es2, [16, NG, 64], F32, "bbr"); bbi = P.sb(es2, [16, NG, 64], F32, "bbi")
                P.tt(t1[:], bre[:], crv, ALU.mult); P.tt(t2[:], bim[:], civ, ALU.mult); P.tt(bbr[:], t1[:], t2[:], ALU.subtract)
                P.tt(t1[:], bim[:], crv, ALU.mult); P.tt(t2[:], bre[:], civ, ALU.mult); P.tt(bbi[:], t1[:], t2[:], ALU.add)
                Bs = P.sb(es2, [16, NG, 2, 128], BF16, "Bs")
                P.copy(Bs[:, :, 0, 0:64], bbr[:]); P.copy(Bs[:, :, 0, 64:128], bbi[:])
                P.copy(Bs[:, :, 1, 0:64], bbi[:]); P.ts(Bs[:, :, 1, 64:128], bbr[:], -1.0, None, op0=ALU.mult)
                P.memset(Bp[d][:], 0.0)
                for j in range(8):
                    P.dma("sp", Bp[d][16 * j:16 * (j + 1), j::8, :, :], Bs[0:16, j::8, :, :])
                A1 = P.sb(es2, [128, NG, 16], F32, "cA1"); A2 = P.sb(es2, [128, NG, 16], F32, "cA2")
                cre_v = View(IN["s5_cT_re"].ap[l][d].rearrange("g p c -> p g c"), None)
                cim_v = View(IN["s5_cT_im"].ap[l][d].rearrange("g p c -> p g c"), None)
                P.dma("sp", A1[0:64, :, :], cre_v); P.dma("sp", A1[64:128, :, :], cim_v)
                P.dma("sp", A2[0:64, :, :], cim_v); P.dma("sp", A2[64:128, :, :], cre_v)
                P.ts(A1[64:128, :, :], A1[64:128, :, :], -1.0, None, op0=ALU.mult)
                P.ts(A2[:], A2[:], -1.0, None, op0=ALU.mult)
                P.memset(Cp[d][:], 0.0)
                for j in range(8):
                    P.copy(Cp[d][:, j::8, 0, 16 * j:16 * (j + 1)], A1[:, j::8, :])
                    P.copy(Cp[d][:, j::8, 1, 16 * j:16 * (j + 1)], A2[:, j::8, :])
        P.barrier()
        for (c0, L) in SEQS:
            is_ctx = (c0 == 0)
            need_out = not (last and is_ctx)
            cks = chunks(L, 256)
            with P.scope() as es2:
                uT = P.sb(es2, [128, L], F32, "s5u"); uT2 = P.sb(es2, [128, L], F32, "s5u2")
                U16 = P.sb(es2, [128, NCT, L], BF16, "s5u16")
                G16 = P.sb(es2, [128, NCT, L], BF16, "s5g16")
                for ct in range(NCT):
                    ldsel(uT, uT2, OFF_S5 + ct * 128, OFF_S5 + 256 + ct * 128, c0, L)
                    P.copy(U16[:, ct, :], uT[:], eng="act")
                nck = len(cks)
                CW = cks[0][1]
                Ts = [P.sb(es2, [128, CW], F32, "Ts") for _ in range(nck)]
                Tc = [P.sb(es2, [128, CW], F32, "Tc") for _ in range(nck)]
                M1 = [P.sb(es2, [128, CW], F32, "M1") for _ in range(nck)]
                Z = [P.sb(es2, [128, CW], F32, "Z") for _ in range(nck)]
                Zc = [P.sb(es2, [128, CW], BF16, "Zc") for _ in range(nck)]
                Zs = [P.sb(es2, [128, CW], BF16, "Zs") for _ in range(nck)]
                KI = [P.sb(es2, [128, CW], I32, "KI") for _ in range(nck)]
                hl = P.sb(es2, [128, 2], F32, "hl")
                for ct in range(NCT):
                    def yv(tci, w):
                        return PS[tci // 2][:, (tci % 2) * 256:(tci % 2) * 256 + w]
                    for d in range(2):
                        for gg in range(8):
                            g = ct * 8 + gg
                            th = thcol[d][:, g:g + 1]; rr = rcol[d][:, g:g + 1]
                            first = (d == 0 and gg == 0); lastm = (d == 1 and gg == 7)
                            for jc, (j0, jw) in enumerate(cks):
                                tci = jc if d == 0 else nck - 1 - jc
                                k0 = cks[tci][0]
                                ts_, tc_, m1, z_, ki = Ts[jc], Tc[jc], M1[jc], Z[jc], KI[jc]
                                P.act(ts_[:, 0:jw], iota[:, j0:j0 + jw], AF.Copy, scale=th)
                                P.copy(ki[:, 0:jw], ts_[:, 0:jw])
                                P.tt(tc_[:, 0:jw], ts_[:, 0:jw], ki[:, 0:jw], ALU.subtract)
                                P.act(ts_[:, 0:jw], tc_[:, 0:jw], AF.Sin, scale=TWO_PI_S)
                                P.act(tc_[:, 0:jw], tc_[:, 0:jw], AF.Sin, scale=PI)
                                P.act(tc_[:, 0:jw], tc_[:, 0:jw], AF.Square)
                                P.act(tc_[:, 0:jw], tc_[:, 0:jw], AF.Identity, bias=1.0, scale=-2.0)
                                pc0 = ((jc // 2) % 2) * 256
                                p1 = PS[4 + (jc % 2)][:, pc0:pc0 + jw]; p2 = PS[6 + (jc % 2)][:, pc0:pc0 + jw]
                                P.mm(p1, Bp[d][:, g, 0, :], U16[:, ct, k0:k0 + jw])
                                P.mm(p2, Bp[d][:, g, 1, :], U16[:, ct, k0:k0 + jw])
                                if d == 0:
                                    P.tt(m1[:, 0:jw], p1, tc_[:, 0:jw], ALU.mult)
                                    P.tt(z_[:, 0:jw], p2, ts_[:, 0:jw], ALU.mult)
                                else:
                                    P.tt(rev(m1[:, 0:jw]), p1, rev(tc_[:, 0:jw]), ALU.mult)
                                    P.tt(rev(z_[:, 0:jw]), p2, rev(ts_[:, 0:jw]), ALU.mult)
                                P.tt(m1[:, 0:jw], m1[:, 0:jw], z_[:, 0:jw], ALU.add)
                                if jc == 0:
                                    init = 0.0 if is_ctx else H0[d][:, g:g + 1]
                                else:
                                    pw_ = cks[jc - 1][1]
                                    init = Z[jc - 1][:, pw_ - 1:pw_]
                                P.scan(z_[:, 0:jw], bcast(rr, [128, jw]), m1[:, 0:jw], init)
                                if is_ctx and jc == nck - 1:
                                    pj_ = PS[7]
                                    P.mm(pj_[:, 0:1], JT, z_[:, jw - 1:jw])
                                    P.tt(hl[:, 0:1], z_[:, jw - 1:jw], tc_[:, jw - 1:jw], ALU.mult)
                                    P.tt(hl[:, 1:2], pj_[:, 0:1], ts_[:, jw - 1:jw], ALU.mult)
                                    P.tt(H0[d][:, g:g + 1], hl[:, 0:1], hl[:, 1:2], ALU.add)
                                if not need_out:
                                    continue
                                zc = Zc[jc]; zs = Zs[jc]
                                if d == 0:
                                    P.tt(zc[:, 0:jw], z_[:, 0:jw], tc_[:, 0:jw], ALU.mult)
                                    P.tt(zs[:, 0:jw], z_[:, 0:jw], ts_[:, 0:jw], ALU.mult)
                                else:
                                    P.tt(rev(zc[:, 0:jw]), z_[:, 0:jw], tc_[:, 0:jw], ALU.mult)
                                    P.tt(rev(zs[:, 0:jw]), z_[:, 0:jw], ts_[:, 0:jw], ALU.mult)
                                P.mm(yv(tci, jw), Cp[d][:, g, 0, :], zc[:, 0:jw], start=(first and tci % 2 == 0), stop=False)
                                P.mm(yv(tci, jw), Cp[d][:, g, 1, :], zs[:, 0:jw], start=False, stop=lastm)
                    if not need_out:
                        continue
                    ldsel(uT, uT2, OFF_S5 + ct * 128, OFF_S5 + 256 + ct * 128, c0, L)
                    for ci, (k0, kw) in enumerate(cks):
                        yt = Ts[ci]; gt_ = Tc[ci]
                        P.stt(yt[:, 0:kw], uT[:, k0:k0 + kw], Dsum[:, ct:ct + 1], yv(ci, kw), ALU.mult, ALU.add)
                        P.tt(gt_[:, 0:kw], yt[:, 0:kw], yt[:, 0:kw], ALU.mult)
                        P.ts(gt_[:, 0:kw], gt_[:, 0:kw], 0.044715, 1.0, op0=ALU.mult, op1=ALU.add)
                        P.tt(gt_[:, 0:kw], gt_[:, 0:kw], yt[:, 0:kw], ALU.mult)
                        P.act(gt_[:, 0:kw], gt_[:, 0:kw], AF.Sigmoid, scale=1.5957691216057308)
                        P.tt(G16[:, ct, k0:k0 + kw], gt_[:, 0:kw], yt[:, 0:kw], ALU.mult)
                    P.dma("sp", g16L[ct * 128:(ct + 1) * 128, c0:c0 + L], G16[:, ct, :])
            P.barrier()
        P.allgather(g16L.ap[:, :], g16G.ap[:, :])
        for (c0, L) in SEQS:
            cks = chunks(L)
            with P.scope() as es2:
                G16 = P.sb(es2, [128, 4, L], BF16, "s5g16f")
                for k4 in range(4):
                    P.dma("sp", G16[:, k4, :], g16G[k4 * 128:(k4 + 1) * 128, c0:c0 + L])
                Tc = [P.sb(es2, [128, 512], F32, "gTc") for _ in range(len(cks))]
                Zc = [P.sb(es2, [128, 512], BF16, "gZc") for _ in range(len(cks))]
                if True:
                    for m in range(4):
                        for ci, (k0, kw) in enumerate(cks):
                            pa = PS[(ci % 2) * 2]; pb = PS[(ci % 2) * 2 + 1]
                            for k4 in range(4):
                                P.mm(pa[:, 0:kw], gluw[:, k4, m * 128:(m + 1) * 128], G16[:, k4, k0:k0 + kw], start=(k4 == 0), stop=(k4 == 3))
                            for k4 in range(4):
                                P.mm(pb[:, 0:kw], gluw[:, k4, 512 + m * 128:512 + (m + 1) * 128], G16[:, k4, k0:k0 + kw], start=(k4 == 0), stop=(k4 == 3))
                            gt_ = Tc[ci]; zo = Zc[ci]
                            P.act(gt_[:, 0:kw], pb[:, 0:kw], AF.Sigmoid, bias=glub[:, 4 + m:5 + m])
                            P.stt(zo[:, 0:kw], pa[:, 0:kw], glub[:, m:m + 1], gt_[:, 0:kw], ALU.add, ALU.mult)
                            sty(1536 + m * 128, c0 + k0, kw, lambda a, b, zo=zo: zo[:, a:b])
            P.barrier()

    P.barrier()


_WKEYS = ["w_ada", "b_ada", "norm_ffn1", "norm_mix", "norm_ffn2", "norm_final", "ffn1_wi", "ffn1_wo", "ffn2_wi", "ffn2_wo",
          "w_in", "w_gate", "b_gate", "w_branch", "w_out", "pool_w", "pool_scale", "q_norm", "k_norm",
          "hy_short_w", "hy_short_b", "hy_f1_w", "hy_f1_b", "hy_f2_w", "hy_f2_b", "hy_f3_w", "hy_freq", "hy_bias",
          "s5_a_re", "s5_a_im", "s5_log_dt", "s5_d", "s5_glu_w", "s5_glu_b"]


def make_in_maps(inputs, depth, batches):
    f = lambda a: np.ascontiguousarray(np.asarray(a, dtype=np.float32))
    shared = {}
    for k in _WKEYS:
        a = f(inputs[k])
        shared[k] = a if k == "norm_final" else np.ascontiguousarray(a[:depth])
    shared["s5_bT_re"] = np.ascontiguousarray(f(inputs["s5_b_re"])[:depth].transpose(0, 1, 2, 4, 3))
    shared["s5_bT_im"] = np.ascontiguousarray(f(inputs["s5_b_im"])[:depth].transpose(0, 1, 2, 4, 3))
    shared["s5_cT_re"] = np.ascontiguousarray(f(inputs["s5_c_re"])[:depth].transpose(0, 1, 2, 4, 3))
    shared["s5_cT_im"] = np.ascontiguousarray(f(inputs["s5_c_im"])[:depth].transpose(0, 1, 2, 4, 3))
    shared.update(_consts())
    x = f(inputs["x"]); c = f(inputs["c"]); ctx = f(inputs["ctx"]); cc = f(inputs["c_ctx"])
    maps = []
    NA = NL - CTX
    for b in batches:
        for r in range(2):
            m = dict(shared)
            if r == 0:
                m["xl"] = np.ascontiguousarray(np.concatenate([ctx[b], x[b, :NA]], axis=0))
                m["cvec"] = np.ascontiguousarray(np.stack([c[b], cc]))
                m["sel"] = np.array([[1.0, 0.0]], np.float32)
            else:
                m["xl"] = np.ascontiguousarray(x[b, NA:])
                m["cvec"] = np.ascontiguousarray(np.stack([c[b], c[b]]))
                m["sel"] = np.array([[0.0, 1.0]], np.float32)
            for k in ("s5_a_re", "s5_a_im", "s5_log_dt", "s5_bT_re", "s5_bT_im", "s5_cT_re", "s5_cT_im"):
                m[k] = np.ascontiguousarray(shared[k][:, :, 16 * r:16 * (r + 1)])
            m["s5_d"] = np.ascontiguousarray(shared["s5_d"][:, :, 256 * r:256 * (r + 1)])
            maps.append(m)
    return maps


def assemble(results):
    outs = []
    for b in range(len(results) // 2):
        a = np.asarray(results[2 * b]["out"], dtype=np.float32)
        bb = np.asarray(results[2 * b + 1]["out"], dtype=np.float32)
        outs.append(np.concatenate([a[CTX:], bb], axis=0))
    return np.stack(outs, axis=0)


def kernel(**inputs):
    depth = 4
    nc = build(depth)
    maps = make_in_maps(inputs, depth, list(range(4)))
    res = run_bass_kernel_spmd(nc, maps, core_ids=list(range(8)))
    return assemble(res.results)
```

```python
import math
from contextlib import ExitStack, contextmanager
import numpy as np
import ml_dtypes
import concourse.bass as bass
import concourse.mybir as mybir
from concourse.bass_utils import run_bass_kernel_spmd

F32 = mybir.dt.float32
BF16 = mybir.dt.bfloat16
AF = mybir.ActivationFunctionType
ALU = mybir.AluOpType

DM = 2048; SEQ = 2048; CTX = 256; NTOK = SEQ + CTX; DFF = 5632; NMOD = 9; EPS = 1e-6
INW = 3584; OFF_Q = 512; OFF_K = 1024; OFF_V = 1280; OFF_HY = 1536; OFF_S5 = 3072
NPASS = 2; PW = 576; NL = 1152
PI = math.pi
TWO_PI_S = 2.0 * math.pi * 0.999999
I32 = mybir.dt.int32


class Buf:
    __slots__ = ("w", "r")

    def __init__(self):
        self.w = None
        self.r = {}


class View:
    __slots__ = ("ap", "buf")

    def __init__(self, ap, buf):
        self.ap = ap
        self.buf = buf


class Tile:
    def __init__(self, t, buf=None):
        self.t = t
        self.buf = buf if buf is not None else Buf()

    def __getitem__(self, key):
        return View(self.t[key], self.buf)


class DT:
    def __init__(self, ap):
        self.ap = ap

    def __getitem__(self, key):
        return View(self.ap[key], None)


def rev(v):
    a = v.ap
    dims = list(a.ap)
    s, c = dims[-1]
    nd = [list(d) for d in dims[:-1]] + [[-s, c]]
    return View(bass.AP(a.tensor, a.offset + s * (c - 1), nd), v.buf)


def bcast(v, shape):
    return View(v.ap.to_broadcast(list(shape)), v.buf)


class Stream:
    def __init__(self, name, eng, sem):
        self.name = name
        self.eng = eng
        self.sem = sem
        self.count = 0
        self.waited = {}


class Prog:
    NRING = 20

    def __init__(self, nc, es):
        self.nc = nc
        self.es = es
        self.S = {}
        for name, eng in (("pe", nc.tensor), ("act", nc.scalar), ("dve", nc.vector),
                          ("pool", nc.gpsimd), ("sp", nc.sync)):
            sem = es.enter_context(nc.semaphore("sem_" + name))
            self.S[name] = Stream(name, eng, sem)
        self.rings = {}
        for q in ("sp", "pool"):
            sems = [es.enter_context(nc.semaphore("ring_%s_%d" % (q, i))) for i in range(self.NRING)]
            self.rings[q] = {"sems": sems, "n": 0, "cnt": [0] * self.NRING}
        self.ntile = 0
        self.cc_sem = None
        self.cc_count = 0

    @contextmanager
    def scope(self):
        with ExitStack() as es:
            yield es
            self.barrier()

    def sb(self, es, shape, dtype, name=None):
        self.ntile += 1
        t = es.enter_context(self.nc.sbuf_tensor("%s_%d" % (name or "t", self.ntile), list(shape), dtype))
        return Tile(t)

    def ps(self, es, shape, dtype=F32, name=None):
        self.ntile += 1
        t = es.enter_context(self.nc.psum_tensor("%s_%d" % (name or "p", self.ntile), list(shape), dtype))
        return Tile(t)

    def _wait(self, st, ev):
        sem, val = ev
        k = id(sem)
        if st.waited.get(k, 0) >= val:
            return
        st.eng.wait_ge(sem, val)
        st.waited[k] = val

    def _deps(self, st, reads, writes):
        evs = []
        for b in reads:
            if b is not None and b.w is not None:
                evs.append(b.w)
        for b in writes:
            if b is None:
                continue
            if b.w is not None:
                evs.append(b.w)
            evs.extend(b.r.values())
        for ev in evs:
            if st.name == "pe" and ev[0] is st.sem:
                continue
            self._wait(st, ev)

    def _mark(self, key, ev, reads, writes):
        for b in reads:
            if b is not None:
                b.r[key] = ev
        for b in writes:
            if b is not None:
                b.w = ev
                b.r = {}

    def op(self, sname, fn, ins, outs):
        st = self.S[sname]
        reads = [v.buf for v in ins if isinstance(v, View)]
        writes = [v.buf for v in outs]
        self._deps(st, reads, writes)
        inst = fn()
        st.count += 1
        inst.then_inc(st.sem, 1)
        self._mark(sname, (st.sem, st.count), reads, writes)

    def dma(self, q, out, in_, extra_r=(), extra_w=(), slow=False):
        st = self.S[q]
        ring = self.rings[q]
        reads = [in_.buf] + list(extra_r)
        writes = [out.buf] + list(extra_w)
        self._deps(st, reads, writes)
        j = ring["n"] % self.NRING
        ring["n"] += 1
        sem = ring["sems"][j]
        if ring["cnt"][j] > 0:
            self._wait(st, (sem, 16 * ring["cnt"][j]))
        if slow:
            with self.nc.allow_non_contiguous_dma(reason="small strided vector load"):
                inst = st.eng.dma_start(out=out.ap, in_=in_.ap)
        else:
            inst = st.eng.dma_start(out=out.ap, in_=in_.ap)
        inst.then_inc(sem, 16)
        ring["cnt"][j] += 1
        ev = (sem, 16 * ring["cnt"][j])
        self._mark(("dma", q, j), ev, reads, writes)
        return ev

    def barrier(self):
        evs = []
        for st in self.S.values():
            if st.count:
                evs.append((st.sem, st.count))
        for ring in self.rings.values():
            for j, c in enumerate(ring["cnt"]):
                if c:
                    evs.append((ring["sems"][j], 16 * c))
        for st in self.S.values():
            for ev in evs:
                if ev[0] is st.sem:
                    continue
                self._wait(st, ev)

    def allgather(self, in_ap, out_ap):
        self.barrier()
        if self.cc_sem is None:
            self.cc_sem = self.es.enter_context(self.nc.semaphore("cc_sem"))
        CR = 256
        nrow = in_ap.shape[0]
        for k in range(nrow // CR):
            inst = self.nc.gpsimd.collective_compute(
                "AllGather", ALU.bypass, replica_groups=[[2 * i, 2 * i + 1] for i in range(self.ncores // 2)],
                ins=[in_ap[k * CR:(k + 1) * CR, :]], outs=[out_ap[2 * k * CR:2 * (k + 1) * CR, :]])
            self.cc_count += 1
            inst.then_inc(self.cc_sem, 1)
        for st in self.S.values():
            self._wait(st, (self.cc_sem, self.cc_count))

    def mm(self, out, lhsT, rhs, start=True, stop=True):
        self.op("pe", lambda: self.nc.tensor.matmul(out.ap, lhsT.ap, rhs.ap, start=start, stop=stop),
                [lhsT, rhs], [out])

    def transpose(self, out, in_, ident):
        self.op("pe", lambda: self.nc.tensor.transpose(out.ap, in_.ap, ident.ap), [in_, ident], [out])

    def act(self, out, in_, func, bias=0.0, scale=1.0):
        ins = [in_]
        b = bias
        s = scale
        if isinstance(bias, View):
            ins.append(bias); b = bias.ap
        if isinstance(scale, View):
            ins.append(scale); s = scale.ap
        self.op("act", lambda: self.nc.scalar.activation(out=out.ap, in_=in_.ap, func=func, bias=b, scale=s),
                ins, [out])

    def tt(self, out, in0, in1, op, eng="dve"):
        e = self.S[eng].eng
        self.op(eng, lambda: e.tensor_tensor(out=out.ap, in0=in0.ap, in1=in1.ap, op=op), [in0, in1], [out])

    def ts(self, out, in0, s1, s2=None, op0=ALU.mult, op1=None, eng="dve"):
        e = self.S[eng].eng
        ins = [in0]
        a1 = s1
        a2 = s2
        if isinstance(s1, View):
            ins.append(s1); a1 = s1.ap
        if isinstance(s2, View):
            ins.append(s2); a2 = s2.ap
        if op1 is None:
            self.op(eng, lambda: e.tensor_scalar(out=out.ap, in0=in0.ap, scalar1=a1, scalar2=None, op0=op0),
                    ins, [out])
        else:
            self.op(eng, lambda: e.tensor_scalar(out=out.ap, in0=in0.ap, scalar1=a1, scalar2=a2, op0=op0, op1=op1),
                    ins, [out])

    def stt(self, out, in0, scalar, in1, op0, op1, eng="dve"):
        e = self.S[eng].eng
        ins = [in0, in1]
        sc = scalar
        if isinstance(scalar, View):
            ins.append(scalar); sc = scalar.ap
        self.op(eng, lambda: e.scalar_tensor_tensor(out=out.ap, in0=in0.ap, scalar=sc, in1=in1.ap, op0=op0, op1=op1),
                ins, [out])

    def copy(self, out, in_, eng="dve"):
        if eng == "act":
            self.act(out, in_, AF.Copy)
        else:
            e = self.S[eng].eng
            self.op(eng, lambda: e.tensor_copy(out=out.ap, in_=in_.ap), [in_], [out])

    def memset(self, out, val, eng="dve"):
        e = self.S[eng].eng
        self.op(eng, lambda: e.memset(out.ap, val), [], [out])

    def scan(self, out, d0, d1, init):
        ins = [d0, d1]
        iv = init
        if isinstance(init, View):
            ins.append(init); iv = init.ap
        self.op("dve", lambda: self.nc.vector.tensor_tensor_scan(out=out.ap, data0=d0.ap, data1=d1.ap, initial=iv,
                                                                 op0=ALU.mult, op1=ALU.add), ins, [out])

    def recip(self, out, in_):
        self.op("dve", lambda: self.nc.vector.reciprocal(out=out.ap, in_=in_.ap), [in_], [out])


def chunks(n, w=512):
    out = []
    c = 0
    while c < n:
        out.append((c, min(w, n - c)))
        c += w
    return out


def _tables(L):
    N = 2 * L
    f = np.arange(L, dtype=np.float64) + 0.5
    s = np.arange(L, dtype=np.float64)
    ang = 2.0 * np.pi * np.outer(s, f) / N
    fc = np.cos(ang); fs = -np.sin(ang)
    nt = L // 128
    def tile_fwd(t):
        return t.reshape(nt, 128, nt, 128).transpose(2, 1, 0, 3)
    fwd = np.stack([tile_fwd(fc), tile_fwd(fs)], axis=2).reshape(nt, 128, 2 * nt * 128)
    ic = (2.0 / N) * np.cos(ang.T); isn = -(2.0 / N) * np.sin(ang.T)
    TC = min(512, L); ntc = L // TC
    def tile_inv(t):
        return t.reshape(nt, 128, ntc, TC).transpose(2, 1, 0, 3)
    inv = np.stack([tile_inv(ic), tile_inv(isn)], axis=2).reshape(ntc, 128, 2 * nt * TC)
    t = np.linspace(0.0, 1.0, L, dtype=np.float32)[:, None]
    fb = np.linspace(1e-4, 15.0, 16, dtype=np.float32)
    w = (2.0 * np.pi * np.arange(L, dtype=np.float32) / L).astype(np.float32)
    fw = w[:, None] * fb[None, :]
    z = np.concatenate([t, np.cos(fw), -np.sin(fw)], axis=-1).astype(np.float32)
    max_decay = math.log(1e-2) / 0.3
    min_decay = math.log(1e-2) / 1.5
    deltas = np.abs(np.linspace(min_decay, max_decay, 512, dtype=np.float32))
    decay = np.exp(-t * deltas[None, :]).astype(np.float32)
    decay_b = decay.copy(); decay_b[0, :] = 0.0
    dec = np.concatenate([decay, decay_b], axis=1)
    tt = np.arange(L)
    rc = []
    for wdw in (2, 4, 8, 16):
        lo = np.clip(tt - wdw // 2, 0, L); hi = np.clip(tt - wdw // 2 + wdw, 0, L)
        rc.append(1.0 / (hi - lo).astype(np.float32))
    rc = np.stack(rc).astype(np.float32)
    return dict(fwd=fwd.astype(ml_dtypes.bfloat16), inv=inv.astype(ml_dtypes.bfloat16),
                zT=np.ascontiguousarray(z.T), dec=dec, rc=rc)


def _consts():
    c = {}
    for L in (2048, 256):
        tb = _tables(L)
        for k, v in tb.items():
            c["%s%d" % (k, L)] = np.ascontiguousarray(v)
    freqs = (10000.0 ** (-np.arange(32, dtype=np.float32) / 32)).astype(np.float32)
    tpos = np.arange(2048)
    rows = (tpos // 64).astype(np.float32); cols = (tpos % 64).astype(np.float32)
    cosT = np.zeros((128, 2048), np.float32); sinT = np.zeros((128, 2048), np.float32)
    for p in range(128):
        pos = rows if p < 64 else cols
        a = pos * freqs[p % 32]
        cosT[p] = np.cos(a); sinT[p] = np.sin(a)
    c["ropecs"] = np.stack([cosT, sinT], axis=1).reshape(128, 4096).copy()
    P = np.zeros((128, 128), np.float32)
    J = np.zeros((128, 128), np.float32)
    for p in range(128):
        if p % 64 < 32:
            P[p, p + 32] = -1.0
        else:
            P[p, p - 32] = 1.0
        if p < 64:
            J[p, p + 64] = -1.0
        else:
            J[p, p - 64] = 1.0
    mats = np.stack([np.eye(128, dtype=np.float32), P.T.copy(), J.T.copy()], axis=1)
    c["mats"] = mats.reshape(128, 384).copy()
    c["iota1"] = (np.arange(2048, dtype=np.float32) + 1.0).reshape(1, 2048)
    return c


def build(depth, dbg=False, mode="full", ncores=8):
    nc = bass.Bass("TRN2", target_bir_lowering=False)
    IN = {}

    BIG = ("w_ada", "ffn1_wi", "ffn1_wo", "ffn2_wi", "ffn2_wo", "w_in", "w_gate", "w_branch", "w_out")

    def din(name, shape, dt=F32):
        if mode == "mix" and name in BIG:
            return None
        IN[name] = DT(nc.dram_tensor(name, list(shape), dt, kind="ExternalInput").ap())
        return IN[name]

    din("xl", [NL, DM]); din("cvec", [2, DM]); din("sel", [1, 2])
    din("w_ada", [depth, DM, NMOD * DM]); din("b_ada", [depth, NMOD * DM])
    din("norm_ffn1", [depth, DM]); din("norm_mix", [depth, DM]); din("norm_ffn2", [depth, DM]); din("norm_final", [DM])
    din("ffn1_wi", [depth, DM, 2 * DFF]); din("ffn1_wo", [depth, DFF, DM])
    din("ffn2_wi", [depth, DM, 2 * DFF]); din("ffn2_wo", [depth, DFF, DM])
    din("w_in", [depth, DM, INW]); din("w_gate", [depth, DM, 4 * DM]); din("b_gate", [depth, 4 * DM])
    din("w_branch", [depth, 4, 512, DM]); din("w_out", [depth, DM, DM])
    din("pool_w", [depth, 4, 128, 128]); din("pool_scale", [depth, 512])
    din("q_norm", [depth, 128]); din("k_norm", [depth, 128])
    din("hy_short_w", [depth, 3, 1536]); din("hy_short_b", [depth, 1536])
    din("hy_f1_w", [depth, 33, 64]); din("hy_f1_b", [depth, 64]); din("hy_f2_w", [depth, 64, 64]); din("hy_f2_b", [depth, 64])
    din("hy_f3_w", [depth, 64, 1024]); din("hy_freq", [depth, 64]); din("hy_bias", [depth, 512])
    NG = 16
    din("s5_a_re", [depth, 2, NG, 64]); din("s5_a_im", [depth, 2, NG, 64]); din("s5_log_dt", [depth, 2, NG])
    din("s5_bT_re", [depth, 2, NG, 16, 64]); din("s5_bT_im", [depth, 2, NG, 16, 64])
    din("s5_cT_re", [depth, 2, NG, 64, 16]); din("s5_cT_im", [depth, 2, NG, 64, 16])
    din("s5_d", [depth, 2, 256]); din("s5_glu_w", [depth, 512, 1024]); din("s5_glu_b", [depth, 1024])
    for L in (2048, 256):
        nt = L // 128; TC = min(512, L); ntc = L // TC
        din("fwd%d" % L, [nt, 128, 2 * nt * 128], BF16); din("inv%d" % L, [ntc, 128, 2 * nt * TC], BF16)
        din("zT%d" % L, [33, L]); din("dec%d" % L, [L, 1024]); din("rc%d" % L, [4, L])
    din("ropecs", [128, 4096]); din("mats", [128, 384]); din("iota1", [1, 2048])
    OUT = DT(nc.dram_tensor("out", [NL, DM], F32, kind="ExternalOutput").ap())
    xT = DT(nc.dram_tensor("xT", [DM, NL], F32).ap())
    projL = [DT(nc.dram_tensor("projL%d" % i, [INW, NL], F32).ap()) for i in range(2)]
    projG = DT(nc.dram_tensor("projG", [2 * INW, NL], F32).ap())
    ybr = [DT(nc.dram_tensor("ybr%d" % i, [DM, NL], BF16).ap()) for i in range(2)]

    def seq_pieces(c0, L):
        out = []
        if c0 < NL:
            w = min(c0 + L, NL) - c0
            out.append((0, c0, w, 0))
        if c0 + L > NL:
            b0 = max(c0, NL) - NL
            out.append((1, b0, c0 + L - NL - b0, max(c0, NL) - c0))
        return out

    def ldp(dstf, r0, c0, L):
        for (h, b0, w, off) in seq_pieces(c0, L):
            g0 = (r0 // 256) * 512 + h * 256 + (r0 % 256)
            P.dma("sp", dstf(off, off + w), projG[g0:g0 + 128, b0:b0 + w])

    def sty(r0, c0, L, srcf):
        for (h, b0, w, off) in seq_pieces(c0, L):
            P.dma("sp", ybr[h][r0:r0 + 128, b0:b0 + w], srcf(off, off + w))
    s5scr = DT(nc.dram_tensor("s5scr", [2, 4, 16, 64], F32).ap())
    g16L = DT(nc.dram_tensor("g16L", [256, NTOK], BF16).ap())
    g16G = DT(nc.dram_tensor("g16G", [512, NTOK], BF16).ap())
    agL = DT(nc.dram_tensor("agL", [256, NTOK], BF16).ap())
    agG = DT(nc.dram_tensor("agG", [512, NTOK], BF16).ap())
    DBG = {}
    if dbg:
        for nm, shp, dt_ in (("d_xT", [DM, NL], F32), ("d_projT", [2 * INW, NL], F32), ("d_ybr0", [DM, NL], BF16),
                             ("d_ybr1", [DM, NL], BF16), ("d_xT2", [DM, NL], F32)):
            DBG[nm] = DT(nc.dram_tensor(nm, shp, dt_, kind="ExternalOutput").ap())

    top = ExitStack()
    with top:
        P = Prog(nc, top)
        P.ncores = ncores
        PS = [P.ps(top, [128, 512], F32, "bank") for _ in range(8)]
        mats = P.sb(top, [128, 3, 128], F32, "mats")
        P.dma("sp", mats[:], View(IN["mats"].ap.rearrange("p (a b) -> p a b", a=3), None))
        ident = mats[:, 0, :]; ropePT = mats[:, 1, :]; JT = mats[:, 2, :]
        ones_dm = P.sb(top, [128, 128], BF16, "ones_dm"); P.memset(ones_dm[:], 1.0 / DM)
        ones_hd = P.sb(top, [128, 128], BF16, "ones_hd"); P.memset(ones_hd[:], 1.0 / 128)
        ones1 = P.sb(top, [128, 128], BF16, "ones1"); P.memset(ones1[:], 1.0)
        modT = P.sb(top, [128, NMOD * 16, 2], F32, "modT")
        Acoef = P.sb(top, [128, 3, 16, 2], F32, "Acoef")
        gateh = P.sb(top, [128, 3, 16, 2], F32, "gateh")
        ncols = P.sb(top, [128, 3, 16], F32, "ncols")
        nfin = P.sb(top, [128, 16], F32, "nfin")
        bgate = P.sb(top, [128, 64], F32, "bgate")
        P.dma("sp", nfin[:], View(IN["norm_final"].ap.rearrange("(t p) -> p t", p=128), None), slow=True)

        def colload(dst, src_ap):
            P.dma("sp", dst, View(src_ap.rearrange("(t p) -> p t", p=128), None), slow=True)

        def segs_of(ps_):
            if ps_ == 0:
                return [(0, 256, 1), (256, 256, 0), (512, 64, 0)]
            return [(0, 512, 0), (512, 64, 0)]

        def modsegs(ps_):
            return [(0, 256, 1), (256, PW - 256, 0)] if ps_ == 0 else [(0, PW, 0)]

        selc = P.sb(top, [128, 2], F32, "selc")
        P.dma("sp", selc[:], View(IN["sel"].ap.broadcast_to([128, 2]), None))
        with P.scope() as es:
            stg = [P.sb(es, [128, DM], F32, "xin") for _ in range(2)]
            oT = [P.sb(es, [128, 16, 128], F32, "xo") for _ in range(2)]
            for ti in range(NL // 128):
                s = stg[ti % 2]; o = oT[ti % 2]
                src = IN["xl"][ti * 128:(ti + 1) * 128, :]
                P.dma("sp", s[:], src)
                for dq in range(4):
                    bank = PS[(ti * 4 + dq) % 8]
                    for di in range(4):
                        d = dq * 4 + di
                        P.transpose(bank[:, di * 128:(di + 1) * 128], s[:, d * 128:(d + 1) * 128], ident)
                    P.copy(View(o.t[:, dq * 4:(dq + 1) * 4, :], o.buf),
                           View(bank.t[:, :].rearrange("p (a b) -> p a b", a=4), bank.buf),
                           eng="act" if dq % 2 else "dve")
                P.dma("sp", View(xT.ap.rearrange("(d p) t -> p d t", p=128)[:, :, ti * 128:(ti + 1) * 128], None), o[:])
        P.barrier()

        def load_X(X, ps_):
            c0 = ps_ * PW
            for dt_ in range(16):
                P.dma("sp", X[:, dt_, :], xT[dt_ * 128:(dt_ + 1) * 128, c0:c0 + PW])

        def store_X(X, ps_):
            c0 = ps_ * PW
            for dt_ in range(16):
                P.dma("sp", xT[dt_ * 128:(dt_ + 1) * 128, c0:c0 + PW], X[:, dt_, :])

        def norm_mod(es, X, U, ps_, sub, lidx):
            sq = [P.sb(es, [128, PW], BF16, "sq") for _ in range(2)]
            RS = P.sb(es, [128, PW], F32, "RS")
            tmp = [P.sb(es, [128, PW], F32, "nt") for _ in range(2)]
            cks = chunks(PW)
            for dt_ in range(16):
                s = sq[dt_ % 2]
                P.act(s[:], X[:, dt_, :], AF.Square)
                for ci, (c0, cw) in enumerate(cks):
                    P.mm(PS[ci][:, 0:cw], ones_dm[:], s[:, c0:c0 + cw], start=(dt_ == 0), stop=(dt_ == 15))
            for ci, (c0, cw) in enumerate(cks):
                P.act(RS[:, c0:c0 + cw], PS[ci][:, 0:cw], AF.Sqrt, bias=EPS)
                P.recip(RS[:, c0:c0 + cw], RS[:, c0:c0 + cw])
            for dt_ in range(16):
                t = tmp[dt_ % 2]
                P.tt(t[:], X[:, dt_, :], RS[:], ALU.mult)
                for (s0, sw, mi) in modsegs(ps_):
                    P.act(U[:, dt_, s0:s0 + sw], t[:, s0:s0 + sw], AF.Identity,
                          bias=modT[:, (3 * sub) * 16 + dt_, mi:mi + 1], scale=Acoef[:, sub, dt_, mi:mi + 1])

        def ffn(es, X, U, ps_, wi, wo, sub, l):
            GT = 4
            wa = [P.sb(es, [128, 16, GT * 128], BF16, "wa") for _ in range(2)]
            wb = [P.sb(es, [128, 16, GT * 128], BF16, "wb") for _ in range(2)]
            wot = [P.sb(es, [128, GT, DM], BF16, "wo") for _ in range(2)]
            H = [P.sb(es, [128, GT, PW], BF16, "H") for _ in range(2)]
            sa = [P.sb(es, [128, 512], F32, "sa") for _ in range(2)]
            cks = chunks(PW)
            wiv = wi.ap[l].rearrange("(k p) n -> p k n", p=128)
            wov = wo.ap[l].rearrange("(j p) n -> p j n", p=128)
            ngrp = DFF // (GT * 128)
            cnt = 0
            for g in range(ngrp):
                sl = g % 2
                P.dma("pool", wa[sl][:], View(wiv[:, :, g * GT * 128:(g + 1) * GT * 128], None))
                P.dma("pool", wb[sl][:], View(wiv[:, :, DFF + g * GT * 128:DFF + (g + 1) * GT * 128], None))
                P.dma("pool", wot[sl][:], View(wov[:, g * GT:(g + 1) * GT, :], None))
                for j in range(GT):
                    for ci, (c0, cw) in enumerate(cks):
                        pa = PS[(cnt % 2) * 2]; pb = PS[(cnt % 2) * 2 + 1]; cnt += 1
                        for kt in range(16):
                            P.mm(pa[:, 0:cw], wa[sl][:, kt, j * 128:(j + 1) * 128], U[:, kt, c0:c0 + cw],
                                 start=(kt == 0), stop=(kt == 15))
                        for kt in range(16):
                            P.mm(pb[:, 0:cw], wb[sl][:, kt, j * 128:(j + 1) * 128], U[:, kt, c0:c0 + cw],
                                 start=(kt == 0), stop=(kt == 15))
                        s_ = sa[cnt % 2]
                        P.act(s_[:, 0:cw], pa[:, 0:cw], AF.Silu)
                        P.tt(H[sl][:, j, c0:c0 + cw], s_[:, 0:cw], pb[:, 0:cw], ALU.mult)
                oc = 0
                for dt_ in range(16):
                    for ci, (c0, cw) in enumerate(cks):
                        po = PS[4 + (oc % 4)]; oc += 1
                        for j in range(GT):
                            P.mm(po[:, 0:cw], wot[sl][:, j, dt_ * 128:(dt_ + 1) * 128], H[sl][:, j, c0:c0 + cw],
                                 start=(j == 0), stop=(j == GT - 1))
                        for (s0, sw, mi) in segs_of(ps_):
                            if s0 < c0 or s0 >= c0 + cw:
                                continue
                            P.stt(X[:, dt_, s0:s0 + sw], po[:, s0 - c0:s0 - c0 + sw], gateh[:, sub, dt_, mi:mi + 1],
                                  X[:, dt_, s0:s0 + sw], ALU.mult, ALU.add)

        for l in range(depth):
            last = (l == depth - 1)
            with P.scope() as es:
                cs = P.sb(es, [128, 16, 2], F32, "cs")
                csb = P.sb(es, [128, 16, 2], BF16, "csb")
                bada = P.sb(es, [128, NMOD * 16], F32, "bada")
                for r in range(2):
                    colload(cs[:, :, r], IN["cvec"].ap[r])
                P.act(csb[:], cs[:], AF.Silu)
                for i in range(NMOD):
                    colload(bada[:, i * 16:(i + 1) * 16], IN["b_ada"].ap[l][i * DM:(i + 1) * DM])
                for i, nm in enumerate(("norm_ffn1", "norm_mix", "norm_ffn2")):
                    colload(ncols[:, i, :], IN[nm].ap[l])
                for n_ in range(4):
                    colload(bgate[:, n_ * 16:(n_ + 1) * 16], IN["b_gate"].ap[l][n_ * DM:(n_ + 1) * DM])
                wsl = [P.sb(es, [128, 16, 512], BF16, "wada") for _ in range(2)]
                wv = IN["w_ada"].ap[l].rearrange("(k p) n -> p k n", p=128)
                for blk in range(NMOD * DM // 512):
                    w = wsl[blk % 2]
                    P.dma("pool", w[:], View(wv[:, :, blk * 512:(blk + 1) * 512], None))
                    bank = PS[blk % 4]
                    for fi in range(4):
                        for kt in range(16):
                            P.mm(bank[:, fi * 2:fi * 2 + 2], w[:, kt, fi * 128:(fi + 1) * 128], csb[:, kt, :],
                                 start=(kt == 0), stop=(kt == 15))
                    P.tt(modT[:, blk * 4:(blk + 1) * 4, :],
                         View(bank.t[:, 0:8].rearrange("p (a b) -> p a b", b=2), bank.buf),
                         View(bada.t[:, blk * 4:(blk + 1) * 4].unsqueeze(2).to_broadcast([128, 4, 2]), bada.buf),
                         ALU.add)
                for sub in range(3):
                    sc = modT[:, (3 * sub + 1) * 16:(3 * sub + 2) * 16, :]
                    P.ts(Acoef[:, sub, :, :], sc, 1.0, None, op0=ALU.add)
                    P.tt(Acoef[:, sub, :, :], Acoef[:, sub, :, :],
                         View(ncols.t[:, sub, :].unsqueeze(2).to_broadcast([128, 16, 2]), ncols.buf), ALU.mult)
                    gt_ = modT[:, (3 * sub + 2) * 16:(3 * sub + 3) * 16, :]
                    P.ts(gateh[:, sub, :, :], gt_, 1.0 if sub == 1 else 0.5, None, op0=ALU.mult)
            P.barrier()


            for ps_ in range(NPASS):
                with P.scope() as es:
                    X = P.sb(es, [128, 16, PW], F32, "X")
                    U = P.sb(es, [128, 16, PW], BF16, "U")
                    load_X(X, ps_)
                    with P.scope() as es2:
                        norm_mod(es2, X, U, ps_, 0, l)
                    with P.scope() as es2:
                        ffn(es2, X, U, ps_, IN["ffn1_wi"], IN["ffn1_wo"], 0, l)
                    P.barrier()

                    with P.scope() as es2:
                        norm_mod(es2, X, U, ps_, 1, l)
                    with P.scope() as es2:
                        wsl = [P.sb(es2, [128, 16, 256], BF16, "win") for _ in range(2)]
                        pj = [P.sb(es2, [128, PW], F32, "pj") for _ in range(2)]
                        wv = IN["w_in"].ap[l].rearrange("(k p) n -> p k n", p=128)
                        for blk in range(INW // 256):
                            w = wsl[blk % 2]
                            P.dma("pool", w[:], View(wv[:, :, blk * 256:(blk + 1) * 256], None))
                            for j in range(2):
                                ct = blk * 2 + j
                                o = pj[ct % 2]
                                for ci, (c0, cw) in enumerate(chunks(PW)):
                                    bank = PS[(ct * 2 + ci) % 8]
                                    for kt in range(16):
                                        P.mm(bank[:, 0:cw], w[:, kt, j * 128:(j + 1) * 128], U[:, kt, c0:c0 + cw],
                                             start=(kt == 0), stop=(kt == 15))
                                    P.copy(o[:, c0:c0 + cw], bank[:, 0:cw], eng="act" if ci else "dve")
                                P.dma("sp", projL[l % 2][ct * 128:(ct + 1) * 128, ps_ * PW:(ps_ + 1) * PW], o[:])
                    store_X(X, ps_)
                P.barrier()
            P.allgather(projL[l % 2].ap[:, :], projG.ap[:, :])
            if dbg and l == 0:
                P.dma("sp", DBG["d_xT"][:, :], xT[:, :]); P.dma("sp", DBG["d_projT"][:, :], projG[:, :])
                P.barrier()

            mixers(P, nc, IN, PS, l, False, ldp, sty, s5scr, ident, ropePT, JT, ones_hd, ones1, colload, DBG,
                   selc=selc, g16L=g16L, g16G=g16G, agL=agL, agG=agG)
            P.barrier()
            if dbg and l == 0:
                P.dma("sp", DBG["d_ybr0"][:, :], ybr[0][:, :]); P.dma("sp", DBG["d_ybr1"][:, :], ybr[1][:, :])
                P.barrier()

            for ps_ in range(NPASS):
                with P.scope() as es:
                    X = P.sb(es, [128, 16, PW], F32, "X")
                    U = P.sb(es, [128, 16, PW], BF16, "U")
                    load_X(X, ps_)
                    with P.scope() as es2:
                        norm_mod(es2, X, U, ps_, 1, l)
                    with P.scope() as es2:
                        MG = P.sb(es2, [128, 16, PW], BF16, "MG")
                        with P.scope() as es3:
                            YB = P.sb(es3, [128, 16, PW], BF16, "YB")
                            ya = [P.sb(es3, [128, PW], BF16, "ya") for _ in range(2)]
                            yb_ = [P.sb(es3, [128, PW], BF16, "yb") for _ in range(2)]
                            for dt_ in range(16):
                                a_ = ya[dt_ % 2]; b_ = yb_[dt_ % 2]
                                if 4 <= dt_ < 8:
                                    P.dma("sp", a_[:], agG[(dt_ - 4) * 128:(dt_ - 3) * 128, ps_ * PW:(ps_ + 1) * PW])
                                    P.dma("sp", b_[:], agG[(dt_ - 4) * 128:(dt_ - 3) * 128, NL + ps_ * PW:NL + (ps_ + 1) * PW])
                                else:
                                    P.dma("sp", a_[:], ybr[0][dt_ * 128:(dt_ + 1) * 128, ps_ * PW:(ps_ + 1) * PW])
                                    P.dma("sp", b_[:], ybr[1][dt_ * 128:(dt_ + 1) * 128, ps_ * PW:(ps_ + 1) * PW])
                                P.ts(a_[:], a_[:], selc[:, 0:1], None, op0=ALU.mult)
                                P.stt(YB[:, dt_, :], b_[:], selc[:, 1:2], a_[:], ALU.mult, ALU.add)
                            wg = [P.sb(es3, [128, 16, 256], BF16, "wg") for _ in range(2)]
                            wbr = [P.sb(es3, [128, 4, 256], BF16, "wbr") for _ in range(2)]
                            acc = P.sb(es3, [128, 2, PW], F32, "acc")
                            sg = [P.sb(es3, [128, 512], F32, "sg") for _ in range(2)]
                            wgv = IN["w_gate"].ap[l].rearrange("(k p) n -> p k n", p=128)
                            it = 0
                            for mq in range(8):
                                for n_ in range(4):
                                    sl = it % 2; it += 1
                                    P.dma("pool", wg[sl][:], View(wgv[:, :, n_ * DM + mq * 256:n_ * DM + (mq + 1) * 256], None))
                                    P.dma("pool", wbr[sl][:], View(IN["w_branch"].ap[l][n_].rearrange("(k p) n -> p k n", p=128)[:, :, mq * 256:(mq + 1) * 256], None))
                                    for mi in range(2):
                                        m = mq * 2 + mi
                                        for ci, (c0, cw) in enumerate(chunks(PW)):
                                            pg = PS[(it * 4 + mi * 2 + ci) % 4]; pp = PS[4 + (it * 4 + mi * 2 + ci) % 4]
                                            for kt in range(16):
                                                P.mm(pg[:, 0:cw], wg[sl][:, kt, mi * 128:(mi + 1) * 128], U[:, kt, c0:c0 + cw],
                                                     start=(kt == 0), stop=(kt == 15))
                                            for k4 in range(4):
                                                P.mm(pp[:, 0:cw], wbr[sl][:, k4, mi * 128:(mi + 1) * 128], YB[:, n_ * 4 + k4, c0:c0 + cw],
                                                     start=(k4 == 0), stop=(k4 == 3))
                                            s_ = sg[(mi * 2 + ci) % 2]
                                            P.act(s_[:, 0:cw], pg[:, 0:cw], AF.Sigmoid, bias=bgate[:, n_ * 16 + m:n_ * 16 + m + 1])
                                            if n_ == 0:
                                                P.tt(acc[:, mi, c0:c0 + cw], s_[:, 0:cw], pp[:, 0:cw], ALU.mult)
                                            else:
                                                P.tt(s_[:, 0:cw], s_[:, 0:cw], pp[:, 0:cw], ALU.mult)
                                                P.tt(acc[:, mi, c0:c0 + cw], acc[:, mi, c0:c0 + cw], s_[:, 0:cw], ALU.add)
                                    if n_ == 3:
                                        for mi in range(2):
                                            P.copy(MG[:, mq * 2 + mi, :], acc[:, mi, :], eng="act")
                        P.barrier()
                        with P.scope() as es3:
                            wsl = [P.sb(es3, [128, 16, 256], BF16, "wout") for _ in range(2)]
                            wv = IN["w_out"].ap[l].rearrange("(k p) n -> p k n", p=128)
                            oc = 0
                            for blk in range(8):
                                w = wsl[blk % 2]
                                P.dma("pool", w[:], View(wv[:, :, blk * 256:(blk + 1) * 256], None))
                                for j in range(2):
                                    m2 = blk * 2 + j
                                    for ci, (c0, cw) in enumerate(chunks(PW)):
                                        po = PS[oc % 8]; oc += 1
                                        for kt in range(16):
                                            P.mm(po[:, 0:cw], w[:, kt, j * 128:(j + 1) * 128], MG[:, kt, c0:c0 + cw],
                                                 start=(kt == 0), stop=(kt == 15))
                                        for (s0, sw, mi) in segs_of(ps_):
                                            if s0 < c0 or s0 >= c0 + cw:
                                                continue
                                            P.stt(X[:, m2, s0:s0 + sw], po[:, s0 - c0:s0 - c0 + sw], gateh[:, 1, m2, mi:mi + 1],
                                                  X[:, m2, s0:s0 + sw], ALU.mult, ALU.add)
                    P.barrier()
                    with P.scope() as es2:
                        norm_mod(es2, X, U, ps_, 2, l)
                    with P.scope() as es2:
                        ffn(es2, X, U, ps_, IN["ffn2_wi"], IN["ffn2_wo"], 2, l)
                    P.barrier()
                    if not last:
                        store_X(X, ps_)
                    else:
                        if dbg:
                            store_X(X, ps_)
                        with P.scope() as es2:
                            sq = [P.sb(es2, [128, PW], BF16, "sq") for _ in range(2)]
                            RS = P.sb(es2, [128, PW], F32, "RS")
                            yo = [P.sb(es2, [128, DM], F32, "yo") for _ in range(2)]
                            cks = chunks(PW)
                            for dt_ in range(16):
                                s = sq[dt_ % 2]
                                P.act(s[:], X[:, dt_, :], AF.Square)
                                for ci, (c0, cw) in enumerate(cks):
                                    P.mm(PS[ci][:, 0:cw], ones_dm[:], s[:, c0:c0 + cw], start=(dt_ == 0), stop=(dt_ == 15))
                            for ci, (c0, cw) in enumerate(cks):
                                P.act(RS[:, c0:c0 + cw], PS[ci][:, 0:cw], AF.Sqrt, bias=EPS)
                                P.recip(RS[:, c0:c0 + cw], RS[:, c0:c0 + cw])
                            for dt_ in range(16):
                                P.stt(X[:, dt_, :], X[:, dt_, :], nfin[:, dt_:dt_ + 1], RS[:], ALU.mult, ALU.mult)
                            P.barrier()
                            for tt_, (k0, kw) in enumerate(chunks(PW, 128)):
                                o = yo[tt_ % 2]
                                for dq in range(4):
                                    bank = PS[(tt_ * 4 + dq) % 8]
                                    for di in range(4):
                                        d = dq * 4 + di
                                        P.transpose(bank[0:kw, di * 128:(di + 1) * 128], X[:, d, k0:k0 + kw], ident)
                                    P.copy(o[0:kw, dq * 512:(dq + 1) * 512], bank[0:kw, :], eng="act" if dq % 2 else "dve")
                                tok0 = ps_ * PW + k0
                                P.dma("sp", OUT[tok0:tok0 + kw, :], o[0:kw, :])
                P.barrier()
            if dbg and l == 0:
                P.dma("sp", DBG["d_xT2"][:, :], xT[:, :])
                P.barrier()
        P.barrier()
    return nc


def mixers(P, nc, IN, PS, l, last, ldp, sty, s5scr, ident, ropePT, JT, ones_hd, ones1, colload, DBG=None,
           selc=None, g16L=None, g16G=None, agL=None, agG=None):
    NG = 16; NCT = 2

    def ldsel(dst, tmp, row_lo, row_hi, c0, L):
        ldp(lambda a, b: dst[:, a:b], row_lo, c0, L)
        ldp(lambda a, b: tmp[:, a:b], row_hi, c0, L)
        P.ts(dst[:, 0:L], dst[:, 0:L], selc[:, 0:1], None, op0=ALU.mult)
        P.stt(dst[:, 0:L], tmp[:, 0:L], selc[:, 1:2], dst[:, 0:L], ALU.mult, ALU.add)

    SEQS = [(0, CTX), (CTX, SEQ)]

    with P.scope() as es:
        OFFP = 16
        pw_ = P.sb(es, [128, 4, 128], BF16, "poolw")
        P.dma("pool", pw_[:], View(IN["pool_w"].ap[l].rearrange("g c d -> c g d"), None))
        pscale = P.sb(es, [128, 4], F32, "pscale")
        colload(pscale[:], IN["pool_scale"].ap[l])
        A = P.sb(es, [128, SEQ + 32], F32, "pa")
        W1 = P.sb(es, [128, SEQ + 32], F32, "pw1")
        W2 = P.sb(es, [128, SEQ + 32], F32, "pw2")
        RC = P.sb(es, [128, SEQ], F32, "prc")
        PO = P.sb(es, [128, SEQ], BF16, "ppo")
        YO = P.sb(es, [128, SEQ], BF16, "pyo")
        for gi in range(4):
            for (c0, L) in SEQS:
                P.memset(A[:, 0:OFFP], 0.0); P.memset(A[:, OFFP + L:OFFP + L + 16], 0.0)
                ldp(lambda a, b: A[:, OFFP + a:OFFP + b], gi * 128, c0, L)
                P.dma("sp", RC[:, 0:L], View(IN["rc%d" % L].ap[gi:gi + 1, :].broadcast_to([128, L]), None))
                lo, hi = OFFP - 8, OFFP + L + 8
                P.tt(W1[:, lo:hi], A[:, lo - 1:hi - 1], A[:, lo:hi], ALU.add)
                cur, nxt = W1, W2
                half = 1
                for lev in range(gi):
                    lo += 2; hi -= 2
                    if lev == 2:
                        lo, hi = OFFP, OFFP + L
                    P.tt(nxt[:, lo:hi], cur[:, lo - half:hi - half], cur[:, lo + half:hi + half], ALU.add)
                    cur, nxt = nxt, cur
                    half *= 2
                P.tt(nxt[:, OFFP:OFFP + L], cur[:, OFFP:OFFP + L], RC[:, 0:L], ALU.mult)
                P.tt(PO[:, 0:L], nxt[:, OFFP:OFFP + L], A[:, OFFP:OFFP + L], ALU.subtract)
                for ci, (k0, kw) in enumerate(chunks(L)):
                    bank = PS[ci % 8]
                    P.mm(bank[:, 0:kw], pw_[:, gi, :], PO[:, k0:k0 + kw])
                    P.act(YO[:, k0:k0 + kw], bank[:, 0:kw], AF.Identity, scale=pscale[:, gi:gi + 1])
                sty(gi * 128, c0, L, lambda a, b: YO[:, a:b])
    P.barrier()

    with P.scope() as es:
        qn = P.sb(es, [128, 1], F32, "qn"); kn = P.sb(es, [128, 1], F32, "kn")
        colload(qn[:], IN["q_norm"].ap[l]); colload(kn[:], IN["k_norm"].ap[l])
        CS = P.sb(es, [128, 2, SEQ], F32, "ropecs")
        P.dma("sp", CS[:], View(IN["ropecs"].ap.rearrange("p (a b) -> p a b", a=2), None))
        Q16 = P.sb(es, [128, 2, NTOK], BF16, "Q16")
        K16 = P.sb(es, [128, 1, NTOK], BF16, "K16")
        Vt = P.sb(es, [128, NTOK // 128, 128], BF16, "Vt")
        with P.scope() as es2:
            raw = [P.sb(es2, [128, NTOK], F32, "raw") for _ in range(2)]
            raw2 = P.sb(es2, [128, NTOK], F32, "raw2")
            sq = P.sb(es2, [128, NTOK], BF16, "asq")
            RS = P.sb(es2, [128, NTOK], F32, "aRS")
            t1 = P.sb(es2, [128, 512], F32, "at1"); t2 = P.sb(es2, [128, 512], F32, "at2")
            for hi_ in range(3):
                r = raw[hi_ % 2]
                if hi_ < 2:
                    ldsel(r, raw2, OFF_Q + hi_ * 128, OFF_Q + (2 + hi_) * 128, 0, NTOK)
                else:
                    ldsel(r, raw2, OFF_K, OFF_K + 128, 0, NTOK)
                gcol = qn if hi_ < 2 else kn
                dst = Q16 if hi_ < 2 else K16
                hh = hi_ if hi_ < 2 else 0
                P.act(sq[:], r[:], AF.Square)
                for ci, (k0, kw) in enumerate(chunks(NTOK)):
                    bank = PS[ci % 4]
                    P.mm(bank[:, 0:kw], ones_hd[:], sq[:, k0:k0 + kw])
                    P.act(RS[:, k0:k0 + kw], bank[:, 0:kw], AF.Sqrt, bias=EPS)
                    P.recip(RS[:, k0:k0 + kw], RS[:, k0:k0 + kw])
                P.stt(r[:], r[:], gcol[:, 0:1], RS[:], ALU.mult, ALU.mult)
                P.copy(dst[:, hh, 0:CTX], r[:, 0:CTX], eng="act")
                for ci, (k0, kw) in enumerate(chunks(SEQ)):
                    bank = PS[4 + ci % 4]
                    P.mm(bank[:, 0:kw], ropePT, r[:, CTX + k0:CTX + k0 + kw])
                    P.tt(t1[:, 0:kw], r[:, CTX + k0:CTX + k0 + kw], CS[:, 0, k0:k0 + kw], ALU.mult)
                    P.tt(t2[:, 0:kw], bank[:, 0:kw], CS[:, 1, k0:k0 + kw], ALU.mult)
                    P.tt(dst[:, hh, CTX + k0:CTX + k0 + kw], t1[:, 0:kw], t2[:, 0:kw], ALU.add)
            for hk in range(1):
                r = raw[hk % 2]
                ldsel(r, raw2, OFF_V, OFF_V + 128, 0, NTOK)
                for kt in range(NTOK // 128):
                    bank = PS[kt % 8]
                    P.transpose(bank[:, 0:128], r[:, kt * 128:(kt + 1) * 128], ident)
                    P.copy(Vt[:, kt, hk * 128:(hk + 1) * 128], bank[:, 0:128], eng="act" if kt % 2 else "dve")
        P.barrier()
        with P.scope() as es2:
            Pt = [P.sb(es2, [128, 512], BF16, "Pt") for _ in range(3)]
            rd = P.sb(es2, [128, 512], F32, "rd")
            O = [P.sb(es2, [128, 512], BF16, "O") for _ in range(2)]
            sc = 1.0 / math.sqrt(128.0)
            it = 0
            for h in range(2):
                kv = 0
                qcs = [(0, CTX, [0, 1])] + [(CTX + k0, kw, list(range(NTOK // 128))) for (k0, kw) in chunks(SEQ)]
                if last:
                    qcs = qcs[1:]
                for (q0, qw, kts) in qcs:
                    po = PS[4 + (it % 2) * 2]; pd = PS[5 + (it % 2) * 2]; it += 1
                    for i, kt in enumerate(kts):
                        pss = PS[i % 4]
                        P.mm(pss[:, 0:qw], K16[:, kv, kt * 128:(kt + 1) * 128], Q16[:, h, q0:q0 + qw])
                        pt = Pt[i % 3]
                        P.act(pt[:, 0:qw], pss[:, 0:qw], AF.Exp, bias=-11.3, scale=sc)
                        P.mm(po[:, 0:qw], Vt[:, kt, kv * 128:(kv + 1) * 128], pt[:, 0:qw], start=(i == 0), stop=(i == len(kts) - 1))
                        P.mm(pd[:, 0:qw], ones1[:], pt[:, 0:qw], start=(i == 0), stop=(i == len(kts) - 1))
                    P.recip(rd[:, 0:qw], pd[:, 0:qw])
                    o = O[it % 2]
                    P.tt(o[:, 0:qw], po[:, 0:qw], rd[:, 0:qw], ALU.mult)
                    P.dma("sp", agL[h * 128:(h + 1) * 128, q0:q0 + qw], o[:, 0:qw])
    P.allgather(agL.ap[:, :], agG.ap[:, :])

    with P.scope() as es:
        hw = P.sb(es, [128, 4, 12], F32, "hyw")
        for tap in range(3):
            colload(hw[:, tap, :], IN["hy_short_w"].ap[l][tap])
        colload(hw[:, 3, :], IN["hy_short_b"].ap[l])
        hbias = P.sb(es, [128, 4], F32, "hybias"); colload(hbias[:], IN["hy_bias"].ap[l])
        f1w = P.sb(es, [33, 64], F32, "f1w"); P.dma("sp", f1w[:], View(IN["hy_f1_w"].ap[l], None))
        f2w = P.sb(es, [64, 64], F32, "f2w"); P.dma("sp", f2w[:], View(IN["hy_f2_w"].ap[l], None))
        f3w = P.sb(es, [64, 1024], F32, "f3w"); P.dma("sp", f3w[:], View(IN["hy_f3_w"].ap[l], None))
        fv = P.sb(es, [64, 3], F32, "fv")
        for i, nm in enumerate(("hy_f1_b", "hy_f2_b", "hy_freq")):
            P.dma("sp", fv[:, i:i + 1], View(IN[nm].ap[l].rearrange("(p o) -> p o", o=1), None), slow=True)
        fb = P.sb(es, [64, 3], F32, "fb")
        P.ts(fb[:, 0:2], fv[:, 0:2], fv[:, 2:3], 1.0 / (2.0 * PI), op0=ALU.mult, op1=ALU.mult)
        P.ts(fb[:, 2:3], fv[:, 2:3], 1.0 / (2.0 * PI), None, op0=ALU.mult)
        for (c0, L) in SEQS:
            if last and c0 == 0:
                continue
            nt = L // 128; TC = min(512, L); ntc = L // TC
            with P.scope() as es1:
                vxT = P.sb(es1, [128, 4, L], F32, "vxT")
                Y = P.sb(es1, [128, nt, 2, 512], BF16, "hyY")
                with P.scope() as es2:
                    AB = P.sb(es2, [128, nt, 2, 512], BF16, "hyAB")
                    vx = P.sb(es2, [128, nt, 512], BF16, "vxtok")
                    with P.scope() as es3:
                        zT = P.sb(es3, [33, L], F32, "zT"); P.dma("sp", zT[:], View(IN["zT%d" % L].ap, None))
                        h1 = P.sb(es3, [64, L], F32, "h1"); h2 = P.sb(es3, [64, L], F32, "h2")
                        hki = P.sb(es3, [64, L], I32, "hki")
                        dec = [P.sb(es3, [128, 1024], F32, "dec") for _ in range(2)]
                        hh = P.sb(es3, [128, 1024], F32, "hh")
                        for (wt, src, dst, bi) in ((f1w, zT, h1, 0), (f2w, h1, h2, 1)):
                            for ci, (k0, kw) in enumerate(chunks(L)):
                                bank = PS[ci % 4]
                                P.mm(bank[0:64, 0:kw], wt[:, :], src[:, k0:k0 + kw])
                                P.ts(dst[:, k0:k0 + kw], bank[0:64, 0:kw], fb[:, 2:3], fb[:, bi:bi + 1], op0=ALU.mult, op1=ALU.add)
                                P.copy(hki[:, k0:k0 + kw], dst[:, k0:k0 + kw])
                                P.tt(dst[:, k0:k0 + kw], dst[:, k0:k0 + kw], hki[:, k0:k0 + kw], ALU.subtract)
                                P.act(dst[:, k0:k0 + kw], dst[:, k0:k0 + kw], AF.Sin, scale=TWO_PI_S)
                        for st in range(nt):
                            d_ = dec[st % 2]
                            P.dma("sp", d_[:], View(IN["dec%d" % L].ap[st * 128:(st + 1) * 128, :], None))
                            for half in range(2):
                                bank = PS[4 + (st * 2 + half) % 4]
                                P.mm(bank[:, 0:512], h2[:, st * 128:(st + 1) * 128], f3w[:, half * 512:(half + 1) * 512])
                                P.tt(hh[:, half * 512:(half + 1) * 512], bank[:, 0:512], d_[:, half * 512:(half + 1) * 512], ALU.mult)
                            P.tt(AB[:, st, 0, :], hh[:, 0:512], hh[:, 512:1024], ALU.add)
                            P.tt(AB[:, st, 1, :], hh[:, 0:512], hh[:, 512:1024], ALU.subtract)
                    with P.scope() as es3:
                        rin = [P.sb(es3, [128, L + 2], F32, "hrin") for _ in range(2)]
                        z1 = P.sb(es3, [128, L], F32, "hz1"); z2 = P.sb(es3, [128, L], F32, "hz2")

                        def sconv(dst, tl, buf_i):
                            r = rin[buf_i]
                            P.memset(r[:, 0:1], 0.0); P.memset(r[:, L + 1:L + 2], 0.0)
                            ldp(lambda a, b, r=r: r[:, 1 + a:1 + b], OFF_HY + tl * 128, c0, L)
                            P.ts(dst, r[:, 1:L + 1], hw[:, 1, tl:tl + 1], hw[:, 3, tl:tl + 1], op0=ALU.mult, op1=ALU.add)
                            P.stt(dst, r[:, 0:L], hw[:, 0, tl:tl + 1], dst, ALU.mult, ALU.add)
                            P.stt(dst, r[:, 2:L + 2], hw[:, 2, tl:tl + 1], dst, ALU.mult, ALU.add)
                        for ct in range(4):
                            sconv(z1[:], 4 + ct, 0)
                            sconv(z2[:], 8 + ct, 1)
                            P.tt(vxT[:, ct, :], z1[:], z2[:], ALU.mult)
                            for st in range(nt):
                                bank = PS[st % 8]
                                P.transpose(bank[:, 0:128], vxT[:, ct, st * 128:(st + 1) * 128], ident)
                                P.copy(vx[:, st, ct * 128:(ct + 1) * 128], bank[:, 0:128], eng="act" if st % 2 else "dve")
                    with P.scope() as es3:
                        ft_ = [P.sb(es3, [128, 2, nt, 128], BF16, "fwd") for _ in range(2)]
                        xk = [P.sb(es3, [128, 4, 512], F32, "xk") for _ in range(2)]
                        tq = P.sb(es3, [128, 4, 512], F32, "tq")
                        for ft in range(nt):
                            tb = ft_[ft % 2]
                            P.dma("sp", tb[:], View(IN["fwd%d" % L].ap[ft].rearrange("p (a s m) -> p a s m", a=2, s=nt), None))
                            banks = [PS[(ft % 2) * 4 + i] for i in range(4)]
                            for i, (cs_, src, comp) in enumerate(((0, vx, None), (1, vx, None), (0, AB, 0), (1, AB, 1))):
                                for st in range(nt):
                                    rhs = src[:, st, :] if comp is None else src[:, st, comp, :]
                                    P.mm(banks[i][:, 0:512], tb[:, cs_, st, :], rhs, start=(st == 0), stop=(st == nt - 1))
                            x_ = xk[ft % 2]
                            for i in range(4):
                                P.copy(x_[:, i, :], banks[i][:, 0:512], eng="act")
                            P.tt(tq[:, 0, :], x_[:, 0, :], x_[:, 2, :], ALU.mult)
                            P.tt(tq[:, 1, :], x_[:, 1, :], x_[:, 3, :], ALU.mult)
                            P.tt(tq[:, 2, :], x_[:, 0, :], x_[:, 3, :], ALU.mult)
                            P.tt(tq[:, 3, :], x_[:, 1, :], x_[:, 2, :], ALU.mult)
                            P.tt(Y[:, ft, 0, :], tq[:, 0, :], tq[:, 1, :], ALU.subtract)
                            P.tt(Y[:, ft, 1, :], tq[:, 2, :], tq[:, 3, :], ALU.add)
                with P.scope() as es2:
                    itb = [P.sb(es2, [128, 2, nt, TC], BF16, "inv") for _ in range(2)]
                    rin = P.sb(es2, [128, L + 2], F32, "hrin0")
                    x0 = P.sb(es2, [128, L], F32, "hx0")
                    yv = P.sb(es2, [128, TC], F32, "hyv")
                    yo = [P.sb(es2, [128, TC], BF16, "hyo") for _ in range(2)]
                    for ct in range(4):
                        P.memset(rin[:, 0:1], 0.0); P.memset(rin[:, L + 1:L + 2], 0.0)
                        ldp(lambda a, b: rin[:, 1 + a:1 + b], OFF_HY + ct * 128, c0, L)
                        P.ts(x0[:], rin[:, 1:L + 1], hw[:, 1, ct:ct + 1], hw[:, 3, ct:ct + 1], op0=ALU.mult, op1=ALU.add)
                        P.stt(x0[:], rin[:, 0:L], hw[:, 0, ct:ct + 1], x0[:], ALU.mult, ALU.add)
                        P.stt(x0[:], rin[:, 2:L + 2], hw[:, 2, ct:ct + 1], x0[:], ALU.mult, ALU.add)
                        for tc in range(ntc):
                            tb = itb[tc % 2]
                            P.dma("sp", tb[:], View(IN["inv%d" % L].ap[tc].rearrange("p (a f j) -> p a f j", a=2, f=nt), None))
                            bank = PS[(ct * ntc + tc) % 8]
                            n_acc = 2 * nt
                            i = 0
                            for comp in range(2):
                                for ft in range(nt):
                                    P.mm(bank[:, 0:TC], Y[:, ft, comp, ct * 128:(ct + 1) * 128], tb[:, comp, ft, :],
                                         start=(i == 0), stop=(i == n_acc - 1))
                                    i += 1
                            P.stt(yv[:], vxT[:, ct, tc * TC:(tc + 1) * TC], hbias[:, ct:ct + 1], bank[:, 0:TC], ALU.mult, ALU.add)
                            o = yo[tc % 2]
                            P.tt(o[:], yv[:], x0[:, tc * TC:(tc + 1) * TC], ALU.mult)
                            sty(1024 + ct * 128, c0 + tc * TC, TC, lambda a, b, o=o: o[:, a:b])
            P.barrier()
    P.barrier()

    with P.scope() as es:
        Bp = [P.sb(es, [128, NG, 2, 128], BF16, "Bp") for _ in range(2)]
        Cp = [P.sb(es, [128, NG, 2, 128], BF16, "Cp") for _ in range(2)]
        rcol = [P.sb(es, [128, NG], F32, "rcol") for _ in range(2)]
        thcol = [P.sb(es, [128, NG], F32, "thcol") for _ in range(2)]
        H0 = [P.sb(es, [128, NG], F32, "H0") for _ in range(2)]
        Dc = P.sb(es, [128, 2, NCT], F32, "Dc"); Dsum = P.sb(es, [128, NCT], F32, "Dsum")
        for d in range(2):
            colload(Dc[:, d, :], IN["s5_d"].ap[l][d])
        P.tt(Dsum[:], Dc[:, 0, :], Dc[:, 1, :], ALU.add)
        glub = P.sb(es, [128, 8], F32, "glub"); colload(glub[:], IN["s5_glu_b"].ap[l])
        gluw = P.sb(es, [128, 4, 1024], BF16, "gluw")
        P.dma("pool", gluw[:], View(IN["s5_glu_w"].ap[l].rearrange("(k p) n -> p k n", p=128), None))
        iota = P.sb(es, [128, SEQ], F32, "iota")
        P.dma("sp", iota[:], View(IN["iota1"].ap.broadcast_to([128, SEQ]), None))
        scrb = Buf()
        with P.scope() as es2:
            for d in range(2):
                are = P.sb(es2, [NG, 64], F32, "are"); aim = P.sb(es2, [NG, 64], F32, "aim")
                ldt = P.sb(es2, [NG, 1], F32, "ldt")
                P.dma("sp", are[:], View(IN["s5_a_re"].ap[l][d], None)); P.dma("sp", aim[:], View(IN["s5_a_im"].ap[l][d], None))
                P.dma("sp", ldt[:], View(IN["s5_log_dt"].ap[l][d].rearrange("(p o) -> p o", o=1), None), slow=True)
                dtc = P.sb(es2, [NG, 1], F32, "dtc"); P.act(dtc[:], ldt[:], AF.Exp)
                lre = P.sb(es2, [NG, 64], F32, "lre"); th = P.sb(es2, [NG, 64], F32, "th")
                P.ts(lre[:], are[:], dtc[:, 0:1], None, op0=ALU.mult)
                P.ts(th[:], aim[:], dtc[:, 0:1], None, op0=ALU.mult)
                r_ = P.sb(es2, [NG, 64], F32, "r_"); P.act(r_[:], lre[:], AF.Exp)
                sn = P.sb(es2, [NG, 64], F32, "sn"); cn = P.sb(es2, [NG, 64], F32, "cn")
                ki = P.sb(es2, [NG, 64], I32, "ki")
                P.ts(th[:], th[:], 1.0 / (2.0 * PI), None, op0=ALU.mult)
                P.copy(ki[:], th[:]); P.tt(sn[:], th[:], ki[:], ALU.subtract)
                P.act(sn[:], sn[:], AF.Sin, scale=TWO_PI_S)
                P.ts(cn[:], th[:], 0.25, None, op0=ALU.add)
                P.copy(ki[:], cn[:]); P.tt(cn[:], cn[:], ki[:], ALU.subtract)
                P.act(cn[:], cn[:], AF.Sin, scale=TWO_PI_S)
                nr = P.sb(es2, [NG, 64], F32, "nr"); ni = P.sb(es2, [NG, 64], F32, "ni")
                P.tt(nr[:], r_[:], cn[:], ALU.mult); P.ts(nr[:], nr[:], -1.0, None, op0=ALU.add)
                P.tt(ni[:], r_[:], sn[:], ALU.mult)
                den = P.sb(es2, [NG, 64], F32, "den"); t_ = P.sb(es2, [NG, 64], F32, "t_")
                P.tt(den[:], are[:], are[:], ALU.mult); P.tt(t_[:], aim[:], aim[:], ALU.mult)
                P.tt(den[:], den[:], t_[:], ALU.add); P.recip(den[:], den[:])
                cr = P.sb(es2, [NG, 64], F32, "cr"); ci_ = P.sb(es2, [NG, 64], F32, "ci")
                P.tt(cr[:], nr[:], are[:], ALU.mult); P.tt(t_[:], ni[:], aim[:], ALU.mult)
                P.tt(cr[:], cr[:], t_[:], ALU.add); P.tt(cr[:], cr[:], den[:], ALU.mult)
                P.tt(ci_[:], ni[:], are[:], ALU.mult); P.tt(t_[:], nr[:], aim[:], ALU.mult)
                P.tt(ci_[:], ci_[:], t_[:], ALU.subtract); P.tt(ci_[:], ci_[:], den[:], ALU.mult)
                for i, src in enumerate((cr, ci_, r_, th)):
                    P.dma("sp", s5scr[d, i, :, :], src[:], extra_w=[scrb])
            P.barrier()
            for d in range(2):
              with P.scope() as es2:
                for half in range(2):
                    P.dma("sp", rcol[d][half * 64:(half + 1) * 64, :], View(s5scr.ap[d, 2].rearrange("g p -> p g"), None), slow=True)
                    P.dma("sp", thcol[d][half * 64:(half + 1) * 64, :], View(s5scr.ap[d, 3].rearrange("g p -> p g"), None), slow=True)
                crb = P.sb(es2, [16, NG * 64], F32, "crb"); cib = P.sb(es2, [16, NG * 64], F32, "cib")
                P.dma("sp", crb[:], View(s5scr.ap[d, 0].rearrange("g p -> (g p)").rearrange("(o n) -> o n", o=1).broadcast_to([16, NG * 64]), None))
                P.dma("sp", cib[:], View(s5scr.ap[d, 1].rearrange("g p -> (g p)").rearrange("(o n) -> o n", o=1).broadcast_to([16, NG * 64]), None))
                bre = P.sb(es2, [16, NG, 64], F32, "bre"); bim = P.sb(es2, [16, NG, 64], F32, "bim")
                P.dma("sp", bre[:], View(IN["s5_bT_re"].ap[l][d].rearrange("g c p -> c g p"), None))
                P.dma("sp", bim[:], View(IN["s5_bT_im"].ap[l][d].rearrange("g c p -> c g p"), None))
                crv = View(crb.t[:, :].rearrange("c (g p) -> c g p", g=NG), crb.buf)
                civ = View(cib.t[:, :].rearrange("c (g p) -> c g p", g=NG), cib.buf)
                t1 = P.sb(es2, [16, NG, 64], F32, "t1"); t2 = P.sb(es2, [16, NG, 64], F32, "t2")
                bbr = P.sb(es2, [16, NG, 64], F32, "bbr"); bbi = P.sb(es2, [16, NG, 64], F32, "bbi")
                P.tt(t1[:], bre[:], crv, ALU.mult); P.tt(t2[:], bim[:], civ, ALU.mult); P.tt(bbr[:], t1[:], t2[:], ALU.subtract)
                P.tt(t1[:], bim[:], crv, ALU.mult); P.tt(t2[:], bre[:], civ, ALU.mult); P.tt(bbi[:], t1[:], t2[:], ALU.add)
                Bs = P.sb(es2, [16, NG, 2, 128], BF16, "Bs")
                P.copy(Bs[:, :, 0, 0:64], bbr[:]); P.copy(Bs[:, :, 0, 64:128], bbi[:])
                P.copy(Bs[:, :, 1, 0:64], bbi[:]); P.ts(Bs[:, :, 1, 64:128], bbr[:], -1.0, None, op0=ALU.mult)
                P.memset(Bp[d][:], 0.0)
                for j in range(8):
                    P.dma("sp", Bp[d][16 * j:16 * (j + 1), j::8, :, :], Bs[0:16, j::8, :, :])
                A1 = P.sb(es2, [128, NG, 16], F32, "cA1"); A2 = P.sb(es2, [128, NG, 16], F32, "cA2")
                cre_v = View(IN["s5_cT_re"].ap[l][d].rearrange("g p c -> p g c"), None)
                cim_v = View(IN["s5_cT_im"].ap[l][d].rearrange("g p c -> p g c"), None)
                P.dma("sp", A1[0:64, :, :], cre_v); P.dma("sp", A1[64:128, :, :], cim_v)
                P.dma("sp", A2[0:64, :, :], cim_v); P.dma("sp", A2[64:128, :, :], cre_v)
                P.ts(A1[64:128, :, :], A1[64:128, :, :], -1.0, None, op0=ALU.mult)
                P.ts(A2[:], A2[:], -1.0, None, op0=ALU.mult)
                P.memset(Cp[d][:], 0.0)
                for j in range(8):
                    P.copy(Cp[d][:, j::8, 0, 16 * j:16 * (j + 1)], A1[:, j::8, :])
                    P.copy(Cp[d][:, j::8, 1, 16 * j:16 * (j + 1)], A2[:, j::8, :])
        P.barrier()
        for (c0, L) in SEQS:
            is_ctx = (c0 == 0)
            need_out = not (last and is_ctx)
            cks = chunks(L)
            with P.scope() as es2:
                uT = P.sb(es2, [128, L], F32, "s5u"); uT2 = P.sb(es2, [128, L], F32, "s5u2")
                U16 = P.sb(es2, [128, NCT, L], BF16, "s5u16")
                G16 = P.sb(es2, [128, NCT, L], BF16, "s5g16")
                for ct in range(NCT):
                    ldsel(uT, uT2, OFF_S5 + ct * 128, OFF_S5 + 256 + ct * 128, c0, L)
                    P.copy(U16[:, ct, :], uT[:], eng="act")
                nck = len(cks)
                CW = cks[0][1]
                NSET = 2
                Ts = [P.sb(es2, [128, CW], F32, "Ts") for _ in range(NSET * nck)]
                Tc = [P.sb(es2, [128, CW], F32, "Tc") for _ in range(NSET * nck)]
                M1 = [P.sb(es2, [128, CW], F32, "M1") for _ in range(NSET * nck)]
                Z = [P.sb(es2, [128, CW], F32, "Z") for _ in range(NSET * nck)]
                Zc = [P.sb(es2, [128, CW], BF16, "Zc") for _ in range(NSET * nck)]
                Zs = [P.sb(es2, [128, CW], BF16, "Zs") for _ in range(NSET * nck)]
                KI = [P.sb(es2, [128, CW], I32, "KI") for _ in range(NSET * nck)]
                itn = 0
                hl = P.sb(es2, [128, 2], F32, "hl")
                for ct in range(NCT):
                    Ybanks = [PS[i] for i in range(nck)]
                    for d in range(2):
                        for gg in range(8):
                            g = ct * 8 + gg
                            th = thcol[d][:, g:g + 1]; rr = rcol[d][:, g:g + 1]
                            first = (d == 0 and gg == 0); lastm = (d == 1 and gg == 7)
                            sb_ = (itn % NSET) * nck; itn += 1
                            for jc, (j0, jw) in enumerate(cks):
                                tci = jc if d == 0 else nck - 1 - jc
                                k0 = cks[tci][0]
                                ts_, tc_, m1, z_, ki = Ts[sb_ + jc], Tc[sb_ + jc], M1[sb_ + jc], Z[sb_ + jc], KI[sb_ + jc]
                                P.act(ts_[:, 0:jw], iota[:, j0:j0 + jw], AF.Copy, scale=th)
                                P.copy(ki[:, 0:jw], ts_[:, 0:jw])
                                P.tt(tc_[:, 0:jw], ts_[:, 0:jw], ki[:, 0:jw], ALU.subtract)
                                P.act(ts_[:, 0:jw], tc_[:, 0:jw], AF.Sin, scale=TWO_PI_S)
                                P.act(tc_[:, 0:jw], tc_[:, 0:jw], AF.Sin, scale=PI)
                                P.act(tc_[:, 0:jw], tc_[:, 0:jw], AF.Square)
                                P.act(tc_[:, 0:jw], tc_[:, 0:jw], AF.Identity, bias=1.0, scale=-2.0)
                                p1 = PS[4 + (jc % 2) * 2]; p2 = PS[5 + (jc % 2) * 2]
                                P.mm(p1[:, 0:jw], Bp[d][:, g, 0, :], U16[:, ct, k0:k0 + jw])
                                P.mm(p2[:, 0:jw], Bp[d][:, g, 1, :], U16[:, ct, k0:k0 + jw])
                                if d == 0:
                                    P.tt(m1[:, 0:jw], p1[:, 0:jw], tc_[:, 0:jw], ALU.mult)
                                    P.tt(z_[:, 0:jw], p2[:, 0:jw], ts_[:, 0:jw], ALU.mult)
                                else:
                                    P.tt(rev(m1[:, 0:jw]), p1[:, 0:jw], rev(tc_[:, 0:jw]), ALU.mult)
                                    P.tt(rev(z_[:, 0:jw]), p2[:, 0:jw], rev(ts_[:, 0:jw]), ALU.mult)
                                P.tt(m1[:, 0:jw], m1[:, 0:jw], z_[:, 0:jw], ALU.add)
                                if jc == 0:
                                    init = 0.0 if is_ctx else H0[d][:, g:g + 1]
                                else:
                                    pw_ = cks[jc - 1][1]
                                    init = Z[sb_ + jc - 1][:, pw_ - 1:pw_]
                                P.scan(z_[:, 0:jw], bcast(rr, [128, jw]), m1[:, 0:jw], init)
                                if is_ctx and jc == nck - 1:
                                    pj_ = PS[7]
                                    P.mm(pj_[:, 0:1], JT, z_[:, jw - 1:jw])
                                    P.tt(hl[:, 0:1], z_[:, jw - 1:jw], tc_[:, jw - 1:jw], ALU.mult)
                                    P.tt(hl[:, 1:2], pj_[:, 0:1], ts_[:, jw - 1:jw], ALU.mult)
                                    P.tt(H0[d][:, g:g + 1], hl[:, 0:1], hl[:, 1:2], ALU.add)
                                if not need_out:
                                    continue
                                zc = Zc[sb_ + jc]; zs = Zs[sb_ + jc]
                                if d == 0:
                                    P.tt(zc[:, 0:jw], z_[:, 0:jw], tc_[:, 0:jw], ALU.mult)
                                    P.tt(zs[:, 0:jw], z_[:, 0:jw], ts_[:, 0:jw], ALU.mult)
                                else:
                                    P.tt(rev(zc[:, 0:jw]), z_[:, 0:jw], tc_[:, 0:jw], ALU.mult)
                                    P.tt(rev(zs[:, 0:jw]), z_[:, 0:jw], ts_[:, 0:jw], ALU.mult)
                                P.mm(Ybanks[tci][:, 0:jw], Cp[d][:, g, 0, :], zc[:, 0:jw], start=first, stop=False)
                                P.mm(Ybanks[tci][:, 0:jw], Cp[d][:, g, 1, :], zs[:, 0:jw], start=False, stop=lastm)
                    if not need_out:
                        continue
                    ldsel(uT, uT2, OFF_S5 + ct * 128, OFF_S5 + 256 + ct * 128, c0, L)
                    for ci, (k0, kw) in enumerate(cks):
                        yt = Ts[ci]; gt_ = Tc[ci]
                        P.stt(yt[:, 0:kw], uT[:, k0:k0 + kw], Dsum[:, ct:ct + 1], Ybanks[ci][:, 0:kw], ALU.mult, ALU.add)
                        P.tt(gt_[:, 0:kw], yt[:, 0:kw], yt[:, 0:kw], ALU.mult)
                        P.ts(gt_[:, 0:kw], gt_[:, 0:kw], 0.044715, 1.0, op0=ALU.mult, op1=ALU.add)
                        P.tt(gt_[:, 0:kw], gt_[:, 0:kw], yt[:, 0:kw], ALU.mult)
                        P.act(gt_[:, 0:kw], gt_[:, 0:kw], AF.Sigmoid, scale=1.5957691216057308)
                        P.tt(G16[:, ct, k0:k0 + kw], gt_[:, 0:kw], yt[:, 0:kw], ALU.mult)
                    P.dma("sp", g16L[ct * 128:(ct + 1) * 128, c0:c0 + L], G16[:, ct, :])
            P.barrier()
        P.allgather(g16L.ap[:, :], g16G.ap[:, :])
        for (c0, L) in SEQS:
            cks = chunks(L)
            with P.scope() as es2:
                G16 = P.sb(es2, [128, 4, L], BF16, "s5g16f")
                for k4 in range(4):
                    P.dma("sp", G16[:, k4, :], g16G[k4 * 128:(k4 + 1) * 128, c0:c0 + L])
                Tc = [P.sb(es2, [128, 512], F32, "gTc") for _ in range(len(cks))]
                Zc = [P.sb(es2, [128, 512], BF16, "gZc") for _ in range(len(cks))]
                if True:
                    for m in range(4):
                        for ci, (k0, kw) in enumerate(cks):
                            pa = PS[(ci % 2) * 2]; pb = PS[(ci % 2) * 2 + 1]
                            for k4 in range(4):
                                P.mm(pa[:, 0:kw], gluw[:, k4, m * 128:(m + 1) * 128], G16[:, k4, k0:k0 + kw], start=(k4 == 0), stop=(k4 == 3))
                            for k4 in range(4):
                                P.mm(pb[:, 0:kw], gluw[:, k4, 512 + m * 128:512 + (m + 1) * 128], G16[:, k4, k0:k0 + kw], start=(k4 == 0), stop=(k4 == 3))
                            gt_ = Tc[ci]; zo = Zc[ci]
                            P.act(gt_[:, 0:kw], pb[:, 0:kw], AF.Sigmoid, bias=glub[:, 4 + m:5 + m])
                            P.stt(zo[:, 0:kw], pa[:, 0:kw], glub[:, m:m + 1], gt_[:, 0:kw], ALU.add, ALU.mult)
                            sty(1536 + m * 128, c0 + k0, kw, lambda a, b, zo=zo: zo[:, a:b])
            P.barrier()

    P.barrier()


_WKEYS = ["w_ada", "b_ada", "norm_ffn1", "norm_mix", "norm_ffn2", "norm_final", "ffn1_wi", "ffn1_wo", "ffn2_wi", "ffn2_wo",
          "w_in", "w_gate", "b_gate", "w_branch", "w_out", "pool_w", "pool_scale", "q_norm", "k_norm",
          "hy_short_w", "hy_short_b", "hy_f1_w", "hy_f1_b", "hy_f2_w", "hy_f2_b", "hy_f3_w", "hy_freq", "hy_bias",
          "s5_a_re", "s5_a_im", "s5_log_dt", "s5_d", "s5_glu_w", "s5_glu_b"]


def make_in_maps(inputs, depth, batches):
    f = lambda a: np.ascontiguousarray(np.asarray(a, dtype=np.float32))
    shared = {}
    for k in _WKEYS:
        a = f(inputs[k])
        shared[k] = a if k == "norm_final" else np.ascontiguousarray(a[:depth])
    shared["s5_bT_re"] = np.ascontiguousarray(f(inputs["s5_b_re"])[:depth].transpose(0, 1, 2, 4, 3))
    shared["s5_bT_im"] = np.ascontiguousarray(f(inputs["s5_b_im"])[:depth].transpose(0, 1, 2, 4, 3))
    shared["s5_cT_re"] = np.ascontiguousarray(f(inputs["s5_c_re"])[:depth].transpose(0, 1, 2, 4, 3))
    shared["s5_cT_im"] = np.ascontiguousarray(f(inputs["s5_c_im"])[:depth].transpose(0, 1, 2, 4, 3))
    shared.update(_consts())
    x = f(inputs["x"]); c = f(inputs["c"]); ctx = f(inputs["ctx"]); cc = f(inputs["c_ctx"])
    maps = []
    NA = NL - CTX
    for b in batches:
        for r in range(2):
            m = dict(shared)
            if r == 0:
                m["xl"] = np.ascontiguousarray(np.concatenate([ctx[b], x[b, :NA]], axis=0))
                m["cvec"] = np.ascontiguousarray(np.stack([c[b], cc]))
                m["sel"] = np.array([[1.0, 0.0]], np.float32)
            else:
                m["xl"] = np.ascontiguousarray(x[b, NA:])
                m["cvec"] = np.ascontiguousarray(np.stack([c[b], c[b]]))
                m["sel"] = np.array([[0.0, 1.0]], np.float32)
            for k in ("s5_a_re", "s5_a_im", "s5_log_dt", "s5_bT_re", "s5_bT_im", "s5_cT_re", "s5_cT_im"):
                m[k] = np.ascontiguousarray(shared[k][:, :, 16 * r:16 * (r + 1)])
            m["s5_d"] = np.ascontiguousarray(shared["s5_d"][:, :, 256 * r:256 * (r + 1)])
            maps.append(m)
    return maps


def assemble(results):
    outs = []
    for b in range(len(results) // 2):
        a = np.asarray(results[2 * b]["out"], dtype=np.float32)
        bb = np.asarray(results[2 * b + 1]["out"], dtype=np.float32)
        outs.append(np.concatenate([a[CTX:], bb], axis=0))
    return np.stack(outs, axis=0)


def kernel(**inputs):
    depth = 4
    nc = build(depth)
    maps = make_in_maps(inputs, depth, list(range(4)))
    res = run_bass_kernel_spmd(nc, maps, core_ids=list(range(8)))
    return assemble(res.results)
```

```python
import math
from contextlib import ExitStack, contextmanager
import numpy as np
import ml_dtypes
import concourse.bass as bass
import concourse.mybir as mybir
from concourse.bass_utils import run_bass_kernel_spmd

F32 = mybir.dt.float32
BF16 = mybir.dt.bfloat16
AF = mybir.ActivationFunctionType
ALU = mybir.AluOpType

DM = 2048; SEQ = 2048; CTX = 256; NTOK = SEQ + CTX; DFF = 5632; NMOD = 9; EPS = 1e-6
INW = 3584; OFF_Q = 512; OFF_K = 1024; OFF_V = 1280; OFF_HY = 1536; OFF_S5 = 3072
NPASS = 2; PW = 576; NL = 1152
PI = math.pi
TWO_PI_S = 2.0 * math.pi * 0.999999
I32 = mybir.dt.int32


class Buf:
    __slots__ = ("w", "r")

    def __init__(self):
        self.w = None
        self.r = {}


class View:
    __slots__ = ("ap", "buf")

    def __init__(self, ap, buf):
        self.ap = ap
        self.buf = buf


class Tile:
    def __init__(self, t, buf=None):
        self.t = t
        self.buf = buf if buf is not None else Buf()

    def __getitem__(self, key):
        return View(self.t[key], self.buf)


class DT:
    def __init__(self, ap):
        self.ap = ap

    def __getitem__(self, key):
        return View(self.ap[key], None)


def rev(v):
    a = v.ap
    dims = list(a.ap)
    s, c = dims[-1]
    nd = [list(d) for d in dims[:-1]] + [[-s, c]]
    return View(bass.AP(a.tensor, a.offset + s * (c - 1), nd), v.buf)


def bcast(v, shape):
    return View(v.ap.to_broadcast(list(shape)), v.buf)


class Stream:
    def __init__(self, name, eng, sem):
        self.name = name
        self.eng = eng
        self.sem = sem
        self.count = 0
        self.waited = {}


class Prog:
    NRING = 20

    def __init__(self, nc, es):
        self.nc = nc
        self.es = es
        self.S = {}
        for name, eng in (("pe", nc.tensor), ("act", nc.scalar), ("dve", nc.vector),
                          ("pool", nc.gpsimd), ("sp", nc.sync)):
            sem = es.enter_context(nc.semaphore("sem_" + name))
            self.S[name] = Stream(name, eng, sem)
        self.rings = {}
        for q in ("sp", "pool"):
            sems = [es.enter_context(nc.semaphore("ring_%s_%d" % (q, i))) for i in range(self.NRING)]
            self.rings[q] = {"sems": sems, "n": 0, "cnt": [0] * self.NRING}
        self.ntile = 0
        self.cc_sem = None
        self.cc_count = 0

    @contextmanager
    def scope(self):
        with ExitStack() as es:
            yield es
            self.barrier()

    def sb(self, es, shape, dtype, name=None):
        self.ntile += 1
        t = es.enter_context(self.nc.sbuf_tensor("%s_%d" % (name or "t", self.ntile), list(shape), dtype))
        return Tile(t)

    def ps(self, es, shape, dtype=F32, name=None):
        self.ntile += 1
        t = es.enter_context(self.nc.psum_tensor("%s_%d" % (name or "p", self.ntile), list(shape), dtype))
        return Tile(t)

    def _wait(self, st, ev):
        sem, val = ev
        k = id(sem)
        if st.waited.get(k, 0) >= val:
            return
        st.eng.wait_ge(sem, val)
        st.waited[k] = val

    def _deps(self, st, reads, writes):
        evs = []
        for b in reads:
            if b is not None and b.w is not None:
                evs.append(b.w)
        for b in writes:
            if b is None:
                continue
            if b.w is not None:
                evs.append(b.w)
            evs.extend(b.r.values())
        for ev in evs:
            if st.name == "pe" and ev[0] is st.sem:
                continue
            self._wait(st, ev)

    def _mark(self, key, ev, reads, writes):
        for b in reads:
            if b is not None:
                b.r[key] = ev
        for b in writes:
            if b is not None:
                b.w = ev
                b.r = {}

    def op(self, sname, fn, ins, outs):
        st = self.S[sname]
        reads = [v.buf for v in ins if isinstance(v, View)]
        writes = [v.buf for v in outs]
        self._deps(st, reads, writes)
        inst = fn()
        st.count += 1
        inst.then_inc(st.sem, 1)
        self._mark(sname, (st.sem, st.count), reads, writes)

    def dma(self, q, out, in_, extra_r=(), extra_w=(), slow=False):
        st = self.S[q]
        ring = self.rings[q]
        reads = [in_.buf] + list(extra_r)
        writes = [out.buf] + list(extra_w)
        self._deps(st, reads, writes)
        j = ring["n"] % self.NRING
        ring["n"] += 1
        sem = ring["sems"][j]
        if ring["cnt"][j] > 0:
            self._wait(st, (sem, 16 * ring["cnt"][j]))
        if slow:
            with self.nc.allow_non_contiguous_dma(reason="small strided vector load"):
                inst = st.eng.dma_start(out=out.ap, in_=in_.ap)
        else:
            inst = st.eng.dma_start(out=out.ap, in_=in_.ap)
        inst.then_inc(sem, 16)
        ring["cnt"][j] += 1
        ev = (sem, 16 * ring["cnt"][j])
        self._mark(("dma", q, j), ev, reads, writes)
        return ev

    def barrier(self):
        evs = []
        for st in self.S.values():
            if st.count:
                evs.append((st.sem, st.count))
        for ring in self.rings.values():
            for j, c in enumerate(ring["cnt"]):
                if c:
                    evs.append((ring["sems"][j], 16 * c))
        for st in self.S.values():
            for ev in evs:
                if ev[0] is st.sem:
                    continue
                self._wait(st, ev)

    def allgather(self, in_ap, out_ap):
        self.barrier()
        if self.cc_sem is None:
            self.cc_sem = self.es.enter_context(self.nc.semaphore("cc_sem"))
        CR = 256
        nrow = in_ap.shape[0]
        for k in range(nrow // CR):
            inst = self.nc.gpsimd.collective_compute(
                "AllGather", ALU.bypass, replica_groups=[[2 * i, 2 * i + 1] for i in range(self.ncores // 2)],
                ins=[in_ap[k * CR:(k + 1) * CR, :]], outs=[out_ap[2 * k * CR:2 * (k + 1) * CR, :]])
            self.cc_count += 1
            inst.then_inc(self.cc_sem, 1)
        for st in self.S.values():
            self._wait(st, (self.cc_sem, self.cc_count))

    def mm(self, out, lhsT, rhs, start=True, stop=True):
        self.op("pe", lambda: self.nc.tensor.matmul(out.ap, lhsT.ap, rhs.ap, start=start, stop=stop),
                [lhsT, rhs], [out])

    def transpose(self, out, in_, ident):
        self.op("pe", lambda: self.nc.tensor.transpose(out.ap, in_.ap, ident.ap), [in_, ident], [out])

    def act(self, out, in_, func, bias=0.0, scale=1.0):
        ins = [in_]
        b = bias
        s = scale
        if isinstance(bias, View):
            ins.append(bias); b = bias.ap
        if isinstance(scale, View):
            ins.append(scale); s = scale.ap
        self.op("act", lambda: self.nc.scalar.activation(out=out.ap, in_=in_.ap, func=func, bias=b, scale=s),
                ins, [out])

    def tt(self, out, in0, in1, op, eng="dve"):
        e = self.S[eng].eng
        self.op(eng, lambda: e.tensor_tensor(out=out.ap, in0=in0.ap, in1=in1.ap, op=op), [in0, in1], [out])

    def ts(self, out, in0, s1, s2=None, op0=ALU.mult, op1=None, eng="dve"):
        e = self.S[eng].eng
        ins = [in0]
        a1 = s1
        a2 = s2
        if isinstance(s1, View):
            ins.append(s1); a1 = s1.ap
        if isinstance(s2, View):
            ins.append(s2); a2 = s2.ap
        if op1 is None:
            self.op(eng, lambda: e.tensor_scalar(out=out.ap, in0=in0.ap, scalar1=a1, scalar2=None, op0=op0),
                    ins, [out])
        else:
            self.op(eng, lambda: e.tensor_scalar(out=out.ap, in0=in0.ap, scalar1=a1, scalar2=a2, op0=op0, op1=op1),
                    ins, [out])

    def stt(self, out, in0, scalar, in1, op0, op1, eng="dve"):
        e = self.S[eng].eng
        ins = [in0, in1]
        sc = scalar
        if isinstance(scalar, View):
            ins.append(scalar); sc = scalar.ap
        self.op(eng, lambda: e.scalar_tensor_tensor(out=out.ap, in0=in0.ap, scalar=sc, in1=in1.ap, op0=op0, op1=op1),
                ins, [out])

    def copy(self, out, in_, eng="dve"):
        if eng == "act":
            self.act(out, in_, AF.Copy)
        else:
            e = self.S[eng].eng
            self.op(eng, lambda: e.tensor_copy(out=out.ap, in_=in_.ap), [in_], [out])

    def memset(self, out, val, eng="dve"):
        e = self.S[eng].eng
        self.op(eng, lambda: e.memset(out.ap, val), [], [out])

    def scan(self, out, d0, d1, init):
        ins = [d0, d1]
        iv = init
        if isinstance(init, View):
            ins.append(init); iv = init.ap
        self.op("dve", lambda: self.nc.vector.tensor_tensor_scan(out=out.ap, data0=d0.ap, data1=d1.ap, initial=iv,
                                                                 op0=ALU.mult, op1=ALU.add), ins, [out])

    def recip(self, out, in_):
        self.op("dve", lambda: self.nc.vector.reciprocal(out=out.ap, in_=in_.ap), [in_], [out])


def chunks(n, w=512):
    out = []
    c = 0
    while c < n:
        out.append((c, min(w, n - c)))
        c += w
    return out


def _tables(L):
    N = 2 * L
    f = np.arange(L, dtype=np.float64) + 0.5
    s = np.arange(L, dtype=np.float64)
    ang = 2.0 * np.pi * np.outer(s, f) / N
    fc = np.cos(ang); fs = -np.sin(ang)
    nt = L // 128
    def tile_fwd(t):
        return t.reshape(nt, 128, nt, 128).transpose(2, 1, 0, 3)
    fwd = np.stack([tile_fwd(fc), tile_fwd(fs)], axis=2).reshape(nt, 128, 2 * nt * 128)
    ic = (2.0 / N) * np.cos(ang.T); isn = -(2.0 / N) * np.sin(ang.T)
    TC = min(512, L); ntc = L // TC
    def tile_inv(t):
        return t.reshape(nt, 128, ntc, TC).transpose(2, 1, 0, 3)
    inv = np.stack([tile_inv(ic), tile_inv(isn)], axis=2).reshape(ntc, 128, 2 * nt * TC)
    t = np.linspace(0.0, 1.0, L, dtype=np.float32)[:, None]
    fb = np.linspace(1e-4, 15.0, 16, dtype=np.float32)
    w = (2.0 * np.pi * np.arange(L, dtype=np.float32) / L).astype(np.float32)
    fw = w[:, None] * fb[None, :]
    z = np.concatenate([t, np.cos(fw), -np.sin(fw)], axis=-1).astype(np.float32)
    max_decay = math.log(1e-2) / 0.3
    min_decay = math.log(1e-2) / 1.5
    deltas = np.abs(np.linspace(min_decay, max_decay, 512, dtype=np.float32))
    decay = np.exp(-t * deltas[None, :]).astype(np.float32)
    decay_b = decay.copy(); decay_b[0, :] = 0.0
    dec = np.concatenate([decay, decay_b], axis=1)
    tt = np.arange(L)
    rc = []
    for wdw in (2, 4, 8, 16):
        lo = np.clip(tt - wdw // 2, 0, L); hi = np.clip(tt - wdw // 2 + wdw, 0, L)
        rc.append(1.0 / (hi - lo).astype(np.float32))
    rc = np.stack(rc).astype(np.float32)
    return dict(fwd=fwd.astype(ml_dtypes.bfloat16), inv=inv.astype(ml_dtypes.bfloat16),
                zT=np.ascontiguousarray(z.T), dec=dec, rc=rc)


def _consts():
    c = {}
    for L in (2048, 256):
        tb = _tables(L)
        for k, v in tb.items():
            c["%s%d" % (k, L)] = np.ascontiguousarray(v)
    freqs = (10000.0 ** (-np.arange(32, dtype=np.float32) / 32)).astype(np.float32)
    tpos = np.arange(2048)
    rows = (tpos // 64).astype(np.float32); cols = (tpos % 64).astype(np.float32)
    cosT = np.zeros((128, 2048), np.float32); sinT = np.zeros((128, 2048), np.float32)
    for p in range(128):
        pos = rows if p < 64 else cols
        a = pos * freqs[p % 32]
        cosT[p] = np.cos(a); sinT[p] = np.sin(a)
    c["ropecs"] = np.stack([cosT, sinT], axis=1).reshape(128, 4096).copy()
    P = np.zeros((128, 128), np.float32)
    J = np.zeros((128, 128), np.float32)
    for p in range(128):
        if p % 64 < 32:
            P[p, p + 32] = -1.0
        else:
            P[p, p - 32] = 1.0
        if p < 64:
            J[p, p + 64] = -1.0
        else:
            J[p, p - 64] = 1.0
    mats = np.stack([np.eye(128, dtype=np.float32), P.T.copy(), J.T.copy()], axis=1)
    c["mats"] = mats.reshape(128, 384).copy()
    c["iota1"] = (np.arange(2048, dtype=np.float32) + 1.0).reshape(1, 2048)
    return c


def build(depth, dbg=False, mode="full", ncores=8):
    nc = bass.Bass("TRN2", target_bir_lowering=False)
    IN = {}

    BIG = ("w_ada", "ffn1_wi", "ffn1_wo", "ffn2_wi", "ffn2_wo", "w_in", "w_gate", "w_branch", "w_out")

    def din(name, shape, dt=F32):
        if mode == "mix" and name in BIG:
            return None
        IN[name] = DT(nc.dram_tensor(name, list(shape), dt, kind="ExternalInput").ap())
        return IN[name]

    din("xl", [NL, DM]); din("cvec", [2, DM]); din("sel", [1, 2])
    din("w_ada", [depth, DM, NMOD * DM]); din("b_ada", [depth, NMOD * DM])
    din("norm_ffn1", [depth, DM]); din("norm_mix", [depth, DM]); din("norm_ffn2", [depth, DM]); din("norm_final", [DM])
    din("ffn1_wi", [depth, DM, 2 * DFF]); din("ffn1_wo", [depth, DFF, DM])
    din("ffn2_wi", [depth, DM, 2 * DFF]); din("ffn2_wo", [depth, DFF, DM])
    din("w_in", [depth, DM, INW]); din("w_gate", [depth, DM, 4 * DM]); din("b_gate", [depth, 4 * DM])
    din("w_branch", [depth, 4, 512, DM]); din("w_out", [depth, DM, DM])
    din("pool_w", [depth, 4, 128, 128]); din("pool_scale", [depth, 512])
    din("q_norm", [depth, 128]); din("k_norm", [depth, 128])
    din("hy_short_w", [depth, 3, 1536]); din("hy_short_b", [depth, 1536])
    din("hy_f1_w", [depth, 33, 64]); din("hy_f1_b", [depth, 64]); din("hy_f2_w", [depth, 64, 64]); din("hy_f2_b", [depth, 64])
    din("hy_f3_w", [depth, 64, 1024]); din("hy_freq", [depth, 64]); din("hy_bias", [depth, 512])
    NG = 16
    din("s5_a_re", [depth, 2, NG, 64]); din("s5_a_im", [depth, 2, NG, 64]); din("s5_log_dt", [depth, 2, NG])
    din("s5_bT_re", [depth, 2, NG, 16, 64]); din("s5_bT_im", [depth, 2, NG, 16, 64])
    din("s5_cT_re", [depth, 2, NG, 64, 16]); din("s5_cT_im", [depth, 2, NG, 64, 16])
    din("s5_d", [depth, 2, 256]); din("s5_glu_w", [depth, 512, 1024]); din("s5_glu_b", [depth, 1024])
    for L in (2048, 256):
        nt = L // 128; TC = min(512, L); ntc = L // TC
        din("fwd%d" % L, [nt, 128, 2 * nt * 128], BF16); din("inv%d" % L, [ntc, 128, 2 * nt * TC], BF16)
        din("zT%d" % L, [33, L]); din("dec%d" % L, [L, 1024]); din("rc%d" % L, [4, L])
    din("ropecs", [128, 4096]); din("mats", [128, 384]); din("iota1", [1, 2048])
    OUT = DT(nc.dram_tensor("out", [NL, DM], F32, kind="ExternalOutput").ap())
    xT = DT(nc.dram_tensor("xT", [DM, NL], F32).ap())
    projL = [DT(nc.dram_tensor("projL%d" % i, [INW, NL], F32).ap()) for i in range(2)]
    projG = DT(nc.dram_tensor("projG", [2 * INW, NL], F32).ap())
    ybr = [DT(nc.dram_tensor("ybr%d" % i, [DM, NL], BF16).ap()) for i in range(2)]

    def seq_pieces(c0, L):
        out = []
        if c0 < NL:
            w = min(c0 + L, NL) - c0
            out.append((0, c0, w, 0))
        if c0 + L > NL:
            b0 = max(c0, NL) - NL
            out.append((1, b0, c0 + L - NL - b0, max(c0, NL) - c0))
        return out

    def ldp(dstf, r0, c0, L):
        for (h, b0, w, off) in seq_pieces(c0, L):
            g0 = (r0 // 256) * 512 + h * 256 + (r0 % 256)
            P.dma("sp", dstf(off, off + w), projG[g0:g0 + 128, b0:b0 + w])

    def sty(r0, c0, L, srcf):
        for (h, b0, w, off) in seq_pieces(c0, L):
            P.dma("sp", ybr[h][r0:r0 + 128, b0:b0 + w], srcf(off, off + w))
    s5scr = DT(nc.dram_tensor("s5scr", [2, 4, 16, 64], F32).ap())
    g16L = DT(nc.dram_tensor("g16L", [256, NTOK], BF16).ap())
    g16G = DT(nc.dram_tensor("g16G", [512, NTOK], BF16).ap())
    agL = DT(nc.dram_tensor("agL", [256, NTOK], BF16).ap())
    agG = DT(nc.dram_tensor("agG", [512, NTOK], BF16).ap())
    DBG = {}
    if dbg:
        for nm, shp, dt_ in (("d_xT", [DM, NL], F32), ("d_projT", [2 * INW, NL], F32), ("d_ybr0", [DM, NL], BF16),
                             ("d_ybr1", [DM, NL], BF16), ("d_xT2", [DM, NL], F32)):
            DBG[nm] = DT(nc.dram_tensor(nm, shp, dt_, kind="ExternalOutput").ap())

    top = ExitStack()
    with top:
        P = Prog(nc, top)
        P.ncores = ncores
        PS = [P.ps(top, [128, 512], F32, "bank") for _ in range(8)]
        mats = P.sb(top, [128, 3, 128], F32, "mats")
        P.dma("sp", mats[:], View(IN["mats"].ap.rearrange("p (a b) -> p a b", a=3), None))
        ident = mats[:, 0, :]; ropePT = mats[:, 1, :]; JT = mats[:, 2, :]
        ones_dm = P.sb(top, [128, 128], BF16, "ones_dm"); P.memset(ones_dm[:], 1.0 / DM)
        ones_hd = P.sb(top, [128, 128], BF16, "ones_hd"); P.memset(ones_hd[:], 1.0 / 128)
        ones1 = P.sb(top, [128, 128], BF16, "ones1"); P.memset(ones1[:], 1.0)
        modT = P.sb(top, [128, NMOD * 16, 2], F32, "modT")
        Acoef = P.sb(top, [128, 3, 16, 2], F32, "Acoef")
        gateh = P.sb(top, [128, 3, 16, 2], F32, "gateh")
        ncols = P.sb(top, [128, 3, 16], F32, "ncols")
        nfin = P.sb(top, [128, 16], F32, "nfin")
        bgate = P.sb(top, [128, 64], F32, "bgate")
        P.dma("sp", nfin[:], View(IN["norm_final"].ap.rearrange("(t p) -> p t", p=128), None), slow=True)

        def colload(dst, src_ap):
            P.dma("sp", dst, View(src_ap.rearrange("(t p) -> p t", p=128), None), slow=True)

        def segs_of(ps_):
            if ps_ == 0:
                return [(0, 256, 1), (256, 256, 0), (512, 64, 0)]
            return [(0, 512, 0), (512, 64, 0)]

        def modsegs(ps_):
            return [(0, 256, 1), (256, PW - 256, 0)] if ps_ == 0 else [(0, PW, 0)]

        selc = P.sb(top, [128, 2], F32, "selc")
        P.dma("sp", selc[:], View(IN["sel"].ap.broadcast_to([128, 2]), None))
        with P.scope() as es:
            stg = [P.sb(es, [128, DM], F32, "xin") for _ in range(2)]
            oT = [P.sb(es, [128, 16, 128], F32, "xo") for _ in range(2)]
            for ti in range(NL // 128):
                s = stg[ti % 2]; o = oT[ti % 2]
                src = IN["xl"][ti * 128:(ti + 1) * 128, :]
                P.dma("sp", s[:], src)
                for dq in range(4):
                    bank = PS[(ti * 4 + dq) % 8]
                    for di in range(4):
                        d = dq * 4 + di
                        P.transpose(bank[:, di * 128:(di + 1) * 128], s[:, d * 128:(d + 1) * 128], ident)
                    P.copy(View(o.t[:, dq * 4:(dq + 1) * 4, :], o.buf),
                           View(bank.t[:, :].rearrange("p (a b) -> p a b", a=4), bank.buf),
                           eng="act" if dq % 2 else "dve")
                P.dma("sp", View(xT.ap.rearrange("(d p) t -> p d t", p=128)[:, :, ti * 128:(ti + 1) * 128], None), o[:])
        P.barrier()

        def load_X(X, ps_):
            c0 = ps_ * PW
            for dt_ in range(16):
                P.dma("sp", X[:, dt_, :], xT[dt_ * 128:(dt_ + 1) * 128, c0:c0 + PW])

        def store_X(X, ps_):
            c0 = ps_ * PW
            for dt_ in range(16):
                P.dma("sp", xT[dt_ * 128:(dt_ + 1) * 128, c0:c0 + PW], X[:, dt_, :])

        def norm_mod(es, X, U, ps_, sub, lidx):
            sq = [P.sb(es, [128, PW], BF16, "sq") for _ in range(2)]
            RS = P.sb(es, [128, PW], F32, "RS")
            tmp = [P.sb(es, [128, PW], F32, "nt") for _ in range(2)]
            cks = chunks(PW)
            for dt_ in range(16):
                s = sq[dt_ % 2]
                P.act(s[:], X[:, dt_, :], AF.Square)
                for ci, (c0, cw) in enumerate(cks):
                    P.mm(PS[ci][:, 0:cw], ones_dm[:], s[:, c0:c0 + cw], start=(dt_ == 0), stop=(dt_ == 15))
            for ci, (c0, cw) in enumerate(cks):
                P.act(RS[:, c0:c0 + cw], PS[ci][:, 0:cw], AF.Sqrt, bias=EPS)
                P.recip(RS[:, c0:c0 + cw], RS[:, c0:c0 + cw])
            for dt_ in range(16):
                t = tmp[dt_ % 2]
                P.tt(t[:], X[:, dt_, :], RS[:], ALU.mult)
                for (s0, sw, mi) in modsegs(ps_):
                    P.act(U[:, dt_, s0:s0 + sw], t[:, s0:s0 + sw], AF.Identity,
                          bias=modT[:, (3 * sub) * 16 + dt_, mi:mi + 1], scale=Acoef[:, sub, dt_, mi:mi + 1])

        def ffn(es, X, U, ps_, wi, wo, sub, l):
            GT = 4
            wa = [P.sb(es, [128, 16, GT * 128], BF16, "wa") for _ in range(2)]
            wb = [P.sb(es, [128, 16, GT * 128], BF16, "wb") for _ in range(2)]
            wot = [P.sb(es, [128, GT, DM], BF16, "wo") for _ in range(2)]
            H = [P.sb(es, [128, GT, PW], BF16, "H") for _ in range(2)]
            sa = [P.sb(es, [128, 512], F32, "sa") for _ in range(2)]
            cks = chunks(PW)
            wiv = wi.ap[l].rearrange("(k p) n -> p k n", p=128)
            wov = wo.ap[l].rearrange("(j p) n -> p j n", p=128)
            ngrp = DFF // (GT * 128)
            cnt = 0
            for g in range(ngrp):
                sl = g % 2
                P.dma("pool", wa[sl][:], View(wiv[:, :, g * GT * 128:(g + 1) * GT * 128], None))
                P.dma("pool", wb[sl][:], View(wiv[:, :, DFF + g * GT * 128:DFF + (g + 1) * GT * 128], None))
                P.dma("pool", wot[sl][:], View(wov[:, g * GT:(g + 1) * GT, :], None))
                for j in range(GT):
                    for ci, (c0, cw) in enumerate(cks):
                        pa = PS[(cnt % 2) * 2]; pb = PS[(cnt % 2) * 2 + 1]; cnt += 1
                        for kt in range(16):
                            P.mm(pa[:, 0:cw], wa[sl][:, kt, j * 128:(j + 1) * 128], U[:, kt, c0:c0 + cw],
                                 start=(kt == 0), stop=(kt == 15))
                        for kt in range(16):
                            P.mm(pb[:, 0:cw], wb[sl][:, kt, j * 128:(j + 1) * 128], U[:, kt, c0:c0 + cw],
                                 start=(kt == 0), stop=(kt == 15))
                        s_ = sa[cnt % 2]
                        P.act(s_[:, 0:cw], pa[:, 0:cw], AF.Silu)
                        P.tt(H[sl][:, j, c0:c0 + cw], s_[:, 0:cw], pb[:, 0:cw], ALU.mult)
                oc = 0
                for dt_ in range(16):
                    for ci, (c0, cw) in enumerate(cks):
                        po = PS[4 + (oc % 4)]; oc += 1
                        for j in range(GT):
                            P.mm(po[:, 0:cw], wot[sl][:, j, dt_ * 128:(dt_ + 1) * 128], H[sl][:, j, c0:c0 + cw],
                                 start=(j == 0), stop=(j == GT - 1))
                        for (s0, sw, mi) in segs_of(ps_):
                            if s0 < c0 or s0 >= c0 + cw:
                                continue
                            P.stt(X[:, dt_, s0:s0 + sw], po[:, s0 - c0:s0 - c0 + sw], gateh[:, sub, dt_, mi:mi + 1],
                                  X[:, dt_, s0:s0 + sw], ALU.mult, ALU.add)

        for l in range(depth):
            last = (l == depth - 1)
            with P.scope() as es:
                cs = P.sb(es, [128, 16, 2], F32, "cs")
                csb = P.sb(es, [128, 16, 2], BF16, "csb")
                bada = P.sb(es, [128, NMOD * 16], F32, "bada")
                for r in range(2):
                    colload(cs[:, :, r], IN["cvec"].ap[r])
                P.act(csb[:], cs[:], AF.Silu)
                for i in range(NMOD):
                    colload(bada[:, i * 16:(i + 1) * 16], IN["b_ada"].ap[l][i * DM:(i + 1) * DM])
                for i, nm in enumerate(("norm_ffn1", "norm_mix", "norm_ffn2")):
                    colload(ncols[:, i, :], IN[nm].ap[l])
                for n_ in range(4):
                    colload(bgate[:, n_ * 16:(n_ + 1) * 16], IN["b_gate"].ap[l][n_ * DM:(n_ + 1) * DM])
                wsl = [P.sb(es, [128, 16, 512], BF16, "wada") for _ in range(2)]
                wv = IN["w_ada"].ap[l].rearrange("(k p) n -> p k n", p=128)
                for blk in range(NMOD * DM // 512):
                    w = wsl[blk % 2]
                    P.dma("pool", w[:], View(wv[:, :, blk * 512:(blk + 1) * 512], None))
                    bank = PS[blk % 4]
                    for fi in range(4):
                        for kt in range(16):
                            P.mm(bank[:, fi * 2:fi * 2 + 2], w[:, kt, fi * 128:(fi + 1) * 128], csb[:, kt, :],
                                 start=(kt == 0), stop=(kt == 15))
                    P.tt(modT[:, blk * 4:(blk + 1) * 4, :],
                         View(bank.t[:, 0:8].rearrange("p (a b) -> p a b", b=2), bank.buf),
                         View(bada.t[:, blk * 4:(blk + 1) * 4].unsqueeze(2).to_broadcast([128, 4, 2]), bada.buf),
                         ALU.add)
                for sub in range(3):
                    sc = modT[:, (3 * sub + 1) * 16:(3 * sub + 2) * 16, :]
                    P.ts(Acoef[:, sub, :, :], sc, 1.0, None, op0=ALU.add)
                    P.tt(Acoef[:, sub, :, :], Acoef[:, sub, :, :],
                         View(ncols.t[:, sub, :].unsqueeze(2).to_broadcast([128, 16, 2]), ncols.buf), ALU.mult)
                    gt_ = modT[:, (3 * sub + 2) * 16:(3 * sub + 3) * 16, :]
                    P.ts(gateh[:, sub, :, :], gt_, 1.0 if sub == 1 else 0.5, None, op0=ALU.mult)
            P.barrier()


            for ps_ in range(NPASS):
                with P.scope() as es:
                    X = P.sb(es, [128, 16, PW], F32, "X")
                    U = P.sb(es, [128, 16, PW], BF16, "U")
                    load_X(X, ps_)
                    with P.scope() as es2:
                        norm_mod(es2, X, U, ps_, 0, l)
                    with P.scope() as es2:
                        ffn(es2, X, U, ps_, IN["ffn1_wi"], IN["ffn1_wo"], 0, l)
                    P.barrier()

                    with P.scope() as es2:
                        norm_mod(es2, X, U, ps_, 1, l)
                    with P.scope() as es2:
                        wsl = [P.sb(es2, [128, 16, 256], BF16, "win") for _ in range(2)]
                        pj = [P.sb(es2, [128, PW], F32, "pj") for _ in range(2)]
                        wv = IN["w_in"].ap[l].rearrange("(k p) n -> p k n", p=128)
                        for blk in range(INW // 256):
                            w = wsl[blk % 2]
                            P.dma("pool", w[:], View(wv[:, :, blk * 256:(blk + 1) * 256], None))
                            for j in range(2):
                                ct = blk * 2 + j
                                o = pj[ct % 2]
                                for ci, (c0, cw) in enumerate(chunks(PW)):
                                    bank = PS[(ct * 2 + ci) % 8]
                                    for kt in range(16):
                                        P.mm(bank[:, 0:cw], w[:, kt, j * 128:(j + 1) * 128], U[:, kt, c0:c0 + cw],
                                             start=(kt == 0), stop=(kt == 15))
                                    P.copy(o[:, c0:c0 + cw], bank[:, 0:cw], eng="act" if ci else "dve")
                                P.dma("sp", projL[l % 2][ct * 128:(ct + 1) * 128, ps_ * PW:(ps_ + 1) * PW], o[:])
                    store_X(X, ps_)
                P.barrier()
            P.allgather(projL[l % 2].ap[:, :], projG.ap[:, :])
            if dbg and l == 0:
                P.dma("sp", DBG["d_xT"][:, :], xT[:, :]); P.dma("sp", DBG["d_projT"][:, :], projG[:, :])
                P.barrier()

            mixers(P, nc, IN, PS, l, False, ldp, sty, s5scr, ident, ropePT, JT, ones_hd, ones1, colload, DBG,
                   selc=selc, g16L=g16L, g16G=g16G, agL=agL, agG=agG)
            P.barrier()
            if dbg and l == 0:
                P.dma("sp", DBG["d_ybr0"][:, :], ybr[0][:, :]); P.dma("sp", DBG["d_ybr1"][:, :], ybr[1][:, :])
                P.barrier()

            for ps_ in range(NPASS):
                with P.scope() as es:
                    X = P.sb(es, [128, 16, PW], F32, "X")
                    U = P.sb(es, [128, 16, PW], BF16, "U")
                    load_X(X, ps_)
                    with P.scope() as es2:
                        norm_mod(es2, X, U, ps_, 1, l)
                    with P.scope() as es2:
                        MG = P.sb(es2, [128, 16, PW], BF16, "MG")
                        with P.scope() as es3:
                            YB = P.sb(es3, [128, 16, PW], BF16, "YB")
                            ya = [P.sb(es3, [128, PW], BF16, "ya") for _ in range(2)]
                            yb_ = [P.sb(es3, [128, PW], BF16, "yb") for _ in range(2)]
                            for dt_ in range(16):
                                a_ = ya[dt_ % 2]; b_ = yb_[dt_ % 2]
                                if 4 <= dt_ < 8:
                                    P.dma("sp", a_[:], agG[(dt_ - 4) * 128:(dt_ - 3) * 128, ps_ * PW:(ps_ + 1) * PW])
                                    P.dma("sp", b_[:], agG[(dt_ - 4) * 128:(dt_ - 3) * 128, NL + ps_ * PW:NL + (ps_ + 1) * PW])
                                else:
                                    P.dma("sp", a_[:], ybr[0][dt_ * 128:(dt_ + 1) * 128, ps_ * PW:(ps_ + 1) * PW])
                                    P.dma("sp", b_[:], ybr[1][dt_ * 128:(dt_ + 1) * 128, ps_ * PW:(ps_ + 1) * PW])
                                P.ts(a_[:], a_[:], selc[:, 0:1], None, op0=ALU.mult)
                                P.stt(YB[:, dt_, :], b_[:], selc[:, 1:2], a_[:], ALU.mult, ALU.add)
                            wg = [P.sb(es3, [128, 16, 256], BF16, "wg") for _ in range(2)]
                            wbr = [P.sb(es3, [128, 4, 256], BF16, "wbr") for _ in range(2)]
                            acc = P.sb(es3, [128, 2, PW], F32, "acc")
                            sg = [P.sb(es3, [128, 512], F32, "sg") for _ in range(2)]
                            wgv = IN["w_gate"].ap[l].rearrange("(k p) n -> p k n", p=128)
                            it = 0
                            for mq in range(8):
                                for n_ in range(4):
                                    sl = it % 2; it += 1
                                    P.dma("pool", wg[sl][:], View(wgv[:, :, n_ * DM + mq * 256:n_ * DM + (mq + 1) * 256], None))
                                    P.dma("pool", wbr[sl][:], View(IN["w_branch"].ap[l][n_].rearrange("(k p) n -> p k n", p=128)[:, :, mq * 256:(mq + 1) * 256], None))
                                    for mi in range(2):
                                        m = mq * 2 + mi
                                        for ci, (c0, cw) in enumerate(chunks(PW)):
                                            pg = PS[(it * 4 + mi * 2 + ci) % 4]; pp = PS[4 + (it * 4 + mi * 2 + ci) % 4]
                                            for kt in range(16):
                                                P.mm(pg[:, 0:cw], wg[sl][:, kt, mi * 128:(mi + 1) * 128], U[:, kt, c0:c0 + cw],
                                                     start=(kt == 0), stop=(kt == 15))
                                            for k4 in range(4):
                                                P.mm(pp[:, 0:cw], wbr[sl][:, k4, mi * 128:(mi + 1) * 128], YB[:, n_ * 4 + k4, c0:c0 + cw],
                                                     start=(k4 == 0), stop=(k4 == 3))
                                            s_ = sg[(mi * 2 + ci) % 2]
                                            P.act(s_[:, 0:cw], pg[:, 0:cw], AF.Sigmoid, bias=bgate[:, n_ * 16 + m:n_ * 16 + m + 1])
                                            if n_ == 0:
                                                P.tt(acc[:, mi, c0:c0 + cw], s_[:, 0:cw], pp[:, 0:cw], ALU.mult)
                                            else:
                                                P.tt(s_[:, 0:cw], s_[:, 0:cw], pp[:, 0:cw], ALU.mult)
                                                P.tt(acc[:, mi, c0:c0 + cw], acc[:, mi, c0:c0 + cw], s_[:, 0:cw], ALU.add)
                                    if n_ == 3:
                                        for mi in range(2):
                                            P.copy(MG[:, mq * 2 + mi, :], acc[:, mi, :], eng="act")
                        P.barrier()
                        with P.scope() as es3:
                            wsl = [P.sb(es3, [128, 16, 256], BF16, "wout") for _ in range(2)]
                            wv = IN["w_out"].ap[l].rearrange("(k p) n -> p k n", p=128)
                            oc = 0
                            for blk in range(8):
                                w = wsl[blk % 2]
                                P.dma("pool", w[:], View(wv[:, :, blk * 256:(blk + 1) * 256], None))
                                for j in range(2):
                                    m2 = blk * 2 + j
                                    for ci, (c0, cw) in enumerate(chunks(PW)):
                                        po = PS[oc % 8]; oc += 1
                                        for kt in range(16):
                                            P.mm(po[:, 0:cw], w[:, kt, j * 128:(j + 1) * 128], MG[:, kt, c0:c0 + cw],
                                                 start=(kt == 0), stop=(kt == 15))
                                        for (s0, sw, mi) in segs_of(ps_):
                                            if s0 < c0 or s0 >= c0 + cw:
                                                continue
                                            P.stt(X[:, m2, s0:s0 + sw], po[:, s0 - c0:s0 - c0 + sw], gateh[:, 1, m2, mi:mi + 1],
                                                  X[:, m2, s0:s0 + sw], ALU.mult, ALU.add)
                    P.barrier()
                    with P.scope() as es2:
                        norm_mod(es2, X, U, ps_, 2, l)
                    with P.scope() as es2:
                        ffn(es2, X, U, ps_, IN["ffn2_wi"], IN["ffn2_wo"], 2, l)
                    P.barrier()
                    if not last:
                        store_X(X, ps_)
                    else:
                        if dbg:
                            store_X(X, ps_)
                        with P.scope() as es2:
                            sq = [P.sb(es2, [128, PW], BF16, "sq") for _ in range(2)]
                            RS = P.sb(es2, [128, PW], F32, "RS")
                            yo = [P.sb(es2, [128, DM], F32, "yo") for _ in range(2)]
                            cks = chunks(PW)
                            for dt_ in range(16):
                                s = sq[dt_ % 2]
                                P.act(s[:], X[:, dt_, :], AF.Square)
                                for ci, (c0, cw) in enumerate(cks):
                                    P.mm(PS[ci][:, 0:cw], ones_dm[:], s[:, c0:c0 + cw], start=(dt_ == 0), stop=(dt_ == 15))
                            for ci, (c0, cw) in enumerate(cks):
                                P.act(RS[:, c0:c0 + cw], PS[ci][:, 0:cw], AF.Sqrt, bias=EPS)
                                P.recip(RS[:, c0:c0 + cw], RS[:, c0:c0 + cw])
                            for dt_ in range(16):
                                P.stt(X[:, dt_, :], X[:, dt_, :], nfin[:, dt_:dt_ + 1], RS[:], ALU.mult, ALU.mult)
                            P.barrier()
                            for tt_, (k0, kw) in enumerate(chunks(PW, 128)):
                                o = yo[tt_ % 2]
                                for dq in range(4):
                                    bank = PS[(tt_ * 4 + dq) % 8]
                                    for di in range(4):
                                        d = dq * 4 + di
                                        P.transpose(bank[0:kw, di * 128:(di + 1) * 128], X[:, d, k0:k0 + kw], ident)
                                    P.copy(o[0:kw, dq * 512:(dq + 1) * 512], bank[0:kw, :], eng="act" if dq % 2 else "dve")
                                tok0 = ps_ * PW + k0
                                P.dma("sp", OUT[tok0:tok0 + kw, :], o[0:kw, :])
                P.barrier()
            if dbg and l == 0:
                P.dma("sp", DBG["d_xT2"][:, :], xT[:, :])
                P.barrier()
        P.barrier()
    return nc


def mixers(P, nc, IN, PS, l, last, ldp, sty, s5scr, ident, ropePT, JT, ones_hd, ones1, colload, DBG=None,
           selc=None, g16L=None, g16G=None, agL=None, agG=None):
    NG = 16; NCT = 2

    def ldsel(dst, tmp, row_lo, row_hi, c0, L):
        ldp(lambda a, b: dst[:, a:b], row_lo, c0, L)
        ldp(lambda a, b: tmp[:, a:b], row_hi, c0, L)
        P.ts(dst[:, 0:L], dst[:, 0:L], selc[:, 0:1], None, op0=ALU.mult)
        P.stt(dst[:, 0:L], tmp[:, 0:L], selc[:, 1:2], dst[:, 0:L], ALU.mult, ALU.add)

    SEQS = [(0, CTX), (CTX, SEQ)]

    with P.scope() as es:
        OFFP = 16
        pw_ = P.sb(es, [128, 4, 128], BF16, "poolw")
        P.dma("pool", pw_[:], View(IN["pool_w"].ap[l].rearrange("g c d -> c g d"), None))
        pscale = P.sb(es, [128, 4], F32, "pscale")
        colload(pscale[:], IN["pool_scale"].ap[l])
        A = P.sb(es, [128, SEQ + 32], F32, "pa")
        W1 = P.sb(es, [128, SEQ + 32], F32, "pw1")
        W2 = P.sb(es, [128, SEQ + 32], F32, "pw2")
        RC = P.sb(es, [128, SEQ], F32, "prc")
        PO = P.sb(es, [128, SEQ], BF16, "ppo")
        YO = P.sb(es, [128, SEQ], BF16, "pyo")
        for gi in range(4):
            for (c0, L) in SEQS:
                P.memset(A[:, 0:OFFP], 0.0); P.memset(A[:, OFFP + L:OFFP + L + 16], 0.0)
                ldp(lambda a, b: A[:, OFFP + a:OFFP + b], gi * 128, c0, L)
                P.dma("sp", RC[:, 0:L], View(IN["rc%d" % L].ap[gi:gi + 1, :].broadcast_to([128, L]), None))
                lo, hi = OFFP - 8, OFFP + L + 8
                P.tt(W1[:, lo:hi], A[:, lo - 1:hi - 1], A[:, lo:hi], ALU.add)
                cur, nxt = W1, W2
                half = 1
                for lev in range(gi):
                    lo += 2; hi -= 2
                    if lev == 2:
                        lo, hi = OFFP, OFFP + L
                    P.tt(nxt[:, lo:hi], cur[:, lo - half:hi - half], cur[:, lo + half:hi + half], ALU.add)
                    cur, nxt = nxt, cur
                    half *= 2
                P.tt(nxt[:, OFFP:OFFP + L], cur[:, OFFP:OFFP + L], RC[:, 0:L], ALU.mult)
                P.tt(PO[:, 0:L], nxt[:, OFFP:OFFP + L], A[:, OFFP:OFFP + L], ALU.subtract)
                for ci, (k0, kw) in enumerate(chunks(L)):
                    bank = PS[ci % 8]
                    P.mm(bank[:, 0:kw], pw_[:, gi, :], PO[:, k0:k0 + kw])
                    P.act(YO[:, k0:k0 + kw], bank[:, 0:kw], AF.Identity, scale=pscale[:, gi:gi + 1])
                sty(gi * 128, c0, L, lambda a, b: YO[:, a:b])
    P.barrier()

    with P.scope() as es:
        qn = P.sb(es, [128, 1], F32, "qn"); kn = P.sb(es, [128, 1], F32, "kn")
        colload(qn[:], IN["q_norm"].ap[l]); colload(kn[:], IN["k_norm"].ap[l])
        CS = P.sb(es, [128, 2, SEQ], F32, "ropecs")
        P.dma("sp", CS[:], View(IN["ropecs"].ap.rearrange("p (a b) -> p a b", a=2), None))
        Q16 = P.sb(es, [128, 2, NTOK], BF16, "Q16")
        K16 = P.sb(es, [128, 1, NTOK], BF16, "K16")
        Vt = P.sb(es, [128, NTOK // 128, 128], BF16, "Vt")
        with P.scope() as es2:
            raw = [P.sb(es2, [128, NTOK], F32, "raw") for _ in range(2)]
            raw2 = P.sb(es2, [128, NTOK], F32, "raw2")
            sq = P.sb(es2, [128, NTOK], BF16, "asq")
            RS = P.sb(es2, [128, NTOK], F32, "aRS")
            t1 = P.sb(es2, [128, 512], F32, "at1"); t2 = P.sb(es2, [128, 512], F32, "at2")
            for hi_ in range(3):
                r = raw[hi_ % 2]
                if hi_ < 2:
                    ldsel(r, raw2, OFF_Q + hi_ * 128, OFF_Q + (2 + hi_) * 128, 0, NTOK)
                else:
                    ldsel(r, raw2, OFF_K, OFF_K + 128, 0, NTOK)
                gcol = qn if hi_ < 2 else kn
                dst = Q16 if hi_ < 2 else K16
                hh = hi_ if hi_ < 2 else 0
                P.act(sq[:], r[:], AF.Square)
                for ci, (k0, kw) in enumerate(chunks(NTOK)):
                    bank = PS[ci % 4]
                    P.mm(bank[:, 0:kw], ones_hd[:], sq[:, k0:k0 + kw])
                    P.act(RS[:, k0:k0 + kw], bank[:, 0:kw], AF.Sqrt, bias=EPS)
                    P.recip(RS[:, k0:k0 + kw], RS[:, k0:k0 + kw])
                P.stt(r[:], r[:], gcol[:, 0:1], RS[:], ALU.mult, ALU.mult)
                P.copy(dst[:, hh, 0:CTX], r[:, 0:CTX], eng="act")
                for ci, (k0, kw) in enumerate(chunks(SEQ)):
                    bank = PS[4 + ci % 4]
                    P.mm(bank[:, 0:kw], ropePT, r[:, CTX + k0:CTX + k0 + kw])
                    P.tt(t1[:, 0:kw], r[:, CTX + k0:CTX + k0 + kw], CS[:, 0, k0:k0 + kw], ALU.mult)
                    P.tt(t2[:, 0:kw], bank[:, 0:kw], CS[:, 1, k0:k0 + kw], ALU.mult)
                    P.tt(dst[:, hh, CTX + k0:CTX + k0 + kw], t1[:, 0:kw], t2[:, 0:kw], ALU.add)
            for hk in range(1):
                r = raw[hk % 2]
                ldsel(r, raw2, OFF_V, OFF_V + 128, 0, NTOK)
                for kt in range(NTOK // 128):
                    bank = PS[kt % 8]
                    P.transpose(bank[:, 0:128], r[:, kt * 128:(kt + 1) * 128], ident)
                    P.copy(Vt[:, kt, hk * 128:(hk + 1) * 128], bank[:, 0:128], eng="act" if kt % 2 else "dve")
        P.barrier()
        with P.scope() as es2:
            Pt = [P.sb(es2, [128, 512], BF16, "Pt") for _ in range(3)]
            rd = P.sb(es2, [128, 512], F32, "rd")
            O = [P.sb(es2, [128, 512], BF16, "O") for _ in range(2)]
            sc = 1.0 / math.sqrt(128.0)
            it = 0
            for h in range(2):
                kv = 0
                qcs = [(0, CTX, [0, 1])] + [(CTX + k0, kw, list(range(NTOK // 128))) for (k0, kw) in chunks(SEQ)]
                if last:
                    qcs = qcs[1:]
                for (q0, qw, kts) in qcs:
                    po = PS[4 + (it % 2) * 2]; pd = PS[5 + (it % 2) * 2]; it += 1
                    for i, kt in enumerate(kts):
                        pss = PS[i % 4]
                        P.mm(pss[:, 0:qw], K16[:, kv, kt * 128:(kt + 1) * 128], Q16[:, h, q0:q0 + qw])
                        pt = Pt[i % 3]
                        P.act(pt[:, 0:qw], pss[:, 0:qw], AF.Exp, bias=-11.3, scale=sc)
                        P.mm(po[:, 0:qw], Vt[:, kt, kv * 128:(kv + 1) * 128], pt[:, 0:qw], start=(i == 0), stop=(i == len(kts) - 1))
                        P.mm(pd[:, 0:qw], ones1[:], pt[:, 0:qw], start=(i == 0), stop=(i == len(kts) - 1))
                    P.recip(rd[:, 0:qw], pd[:, 0:qw])
                    o = O[it % 2]
                    P.tt(o[:, 0:qw], po[:, 0:qw], rd[:, 0:qw], ALU.mult)
                    P.dma("sp", agL[h * 128:(h + 1) * 128, q0:q0 + qw], o[:, 0:qw])
    P.allgather(agL.ap[:, :], agG.ap[:, :])

    with P.scope() as es:
        hw = P.sb(es, [128, 4, 12], F32, "hyw")
        for tap in range(3):
            colload(hw[:, tap, :], IN["hy_short_w"].ap[l][tap])
        colload(hw[:, 3, :], IN["hy_short_b"].ap[l])
        hbias = P.sb(es, [128, 4], F32, "hybias"); colload(hbias[:], IN["hy_bias"].ap[l])
        f1w = P.sb(es, [33, 64], F32, "f1w"); P.dma("sp", f1w[:], View(IN["hy_f1_w"].ap[l], None))
        f2w = P.sb(es, [64, 64], F32, "f2w"); P.dma("sp", f2w[:], View(IN["hy_f2_w"].ap[l], None))
        f3w = P.sb(es, [64, 1024], F32, "f3w"); P.dma("sp", f3w[:], View(IN["hy_f3_w"].ap[l], None))
        fv = P.sb(es, [64, 3], F32, "fv")
        for i, nm in enumerate(("hy_f1_b", "hy_f2_b", "hy_freq")):
            P.dma("sp", fv[:, i:i + 1], View(IN[nm].ap[l].rearrange("(p o) -> p o", o=1), None), slow=True)
        fb = P.sb(es, [64, 3], F32, "fb")
        P.ts(fb[:, 0:2], fv[:, 0:2], fv[:, 2:3], 1.0 / (2.0 * PI), op0=ALU.mult, op1=ALU.mult)
        P.ts(fb[:, 2:3], fv[:, 2:3], 1.0 / (2.0 * PI), None, op0=ALU.mult)
        for (c0, L) in SEQS:
            if last and c0 == 0:
                continue
            nt = L // 128; TC = min(512, L); ntc = L // TC
            with P.scope() as es1:
                vxT = P.sb(es1, [128, 4, L], F32, "vxT")
                Y = P.sb(es1, [128, nt, 2, 512], BF16, "hyY")
                with P.scope() as es2:
                    AB = P.sb(es2, [128, nt, 2, 512], BF16, "hyAB")
                    vx = P.sb(es2, [128, nt, 512], BF16, "vxtok")
                    with P.scope() as es3:
                        zT = P.sb(es3, [33, L], F32, "zT"); P.dma("sp", zT[:], View(IN["zT%d" % L].ap, None))
                        h1 = P.sb(es3, [64, L], F32, "h1"); h2 = P.sb(es3, [64, L], F32, "h2")
                        hki = P.sb(es3, [64, L], I32, "hki")
                        dec = [P.sb(es3, [128, 1024], F32, "dec") for _ in range(2)]
                        hh = P.sb(es3, [128, 1024], F32, "hh")
                        for (wt, src, dst, bi) in ((f1w, zT, h1, 0), (f2w, h1, h2, 1)):
                            for ci, (k0, kw) in enumerate(chunks(L)):
                                bank = PS[ci % 4]
                                P.mm(bank[0:64, 0:kw], wt[:, :], src[:, k0:k0 + kw])
                                P.ts(dst[:, k0:k0 + kw], bank[0:64, 0:kw], fb[:, 2:3], fb[:, bi:bi + 1], op0=ALU.mult, op1=ALU.add)
                                P.copy(hki[:, k0:k0 + kw], dst[:, k0:k0 + kw])
                                P.tt(dst[:, k0:k0 + kw], dst[:, k0:k0 + kw], hki[:, k0:k0 + kw], ALU.subtract)
                                P.act(dst[:, k0:k0 + kw], dst[:, k0:k0 + kw], AF.Sin, scale=TWO_PI_S)
                        for st in range(nt):
                            d_ = dec[st % 2]
                            P.dma("sp", d_[:], View(IN["dec%d" % L].ap[st * 128:(st + 1) * 128, :], None))
                            for half in range(2):
                                bank = PS[4 + (st * 2 + half) % 4]
                                P.mm(bank[:, 0:512], h2[:, st * 128:(st + 1) * 128], f3w[:, half * 512:(half + 1) * 512])
                                P.tt(hh[:, half * 512:(half + 1) * 512], bank[:, 0:512], d_[:, half * 512:(half + 1) * 512], ALU.mult)
                            P.tt(AB[:, st, 0, :], hh[:, 0:512], hh[:, 512:1024], ALU.add)
                            P.tt(AB[:, st, 1, :], hh[:, 0:512], hh[:, 512:1024], ALU.subtract)
                    with P.scope() as es3:
                        rin = [P.sb(es3, [128, L + 2], F32, "hrin") for _ in range(2)]
                        z1 = P.sb(es3, [128, L], F32, "hz1"); z2 = P.sb(es3, [128, L], F32, "hz2")

                        def sconv(dst, tl, buf_i):
                            r = rin[buf_i]
                            P.memset(r[:, 0:1], 0.0); P.memset(r[:, L + 1:L + 2], 0.0)
                            ldp(lambda a, b, r=r: r[:, 1 + a:1 + b], OFF_HY + tl * 128, c0, L)
                            P.ts(dst, r[:, 1:L + 1], hw[:, 1, tl:tl + 1], hw[:, 3, tl:tl + 1], op0=ALU.mult, op1=ALU.add)
                            P.stt(dst, r[:, 0:L], hw[:, 0, tl:tl + 1], dst, ALU.mult, ALU.add)
                            P.stt(dst, r[:, 2:L + 2], hw[:, 2, tl:tl + 1], dst, ALU.mult, ALU.add)
                        for ct in range(4):
                            sconv(z1[:], 4 + ct, 0)
                            sconv(z2[:], 8 + ct, 1)
                            P.tt(vxT[:, ct, :], z1[:], z2[:], ALU.mult)
                            for st in range(nt):
                                bank = PS[st % 8]
                                P.transpose(bank[:, 0:128], vxT[:, ct, st * 128:(st + 1) * 128], ident)
                                P.copy(vx[:, st, ct * 128:(ct + 1) * 128], bank[:, 0:128], eng="act" if st % 2 else "dve")
                    with P.scope() as es3:
                        ft_ = [P.sb(es3, [128, 2, nt, 128], BF16, "fwd") for _ in range(2)]
                        xk = [P.sb(es3, [128, 4, 512], F32, "xk") for _ in range(2)]
                        tq = P.sb(es3, [128, 4, 512], F32, "tq")
                        for ft in range(nt):
                            tb = ft_[ft % 2]
                            P.dma("sp", tb[:], View(IN["fwd%d" % L].ap[ft].rearrange("p (a s m) -> p a s m", a=2, s=nt), None))
                            banks = [PS[(ft % 2) * 4 + i] for i in range(4)]
                            for i, (cs_, src, comp) in enumerate(((0, vx, None), (1, vx, None), (0, AB, 0), (1, AB, 1))):
                                for st in range(nt):
                                    rhs = src[:, st, :] if comp is None else src[:, st, comp, :]
                                    P.mm(banks[i][:, 0:512], tb[:, cs_, st, :], rhs, start=(st == 0), stop=(st == nt - 1))
                            x_ = xk[ft % 2]
                            for i in range(4):
                                P.copy(x_[:, i, :], banks[i][:, 0:512], eng="act")
                            P.tt(tq[:, 0, :], x_[:, 0, :], x_[:, 2, :], ALU.mult)
                            P.tt(tq[:, 1, :], x_[:, 1, :], x_[:, 3, :], ALU.mult)
                            P.tt(tq[:, 2, :], x_[:, 0, :], x_[:, 3, :], ALU.mult)
                            P.tt(tq[:, 3, :], x_[:, 1, :], x_[:, 2, :], ALU.mult)
                            P.tt(Y[:, ft, 0, :], tq[:, 0, :], tq[:, 1, :], ALU.subtract)
                            P.tt(Y[:, ft, 1, :], tq[:, 2, :], tq[:, 3, :], ALU.add)
                with P.scope() as es2:
                    itb = [P.sb(es2, [128, 2, nt, TC], BF16, "inv") for _ in range(2)]
                    rin = P.sb(es2, [128, L + 2], F32, "hrin0")
                    x0 = P.sb(es2, [128, L], F32, "hx0")
                    yv = P.sb(es2, [128, TC], F32, "hyv")
                    yo = [P.sb(es2, [128, TC], BF16, "hyo") for _ in range(2)]
                    for ct in range(4):
                        P.memset(rin[:, 0:1], 0.0); P.memset(rin[:, L + 1:L + 2], 0.0)
                        ldp(lambda a, b: rin[:, 1 + a:1 + b], OFF_HY + ct * 128, c0, L)
                        P.ts(x0[:], rin[:, 1:L + 1], hw[:, 1, ct:ct + 1], hw[:, 3, ct:ct + 1], op0=ALU.mult, op1=ALU.add)
                        P.stt(x0[:], rin[:, 0:L], hw[:, 0, ct:ct + 1], x0[:], ALU.mult, ALU.add)
                        P.stt(x0[:], rin[:, 2:L + 2], hw[:, 2, ct:ct + 1], x0[:], ALU.mult, ALU.add)
                        for tc in range(ntc):
                            tb = itb[tc % 2]
                            P.dma("sp", tb[:], View(IN["inv%d" % L].ap[tc].rearrange("p (a f j) -> p a f j", a=2, f=nt), None))
                            bank = PS[(ct * ntc + tc) % 8]
                            n_acc = 2 * nt
                            i = 0
                            for comp in range(2):
                                for ft in range(nt):
                                    P.mm(bank[:, 0:TC], Y[:, ft, comp, ct * 128:(ct + 1) * 128], tb[:, comp, ft, :],
                                         start=(i == 0), stop=(i == n_acc - 1))
                                    i += 1
                            P.stt(yv[:], vxT[:, ct, tc * TC:(tc + 1) * TC], hbias[:, ct:ct + 1], bank[:, 0:TC], ALU.mult, ALU.add)
                            o = yo[tc % 2]
                            P.tt(o[:], yv[:], x0[:, tc * TC:(tc + 1) * TC], ALU.mult)
                            sty(1024 + ct * 128, c0 + tc * TC, TC, lambda a, b, o=o: o[:, a:b])
            P.barrier()
    P.barrier()

    with P.scope() as es:
        Bp = [P.sb(es, [128, NG, 2, 128], BF16, "Bp") for _ in range(2)]
        Cp = [P.sb(es, [128, NG, 2, 128], BF16, "Cp") for _ in range(2)]
        rcol = [P.sb(es, [128, NG], F32, "rcol") for _ in range(2)]
        thcol = [P.sb(es, [128, NG], F32, "thcol") for _ in range(2)]
        H0 = [P.sb(es, [128, NG], F32, "H0") for _ in range(2)]
        Dc = P.sb(es, [128, 2, NCT], F32, "Dc"); Dsum = P.sb(es, [128, NCT], F32, "Dsum")
        for d in range(2):
            colload(Dc[:, d, :], IN["s5_d"].ap[l][d])
        P.tt(Dsum[:], Dc[:, 0, :], Dc[:, 1, :], ALU.add)
        glub = P.sb(es, [128, 8], F32, "glub"); colload(glub[:], IN["s5_glu_b"].ap[l])
        gluw = P.sb(es, [128, 4, 1024], BF16, "gluw")
        P.dma("pool", gluw[:], View(IN["s5_glu_w"].ap[l].rearrange("(k p) n -> p k n", p=128), None))
        iota = P.sb(es, [128, SEQ], F32, "iota")
        P.dma("sp", iota[:], View(IN["iota1"].ap.broadcast_to([128, SEQ]), None))
        scrb = Buf()
        with P.scope() as es2:
            for d in range(2):
                are = P.sb(es2, [NG, 64], F32, "are"); aim = P.sb(es2, [NG, 64], F32, "aim")
                ldt = P.sb(es2, [NG, 1], F32, "ldt")
                P.dma("sp", are[:], View(IN["s5_a_re"].ap[l][d], None)); P.dma("sp", aim[:], View(IN["s5_a_im"].ap[l][d], None))
                P.dma("sp", ldt[:], View(IN["s5_log_dt"].ap[l][d].rearrange("(p o) -> p o", o=1), None), slow=True)
                dtc = P.sb(es2, [NG, 1], F32, "dtc"); P.act(dtc[:], ldt[:], AF.Exp)
                lre = P.sb(es2, [NG, 64], F32, "lre"); th = P.sb(es2, [NG, 64], F32, "th")
                P.ts(lre[:], are[:], dtc[:, 0:1], None, op0=ALU.mult)
                P.ts(th[:], aim[:], dtc[:, 0:1], None, op0=ALU.mult)
                r_ = P.sb(es2, [NG, 64], F32, "r_"); P.act(r_[:], lre[:], AF.Exp)
                sn = P.sb(es2, [NG, 64], F32, "sn"); cn = P.sb(es2, [NG, 64], F32, "cn")
                ki = P.sb(es2, [NG, 64], I32, "ki")
                P.ts(th[:], th[:], 1.0 / (2.0 * PI), None, op0=ALU.mult)
                P.copy(ki[:], th[:]); P.tt(sn[:], th[:], ki[:], ALU.subtract)
                P.act(sn[:], sn[:], AF.Sin, scale=TWO_PI_S)
                P.ts(cn[:], th[:], 0.25, None, op0=ALU.add)
                P.copy(ki[:], cn[:]); P.tt(cn[:], cn[:], ki[:], ALU.subtract)
                P.act(cn[:], cn[:], AF.Sin, scale=TWO_PI_S)
                nr = P.sb(es2, [NG, 64], F32, "nr"); ni = P.sb(es2, [NG, 64], F32, "ni")
                P.tt(nr[:], r_[:], cn[:], ALU.mult); P.ts(nr[:], nr[:], -1.0, None, op0=ALU.add)
                P.tt(ni[:], r_[:], sn[:], ALU.mult)
                den = P.sb(es2, [NG, 64], F32, "den"); t_ = P.sb(es2, [NG, 64], F32, "t_")
                P.tt(den[:], are[:], are[:], ALU.mult); P.tt(t_[:], aim[:], aim[:], ALU.mult)
                P.tt(den[:], den[:], t_[:], ALU.add); P.recip(den[:], den[:])
                cr = P.sb(es2, [NG, 64], F32, "cr"); ci_ = P.sb(es2, [NG, 64], F32, "ci")
                P.tt(cr[:], nr[:], are[:], ALU.mult); P.tt(t_[:], ni[:], aim[:], ALU.mult)
                P.tt(cr[:], cr[:], t_[:], ALU.add); P.tt(cr[:], cr[:], den[:], ALU.mult)
                P.tt(ci_[:], ni[:], are[:], ALU.mult); P.tt(t_[:], nr[:], aim[:], ALU.mult)
                P.tt(ci_[:], ci_[:], t_[:], ALU.subtract); P.tt(ci_[:], ci_[:], den[:], ALU.mult)
                for i, src in enumerate((cr, ci_, r_, th)):
                    P.dma("sp", s5scr[d, i, :, :], src[:], extra_w=[scrb])
            P.barrier()
            for d in range(2):
              with P.scope() as es2:
                for half in range(2):
                    P.dma("sp", rcol[d][half * 64:(half + 1) * 64, :], View(s5scr.ap[d, 2].rearrange("g p -> p g"), None), slow=True)
                    P.dma("sp", thcol[d][half * 64:(half + 1) * 64, :], View(s5scr.ap[d, 3].rearrange("g p -> p g"), None), slow=True)
                crb = P.sb(es2, [16, NG * 64], F32, "crb"); cib = P.sb(es2, [16, NG * 64], F32, "cib")
                P.dma("sp", crb[:], View(s5scr.ap[d, 0].rearrange("g p -> (g p)").rearrange("(o n) -> o n", o=1).broadcast_to([16, NG * 64]), None))
                P.dma("sp", cib[:], View(s5scr.ap[d, 1].rearrange("g p -> (g p)").rearrange("(o n) -> o n", o=1).broadcast_to([16, NG * 64]), None))
                bre = P.sb(es2, [16, NG, 64], F32, "bre"); bim = P.sb(es2, [16, NG, 64], F32, "bim")
                P.dma("sp", bre[:], View(IN["s5_bT_re"].ap[l][d].rearrange("g c p -> c g p"), None))
                P.dma("sp", bim[:], View(IN["s5_bT_im"].ap[l][d].rearrange("g c p -> c g p"), None))
                crv = View(crb.t[:, :].rearrange("c (g p) -> c g p", g=NG), crb.buf)
                civ = View(cib.t[:, :].rearrange("c (g p) -> c g p", g=NG), cib.buf)
                t1 = P.sb(es2, [16, NG, 64], F32, "t1"); t2 = P.sb(es2, [16, NG, 64], F32, "t2")
                bbr = P.sb(es2, [16, NG, 64], F32, "bbr"); bbi = P.sb(es2, [16, NG, 64], F32, "bbi")
                P.tt(t1[:], bre[:], crv, ALU.mult); P.tt(t2[:], bim[:], civ, ALU.mult); P.tt(bbr[:], t1[:], t2[:], ALU.subtract)
                P.tt(t1[:], bim[:], crv, ALU.mult); P.tt(t2[:], bre[:], civ, ALU.mult); P.tt(bbi[:], t1[:], t2[:], ALU.add)
                Bs = P.sb(es2, [16, NG, 2, 128], BF16, "Bs")
                P.copy(Bs[:, :, 0, 0:64], bbr[:]); P.copy(Bs[:, :, 0, 64:128], bbi[:])
                P.copy(Bs[:, :, 1, 0:64], bbi[:]); P.ts(Bs[:, :, 1, 64:128], bbr[:], -1.0, None, op0=ALU.mult)
                P.memset(Bp[d][:], 0.0)
                for j in range(8):
                    P.dma("sp", Bp[d][16 * j:16 * (j + 1), j::8, :, :], Bs[0:16, j::8, :, :])
                A1 = P.sb(es2, [128, NG, 16], F32, "cA1"); A2 = P.sb(es2, [128, NG, 16], F32, "cA2")
                cre_v = View(IN["s5_cT_re"].ap[l][d].rearrange("g p c -> p g c"), None)
                cim_v = View(IN["s5_cT_im"].ap[l][d].rearrange("g p c -> p g c"), None)
                P.dma("sp", A1[0:64, :, :], cre_v); P.dma("sp", A1[64:128, :, :], cim_v)
                P.dma("sp", A2[0:64, :, :], cim_v); P.dma("sp", A2[64:128, :, :], cre_v)
                P.ts(A1[64:128, :, :], A1[64:128, :, :], -1.0, None, op0=ALU.mult)
                P.ts(A2[:], A2[:], -1.0, None, op0=ALU.mult)
                P.memset(Cp[d][:], 0.0)
                for j in range(8):
                    P.copy(Cp[d][:, j::8, 0, 16 * j:16 * (j + 1)], A1[:, j::8, :])
                    P.copy(Cp[d][:, j::8, 1, 16 * j:16 * (j + 1)], A2[:, j::8, :])
        P.barrier()
        for (c0, L) in SEQS:
            is_ctx = (c0 == 0)
            need_out = not (last and is_ctx)
            cks = chunks(L)
            with P.scope() as es2:
                uT = P.sb(es2, [128, L], F32, "s5u"); uT2 = P.sb(es2, [128, L], F32, "s5u2")
                U16 = P.sb(es2, [128, NCT, L], BF16, "s5u16")
                G16 = P.sb(es2, [128, NCT, L], BF16, "s5g16")
                for ct in range(NCT):
                    ldsel(uT, uT2, OFF_S5 + ct * 128, OFF_S5 + 256 + ct * 128, c0, L)
                    P.copy(U16[:, ct, :], uT[:], eng="act")
                nck = len(cks)
                CW = cks[0][1]
                NSET = 2
                Ts = [P.sb(es2, [128, CW], F32, "Ts") for _ in range(NSET * nck)]
                Tc = [P.sb(es2, [128, CW], F32, "Tc") for _ in range(NSET * nck)]
                M1 = [P.sb(es2, [128, CW], F32, "M1") for _ in range(NSET * nck)]
                Z = [P.sb(es2, [128, CW], F32, "Z") for _ in range(NSET * nck)]
                Zc = [P.sb(es2, [128, CW], BF16, "Zc") for _ in range(NSET * nck)]
                Zs = [P.sb(es2, [128, CW], BF16, "Zs") for _ in range(NSET * nck)]
                KI = [P.sb(es2, [128, CW], I32, "KI") for _ in range(NSET * nck)]
                itn = 0
                hl = P.sb(es2, [128, 2], F32, "hl")
                for ct in range(NCT):
                    Ybanks = [PS[i] for i in range(nck)]

                    def bufs(sb_, jc):
                        return Ts[sb_ + jc], Tc[sb_ + jc], M1[sb_ + jc], Z[sb_ + jc], KI[sb_ + jc]

                    def stageA(d, gg, jc, sb_):
                        g = ct * 8 + gg
                        th = thcol[d][:, g:g + 1]
                        j0, jw = cks[jc]
                        ts_, tc_, m1, z_, ki = bufs(sb_, jc)
                        P.act(ts_[:, 0:jw], iota[:, j0:j0 + jw], AF.Copy, scale=th)
                        P.copy(ki[:, 0:jw], ts_[:, 0:jw])
                        P.tt(tc_[:, 0:jw], ts_[:, 0:jw], ki[:, 0:jw], ALU.subtract)
                        P.act(ts_[:, 0:jw], tc_[:, 0:jw], AF.Sin, scale=TWO_PI_S)
                        P.act(tc_[:, 0:jw], tc_[:, 0:jw], AF.Sin, scale=PI)
                        P.act(tc_[:, 0:jw], tc_[:, 0:jw], AF.Square)
                        P.act(tc_[:, 0:jw], tc_[:, 0:jw], AF.Identity, bias=1.0, scale=-2.0)

                    def stageB(d, gg, jc, sb_):
                        g = ct * 8 + gg
                        rr = rcol[d][:, g:g + 1]
                        first = (d == 0 and gg == 0); lastm = (d == 1 and gg == 7)
                        j0, jw = cks[jc]
                        tci = jc if d == 0 else nck - 1 - jc
                        k0 = cks[tci][0]
                        ts_, tc_, m1, z_, ki = bufs(sb_, jc)
                        p1 = PS[4 + (jc % 2) * 2]; p2 = PS[5 + (jc % 2) * 2]
                        P.mm(p1[:, 0:jw], Bp[d][:, g, 0, :], U16[:, ct, k0:k0 + jw])
                        P.mm(p2[:, 0:jw], Bp[d][:, g, 1, :], U16[:, ct, k0:k0 + jw])
                        if d == 0:
                            P.tt(m1[:, 0:jw], p1[:, 0:jw], tc_[:, 0:jw], ALU.mult)
                            P.tt(z_[:, 0:jw], p2[:, 0:jw], ts_[:, 0:jw], ALU.mult)
                        else:
                            P.tt(rev(m1[:, 0:jw]), p1[:, 0:jw], rev(tc_[:, 0:jw]), ALU.mult)
                            P.tt(rev(z_[:, 0:jw]), p2[:, 0:jw], rev(ts_[:, 0:jw]), ALU.mult)
                        P.tt(m1[:, 0:jw], m1[:, 0:jw], z_[:, 0:jw], ALU.add)
                        if jc == 0:
                            init = 0.0 if is_ctx else H0[d][:, g:g + 1]
                        else:
                            pw_ = cks[jc - 1][1]
                            init = Z[sb_ + jc - 1][:, pw_ - 1:pw_]
                        P.scan(z_[:, 0:jw], bcast(rr, [128, jw]), m1[:, 0:jw], init)
                        if is_ctx and jc == nck - 1:
                            pj_ = PS[7]
                            P.mm(pj_[:, 0:1], JT, z_[:, jw - 1:jw])
                            P.tt(hl[:, 0:1], z_[:, jw - 1:jw], tc_[:, jw - 1:jw], ALU.mult)
                            P.tt(hl[:, 1:2], pj_[:, 0:1], ts_[:, jw - 1:jw], ALU.mult)
                            P.tt(H0[d][:, g:g + 1], hl[:, 0:1], hl[:, 1:2], ALU.add)
                        if not need_out:
                            return
                        zc = Zc[sb_ + jc]; zs = Zs[sb_ + jc]
                        if d == 0:
                            P.tt(zc[:, 0:jw], z_[:, 0:jw], tc_[:, 0:jw], ALU.mult)
                            P.tt(zs[:, 0:jw], z_[:, 0:jw], ts_[:, 0:jw], ALU.mult)
                        else:
                            P.tt(rev(zc[:, 0:jw]), z_[:, 0:jw], tc_[:, 0:jw], ALU.mult)
                            P.tt(rev(zs[:, 0:jw]), z_[:, 0:jw], ts_[:, 0:jw], ALU.mult)
                        P.mm(Ybanks[tci][:, 0:jw], Cp[d][:, g, 0, :], zc[:, 0:jw], start=first, stop=False)
                        P.mm(Ybanks[tci][:, 0:jw], Cp[d][:, g, 1, :], zs[:, 0:jw], start=False, stop=lastm)

                    items = []
                    for d in range(2):
                        for gg in range(8):
                            sb_ = (itn % NSET) * nck; itn += 1
                            for jc in range(nck):
                                items.append((d, gg, jc, sb_))
                    stageA(*items[0])
                    for i_, it_ in enumerate(items):
                        if i_ + 1 < len(items):
                            stageA(*items[i_ + 1])
                        stageB(*it_)
                    if not need_out:
                        continue
                    ldsel(uT, uT2, OFF_S5 + ct * 128, OFF_S5 + 256 + ct * 128, c0, L)
                    for ci, (k0, kw) in enumerate(cks):
                        yt = Ts[ci]; gt_ = Tc[ci]
                        P.stt(yt[:, 0:kw], uT[:, k0:k0 + kw], Dsum[:, ct:ct + 1], Ybanks[ci][:, 0:kw], ALU.mult, ALU.add)
                        P.tt(gt_[:, 0:kw], yt[:, 0:kw], yt[:, 0:kw], ALU.mult)
                        P.ts(gt_[:, 0:kw], gt_[:, 0:kw], 0.044715, 1.0, op0=ALU.mult, op1=ALU.add)
                        P.tt(gt_[:, 0:kw], gt_[:, 0:kw], yt[:, 0:kw], ALU.mult)
                        P.act(gt_[:, 0:kw], gt_[:, 0:kw], AF.Sigmoid, scale=1.5957691216057308)
                        P.tt(G16[:, ct, k0:k0 + kw], gt_[:, 0:kw], yt[:, 0:kw], ALU.mult)
                    P.dma("sp", g16L[ct * 128:(ct + 1) * 128, c0:c0 + L], G16[:, ct, :])
            P.barrier()
        P.allgather(g16L.ap[:, :], g16G.ap[:, :])
        for (c0, L) in SEQS:
            cks = chunks(L)
            with P.scope() as es2:
                G16 = P.sb(es2, [128, 4, L], BF16, "s5g16f")
                for k4 in range(4):
                    P.dma("sp", G16[:, k4, :], g16G[k4 * 128:(k4 + 1) * 128, c0:c0 + L])
                Tc = [P.sb(es2, [128, 512], F32, "gTc") for _ in range(len(cks))]
                Zc = [P.sb(es2, [128, 512], BF16, "gZc") for _ in range(len(cks))]
                if True:
                    for m in range(4):
                        for ci, (k0, kw) in enumerate(cks):
                            pa = PS[(ci % 2) * 2]; pb = PS[(ci % 2) * 2 + 1]
                            for k4 in range(4):
                                P.mm(pa[:, 0:kw], gluw[:, k4, m * 128:(m + 1) * 128], G16[:, k4, k0:k0 + kw], start=(k4 == 0), stop=(k4 == 3))
                            for k4 in range(4):
                                P.mm(pb[:, 0:kw], gluw[:, k4, 512 + m * 128:512 + (m + 1) * 128], G16[:, k4, k0:k0 + kw], start=(k4 == 0), stop=(k4 == 3))
                            gt_ = Tc[ci]; zo = Zc[ci]
                            P.act(gt_[:, 0:kw], pb[:, 0:kw], AF.Sigmoid, bias=glub[:, 4 + m:5 + m])
                            P.stt(zo[:, 0:kw], pa[:, 0:kw], glub[:, m:m + 1], gt_[:, 0:kw], ALU.add, ALU.mult)
                            sty(1536 + m * 128, c0 + k0, kw, lambda a, b, zo=zo: zo[:, a:b])
            P.barrier()

    P.barrier()


_WKEYS = ["w_ada", "b_ada", "norm_ffn1", "norm_mix", "norm_ffn2", "norm_final", "ffn1_wi", "ffn1_wo", "ffn2_wi", "ffn2_wo",
          "w_in", "w_gate", "b_gate", "w_branch", "w_out", "pool_w", "pool_scale", "q_norm", "k_norm",
          "hy_short_w", "hy_short_b", "hy_f1_w", "hy_f1_b", "hy_f2_w", "hy_f2_b", "hy_f3_w", "hy_freq", "hy_bias",
          "s5_a_re", "s5_a_im", "s5_log_dt", "s5_d", "s5_glu_w", "s5_glu_b"]


def make_in_maps(inputs, depth, batches):
    f = lambda a: np.ascontiguousarray(np.asarray(a, dtype=np.float32))
    shared = {}
    for k in _WKEYS:
        a = f(inputs[k])
        shared[k] = a if k == "norm_final" else np.ascontiguousarray(a[:depth])
    shared["s5_bT_re"] = np.ascontiguousarray(f(inputs["s5_b_re"])[:depth].transpose(0, 1, 2, 4, 3))
    shared["s5_bT_im"] = np.ascontiguousarray(f(inputs["s5_b_im"])[:depth].transpose(0, 1, 2, 4, 3))
    shared["s5_cT_re"] = np.ascontiguousarray(f(inputs["s5_c_re"])[:depth].transpose(0, 1, 2, 4, 3))
    shared["s5_cT_im"] = np.ascontiguousarray(f(inputs["s5_c_im"])[:depth].transpose(0, 1, 2, 4, 3))
    shared.update(_consts())
    x = f(inputs["x"]); c = f(inputs["c"]); ctx = f(inputs["ctx"]); cc = f(inputs["c_ctx"])
    maps = []
    NA = NL - CTX
    for b in batches:
        for r in range(2):
            m = dict(shared)
            if r == 0:
                m["xl"] = np.ascontiguousarray(np.concatenate([ctx[b], x[b, :NA]], axis=0))
                m["cvec"] = np.ascontiguousarray(np.stack([c[b], cc]))
                m["sel"] = np.array([[1.0, 0.0]], np.float32)
            else:
                m["xl"] = np.ascontiguousarray(x[b, NA:])
                m["cvec"] = np.ascontiguousarray(np.stack([c[b], c[b]]))
                m["sel"] = np.array([[0.0, 1.0]], np.float32)
            for k in ("s5_a_re", "s5_a_im", "s5_log_dt", "s5_bT_re", "s5_bT_im", "s5_cT_re", "s5_cT_im"):
                m[k] = np.ascontiguousarray(shared[k][:, :, 16 * r:16 * (r + 1)])
            m["s5_d"] = np.ascontiguousarray(shared["s5_d"][:, :, 256 * r:256 * (r + 1)])
            maps.append(m)
    return maps


def assemble(results):
    outs = []
    for b in range(len(results) // 2):
        a = np.asarray(results[2 * b]["out"], dtype=np.float32)
        bb = np.asarray(results[2 * b + 1]["out"], dtype=np.float32)
        outs.append(np.concatenate([a[CTX:], bb], axis=0))
    return np.stack(outs, axis=0)


def kernel(**inputs):
    depth = 4
    nc = build(depth)
    maps = make_in_maps(inputs, depth, list(range(4)))
    res = run_bass_kernel_spmd(nc, maps, core_ids=list(range(8)))
    return assemble(res.results)
```
